# Optimizing a Trainium2 kernel written in Bass

```python
import math
import jax, jax.numpy as jnp
from jax import lax
import numpy as np

D_MODEL = 1024
BATCH = 4
SEQ = 4096
DEPTH = 1
DEC_BATCH = 128
DEC_SEQ = 1
PAST_LEN = 8192
PAGE_SIZE = 128

D_MIX = D_MODEL
ATTN_WIDTH = D_MIX // 2
SSM_WIDTH = D_MIX - ATTN_WIDTH
HEAD_DIM = 64
N_HEADS = ATTN_WIDTH // HEAD_DIM
N_KV_HEADS = 2
KV_REP = N_HEADS // N_KV_HEADS
WINDOW = 128
ATTN_SCALE = HEAD_DIM ** -0.5
N_BUCKETS = 32
MAX_DISTANCE = 128
SSM_HEAD_DIM = 64
SSM_HEADS = SSM_WIDTH // SSM_HEAD_DIM
SSM_GROUPS = 2
SSM_REP = SSM_HEADS // SSM_GROUPS
D_STATE = 128
CONV_W = 4
CONV_DIM = SSM_WIDTH + 2 * SSM_GROUPS * D_STATE
SSD_CHUNK = 128
D_FF = -(-(8 * D_MODEL) // (3 * 256)) * 256
Q_COLS = N_HEADS * HEAD_DIM
KV_COLS = N_KV_HEADS * HEAD_DIM
SPLIT_IDX = (Q_COLS, Q_COLS + KV_COLS, Q_COLS + 2 * KV_COLS, Q_COLS + 2 * KV_COLS + SSM_WIDTH, Q_COLS + 2 * KV_COLS + SSM_WIDTH + CONV_DIM)
D_IN_PROJ = Q_COLS + 2 * KV_COLS + SSM_WIDTH + CONV_DIM + SSM_HEADS
EPS = 1e-6

kernel_name = "hymba_swa_sink_ssd_decode_step"


def rms_norm(x, w):
    xf = x.astype(jnp.float32)
    xf = xf * lax.rsqrt(jnp.mean(xf * xf, axis=-1, keepdims=True) + EPS)
    return (xf * w.astype(jnp.float32)).astype(x.dtype)


def rel_pos_bias(dist, rel_bias):
    n = jnp.maximum(dist, 0)
    exact = N_BUCKETS // 2
    nf = jnp.maximum(n, 1).astype(jnp.float32)
    large = exact + (jnp.log(nf / exact) / math.log(MAX_DISTANCE / exact) * (N_BUCKETS - exact)).astype(jnp.int32)
    bucket = jnp.where(n < exact, n, jnp.minimum(large, N_BUCKETS - 1))
    b = rel_bias[bucket].astype(jnp.float32)
    return jnp.transpose(b, (2, 0, 1)).reshape(N_KV_HEADS, KV_REP, dist.shape[0], dist.shape[1])


def sink_softmax(s, sinks, mask):
    s = jnp.where(mask, s, -jnp.inf)
    sk = sinks.astype(jnp.float32).reshape(N_KV_HEADS, KV_REP, 1, 1)
    m = jnp.maximum(jnp.max(s, axis=-1, keepdims=True), sk)
    e = jnp.exp(s - m)
    return e / (jnp.sum(e, axis=-1, keepdims=True) + jnp.exp(sk - m))


def swa_prompt(q, k, v, sinks, rel_bias):
    b, L = q.shape[0], q.shape[1]
    nb = L // WINDOW
    qb = q.reshape(b, nb, WINDOW, N_KV_HEADS, KV_REP, HEAD_DIM)
    kb = k.reshape(b, nb, WINDOW, N_KV_HEADS, HEAD_DIM)
    vb = v.reshape(b, nb, WINDOW, N_KV_HEADS, HEAD_DIM)
    pad = ((0, 0), (1, 0), (0, 0), (0, 0), (0, 0))
    kc = jnp.concatenate([jnp.pad(kb[:, :-1], pad), kb], axis=2)
    vc = jnp.concatenate([jnp.pad(vb[:, :-1], pad), vb], axis=2)
    qi = jnp.arange(WINDOW)[:, None]
    kj = jnp.arange(2 * WINDOW)[None, :]
    dist = qi + WINDOW - kj
    band = (dist >= 0) & (dist < WINDOW)
    has_prev = (jnp.arange(nb) > 0)[:, None, None] | (kj >= WINDOW)[None]
    mask = (band[None] & has_prev)[None, :, None, None]
    s = jnp.einsum('bnqhrd,bnkhd->bnhrqk', qb, kc).astype(jnp.float32) * ATTN_SCALE + rel_pos_bias(dist, rel_bias)
    p = sink_softmax(s, sinks, mask)
    o = jnp.einsum('bnhrqk,bnkhd->bnqhrd', p.astype(vc.dtype), vc)
    return o.reshape(b, L, ATTN_WIDTH)


def swa_sample(q, k, v, k_buf, v_buf, sinks, rel_bias):
    b, S = q.shape[0], q.shape[1]
    kc = jnp.concatenate([k_buf.astype(k.dtype), k], axis=1)
    vc = jnp.concatenate([v_buf.astype(v.dtype), v], axis=1)
    dist = jnp.arange(S)[:, None] + WINDOW - jnp.arange(WINDOW + S)[None, :]
    mask = (dist >= 0) & (dist < WINDOW)
    s = jnp.einsum('bqhrd,bkhd->bhrqk', q, kc).astype(jnp.float32) * ATTN_SCALE + rel_pos_bias(dist, rel_bias)
    p = sink_softmax(s, sinks, mask)
    o = jnp.einsum('bhrqk,bkhd->bqhrd', p.astype(vc.dtype), vc)
    return o.reshape(b, S, ATTN_WIDTH), kc[:, -WINDOW:], vc[:, -WINDOW:]


def causal_conv(xbc, conv_buf, w, bias):
    L = xbc.shape[1]
    xp = jnp.concatenate([conv_buf.astype(xbc.dtype), xbc], axis=1)
    out = bias + sum(xp[:, j:j + L] * w[j] for j in range(CONV_W))
    return jax.nn.silu(out), xp[:, -(CONV_W - 1):]


def segsum_exp(cs):
    q = cs.shape[-1]
    causal = jnp.tril(jnp.ones((q, q), dtype=bool))
    return jnp.exp(jnp.where(causal, cs[..., :, None] - cs[..., None, :], -jnp.inf))


def ssd(x, dt, A, B, C, h0):
    b, L = x.shape[0], x.shape[1]
    q = min(SSD_CHUNK, L)
    nc = -(-L // q)
    pad = nc * q - L
    if pad:
        padf = lambda t: jnp.pad(t, [(0, 0), (0, pad)] + [(0, 0)] * (t.ndim - 2))
        x, dt, B, C = padf(x), padf(dt), padf(B), padf(C)
    x = x.reshape(b, nc, q, SSM_GROUPS, SSM_REP, SSM_HEAD_DIM)
    dt = dt.reshape(b, nc, q, SSM_GROUPS, SSM_REP)
    B = B.reshape(b, nc, q, SSM_GROUPS, D_STATE)
    C = C.reshape(b, nc, q, SSM_GROUPS, D_STATE)
    cs = jnp.cumsum(jnp.moveaxis(dt * A.reshape(SSM_GROUPS, SSM_REP), 2, -1), axis=-1)
    xdt = x * dt[..., None]
    CB = jnp.einsum('bcign,bcjgn->bcgij', C, B)
    M = CB[:, :, :, None] * segsum_exp(cs)
    y_diag = jnp.einsum('bcgrij,bcjgrp->bcigrp', M, xdt)
    decay_to_end = jnp.exp(cs[..., -1:] - cs)
    chunk_states = jnp.einsum('bcjgn,bcgrj,bcjgrp->bcgrpn', B, decay_to_end, xdt)
    chunk_decay = jnp.exp(cs[..., -1])

    def step(h, inp):
        st, dec = inp
        return h * dec[..., None, None] + st, h

    h_final, h_prev = lax.scan(step, h0.reshape(b, SSM_GROUPS, SSM_REP, SSM_HEAD_DIM, D_STATE),
                               (jnp.moveaxis(chunk_states, 1, 0), jnp.moveaxis(chunk_decay, 1, 0)))
    h_prev = jnp.moveaxis(h_prev, 0, 1)
    y_off = jnp.einsum('bcign,bcgri,bcgrpn->bcigrp', C, jnp.exp(cs), h_prev)
    y = (y_diag + y_off).reshape(b, nc * q, SSM_HEADS, SSM_HEAD_DIM)[:, :L]
    return y, h_final.reshape(b, SSM_HEADS, SSM_HEAD_DIM, D_STATE)


def ssm_mixer(z, xbc, dt_raw, conv_buf, h0, conv_w, conv_b, dt_bias, A_log, D_skip, norm_w):
    b, L = z.shape[0], z.shape[1]
    xbc_c, new_buf = causal_conv(xbc, conv_buf, conv_w, conv_b)
    xs, Bm, Cm = jnp.split(xbc_c.astype(jnp.float32), (SSM_WIDTH, SSM_WIDTH + SSM_GROUPS * D_STATE), axis=-1)
    xs = xs.reshape(b, L, SSM_HEADS, SSM_HEAD_DIM)
    Bm = Bm.reshape(b, L, SSM_GROUPS, D_STATE)
    Cm = Cm.reshape(b, L, SSM_GROUPS, D_STATE)
    dt = jax.nn.softplus(dt_raw.astype(jnp.float32) + dt_bias.astype(jnp.float32))
    A = -jnp.exp(A_log.astype(jnp.float32))
    y, h = ssd(xs, dt, A, Bm, Cm, h0.astype(jnp.float32))
    y = y + xs * D_skip.astype(jnp.float32)[:, None]
    y = y.reshape(b, L, SSM_WIDTH) * jax.nn.silu(z.astype(jnp.float32))
    yg = y.reshape(b, L, SSM_GROUPS, SSM_WIDTH // SSM_GROUPS)
    yg = yg * lax.rsqrt(jnp.mean(yg * yg, axis=-1, keepdims=True) + EPS)
    y = yg.reshape(b, L, SSM_WIDTH) * norm_w.astype(jnp.float32)
    return y.astype(z.dtype), new_buf, h


def hybrid_layer(x, k_buf, v_buf, conv_buf, h0, rel_bias, norm1_w, w_in, attn_sinks, conv_w, conv_b,
                 dt_bias, A_log, D_skip, ssm_norm_w, w_out, norm2_w, w_gate, w_up, w_down):
    b, L = x.shape[0], x.shape[1]
    h = rms_norm(x, norm1_w)
    proj = h @ w_in
    q, k, v, z, xbc, dt_raw = jnp.split(proj, SPLIT_IDX, axis=-1)
    q = q.reshape(b, L, N_KV_HEADS, KV_REP, HEAD_DIM)
    k = k.reshape(b, L, N_KV_HEADS, HEAD_DIM)
    v = v.reshape(b, L, N_KV_HEADS, HEAD_DIM)
    if k_buf is None:
        a = swa_prompt(q, k, v, attn_sinks, rel_bias)
        new_k, new_v = k[:, -WINDOW:], v[:, -WINDOW:]
    else:
        a, new_k, new_v = swa_sample(q, k, v, k_buf, v_buf, attn_sinks, rel_bias)
    s_out, new_conv, new_h = ssm_mixer(z, xbc, dt_raw, conv_buf, h0, conv_w, conv_b, dt_bias, A_log, D_skip, ssm_norm_w)
    x = x + jnp.concatenate([a, s_out], axis=-1) @ w_out
    h2 = rms_norm(x, norm2_w)
    x = x + (jax.nn.silu(h2 @ w_gate) * (h2 @ w_up)) @ w_down
    return x, new_k, new_v, new_conv, new_h


def setup_inputs(seed: int = 0) -> dict:
    key = jax.random.key(seed)
    ks = jax.random.split(key, 24)
    f32 = jnp.float32

    def nrm(k, shape, s):
        return jax.random.normal(k, shape, f32) * s

    dt0 = jnp.exp(jax.random.uniform(ks[12], (DEPTH, SSM_HEADS), f32, math.log(1e-3), math.log(1e-1)))
    return {
        "x_prompt": nrm(ks[0], (BATCH, SEQ, D_MODEL), 1.0),
        "x_sample": nrm(ks[1], (DEC_BATCH, DEC_SEQ, D_MODEL), 1.0),
        "cache_k": nrm(ks[2], (DEPTH, DEC_BATCH, WINDOW, N_KV_HEADS, HEAD_DIM), 1.0),
        "cache_v": nrm(ks[3], (DEPTH, DEC_BATCH, WINDOW, N_KV_HEADS, HEAD_DIM), 1.0),
        "state_conv": nrm(ks[4], (DEPTH, DEC_BATCH, CONV_W - 1, CONV_DIM), 1.0),
        "state_ssm": nrm(ks[5], (DEPTH, DEC_BATCH, SSM_HEADS, SSM_HEAD_DIM, D_STATE), 0.5),
        "rel_bias": nrm(ks[6], (N_BUCKETS, N_HEADS), 0.5),
        "norm1_w": 1.0 + nrm(ks[7], (DEPTH, D_MODEL), 0.02),
        "w_in": nrm(ks[8], (DEPTH, D_MODEL, D_IN_PROJ), D_MODEL ** -0.5),
        "attn_sinks": nrm(ks[9], (DEPTH, N_HEADS), 1.0),
        "conv_w": nrm(ks[10], (DEPTH, CONV_W, CONV_DIM), CONV_W ** -0.5),
        "conv_b": nrm(ks[11], (DEPTH, CONV_DIM), 0.02),
        "dt_bias": dt0 + jnp.log(-jnp.expm1(-dt0)),
        "A_log": jnp.log(jax.random.uniform(ks[13], (DEPTH, SSM_HEADS), f32, 1.0, 16.0)),
        "D_skip": 1.0 + nrm(ks[14], (DEPTH, SSM_HEADS), 0.1),
        "ssm_norm_w": 1.0 + nrm(ks[15], (DEPTH, SSM_WIDTH), 0.02),
        "w_out": nrm(ks[16], (DEPTH, D_MIX, D_MODEL), D_MIX ** -0.5),
        "norm2_w": 1.0 + nrm(ks[17], (DEPTH, D_MODEL), 0.02),
        "w_gate": nrm(ks[18], (DEPTH, D_MODEL, D_FF), D_MODEL ** -0.5),
        "w_up": nrm(ks[19], (DEPTH, D_MODEL, D_FF), D_MODEL ** -0.5),
        "w_down": nrm(ks[20], (DEPTH, D_FF, D_MODEL), D_FF ** -0.5),
        "final_norm_w": 1.0 + nrm(ks[21], (D_MODEL,), 0.02),
    }


def reference(x_prompt, x_sample, cache_k, cache_v, state_conv, state_ssm, rel_bias, norm1_w, w_in,
              attn_sinks, conv_w, conv_b, dt_bias, A_log, D_skip, ssm_norm_w, w_out, norm2_w,
              w_gate, w_up, w_down, final_norm_w):
    yp, ys = x_prompt, x_sample
    kp, vp, cp, sp, kss, vss, css, sss = [], [], [], [], [], [], [], []
    conv0 = jnp.zeros((x_prompt.shape[0], CONV_W - 1, CONV_DIM), x_prompt.dtype)
    h00 = jnp.zeros((x_prompt.shape[0], SSM_HEADS, SSM_HEAD_DIM, D_STATE), jnp.float32)
    for l in range(DEPTH):
        wts = (rel_bias, norm1_w[l], w_in[l], attn_sinks[l], conv_w[l], conv_b[l], dt_bias[l], A_log[l],
               D_skip[l], ssm_norm_w[l], w_out[l], norm2_w[l], w_gate[l], w_up[l], w_down[l])
        yp, k1, v1, c1, s1 = hybrid_layer(yp, None, None, conv0, h00, *wts)
        ys, k2, v2, c2, s2 = hybrid_layer(ys, cache_k[l], cache_v[l], state_conv[l], state_ssm[l], *wts)
        kp.append(k1.astype(cache_k.dtype)); vp.append(v1.astype(cache_v.dtype))
        cp.append(c1.astype(state_conv.dtype)); sp.append(s1.astype(state_ssm.dtype))
        kss.append(k2.astype(cache_k.dtype)); vss.append(v2.astype(cache_v.dtype))
        css.append(c2.astype(state_conv.dtype)); sss.append(s2.astype(state_ssm.dtype))
    yp = rms_norm(yp, final_norm_w)
    ys = rms_norm(ys, final_norm_w)
    return (yp, ys, jnp.stack(kp), jnp.stack(vp), jnp.stack(cp), jnp.stack(sp),
            jnp.stack(kss), jnp.stack(vss), jnp.stack(css), jnp.stack(sss))
```

```python
import math
from contextlib import ExitStack
import numpy as np
import concourse.bass as bass
import concourse.mybir as mybir
from concourse.bass_utils import run_bass_kernel_spmd

F32 = mybir.dt.float32
BF16 = mybir.dt.bfloat16
AF = mybir.ActivationFunctionType
ALU = mybir.AluOpType
AX = mybir.AxisListType

NCORES = 8
D = 1024
NBLK = 16
GB = 4
DFF = 2816
NFC = DFF // 128
DIN = 2312
WIN_W = DIN + 256
SCALE = 0.125
EPS = 1e-6
NEG = -30000.0
SB = 16


class Tl:
    def __init__(self, name, t, nparts=1):
        self.name, self.t, self.nparts = name, t, nparts

    def all(self):
        return [(self.name, i) for i in range(self.nparts)]

    def p(self, *idx):
        return [(self.name, i) for i in idx]


class Prog:
    def __init__(self, nc):
        self.nc = nc
        self.ops = []
        self.marks = {}
        self.eng = {'pe': nc.tensor, 'act': nc.scalar, 'dve': nc.vector, 'pool': nc.gpsimd, 'sp': nc.sync}

    def add(self, eng, fns, reads=(), writes=(), stream=None):
        if callable(fns):
            fns = [fns]
        import sys as _s
        fr = _s._getframe(1)
        if fr.f_code.co_name == 'dma':
            fr = fr.f_back
        self.ops.append(dict(eng=eng, fns=list(fns), reads=list(reads), writes=list(writes), stream=stream, barrier=False, line=fr.f_lineno))

    def dma(self, eng, stream, out, in_, reads=(), writes=(), **kw):
        self.add(eng, [lambda e: e.dma_start(out=out, in_=in_, **kw)], reads, writes, stream=stream)

    def capture(self):
        if not hasattr(self, '_cstack'):
            self._cstack = []
        self._cstack.append(self.ops)
        self.ops = []

    def end_capture(self):
        got = self.ops
        self.ops = self._cstack.pop()
        return got

    @staticmethod
    def merge(a, b):
        out = []
        i = j = 0
        while i < len(a) or j < len(b):
            if j >= len(b) or (i < len(a) and i * len(b) <= j * len(a)):
                out.append(a[i]); i += 1
            else:
                out.append(b[j]); j += 1
        return out

    def pipe_push(self, blk_ops, split=True, frac=0.44):
        cur = getattr(self, '_cur', [])
        if split:
            h = int(len(blk_ops) * frac)
            head, tail = blk_ops[:h], blk_ops[h:]
        else:
            head, tail = [], blk_ops
        i = j = 0
        a, b = len(cur), len(head)
        while i < a or j < b:
            if j >= b or (i < a and i * b <= j * a):
                self.ops.append(cur[i]); i += 1
            else:
                self.ops.append(head[j]); j += 1
        self._cur = tail

    def pipe_drain(self):
        self.ops.extend(getattr(self, '_cur', []))
        self._cur = []

    def barrier(self):
        for e in ('pe', 'act', 'dve', 'pool', 'sp'):
            self.ops.append(dict(eng=e, fns=[], reads=[], writes=[], stream=None, barrier=True))

    def finish(self, stack):
        nc = self.nc
        import os
        mx = os.environ.get("K_MAXOPS")
        if mx:
            mx = self.marks.get(mx, None) if not mx.isdigit() else int(mx)
            self.ops = self.ops[:mx]
        ops = self.ops
        n = len(ops)
        last_w, readers = {}, {}
        deps = [set() for _ in range(n)]
        for i, op in enumerate(ops):
            if op['barrier']:
                seen_e = {}
                for j in range(i - 1, -1, -1):
                    o = ops[j]
                    if o['barrier']:
                        continue
                    key = o['stream'] if o['stream'] else o['eng']
                    if key == 'wcast':
                        continue
                    if key not in seen_e:
                        seen_e[key] = j
                deps[i] = set(seen_e.values())
                continue
            dset = deps[i]
            for k in op['reads']:
                if k in last_w:
                    dset.add(last_w[k])
                if k[0].startswith('ps'):
                    for r in readers.get(k, ()):
                        if ops[r]['eng'] != op['eng']:
                            dset.add(r)
            for k in op['writes']:
                if k in last_w:
                    dset.add(last_w[k])
                for r in readers.get(k, ()):
                    dset.add(r)
            dset.discard(i)
            for k in op['reads']:
                readers.setdefault(k, []).append(i)
            for k in op['writes']:
                last_w[k] = i
                readers[k] = []
            if op['eng'] == 'pe' and op['stream'] is None:
                deps[i] = {j for j in dset if not (ops[j]['eng'] == 'pe' and ops[j]['stream'] is None)}
        needed = [False] * n
        for i in range(n):
            for j in deps[i]:
                needed[j] = True
        sems = {}

        def getsem(name):
            if name not in sems:
                sems[name] = stack.enter_context(nc.semaphore("s_" + name))
            return sems[name]
        counts = {}
        sig = [None] * n
        for i, op in enumerate(ops):
            if op['barrier'] or not op['fns']:
                continue
            if op['stream']:
                key = 'd_' + op['stream']
                counts[key] = counts.get(key, 0) + 16 * len(op['fns'])
                sig[i] = (key, counts[key])
            elif needed[i]:
                key = 'e_' + op['eng']
                counts[key] = counts.get(key, 0) + 1
                sig[i] = (key, counts[key])
        seen = {e: {} for e in self.eng}
        issued = {}
        for i, op in enumerate(ops):
            e = self.eng[op['eng']]
            waits = {}
            for j in deps[i]:
                if sig[j] is None:
                    continue
                k, v = sig[j]
                if k.startswith('d_'):
                    v = issued[k]
                if v > waits.get(k, 0):
                    waits[k] = v
            wl = [(k, v) for k, v in waits.items() if seen[op['eng']].get(k, 0) < v]
            for k, v in wl:
                seen[op['eng']][k] = v
            if op['barrier'] or not op['fns']:
                for k, v in wl:
                    e.wait_ge(getsem(k), v)
                continue
            attach = None
            if wl and op['stream'] is None:
                attach = wl[0]
                wl = wl[1:]
            for k, v in wl:
                e.wait_ge(getsem(k), v)
            ins = None
            for fi, fn in enumerate(op['fns']):
                ins = fn(e)
                if fi == 0 and attach is not None:
                    ins._wait_ge(getsem(attach[0]), attach[1])
                if op['stream']:
                    ins.then_inc(getsem(sig[i][0]), 16)
            if sig[i] is not None and not op['stream']:
                ins.then_inc(getsem(sig[i][0]), 1)
            if op['stream']:
                issued[sig[i][0]] = sig[i][1]
        sp = nc.sync
        for k, v in counts.items():
            if k.startswith('d_'):
                sp.wait_ge(getsem(k), v)


def bc(ap, shape):
    return ap.broadcast_to(shape)


def build():
    nc = bass.Bass("TRN2", target_bir_lowering=False)
    P = Prog(nc)

    def din(name, shape):
        return nc.dram_tensor(name, shape, F32, kind="ExternalInput").ap()

    def dout(name, shape):
        return nc.dram_tensor(name, shape, F32, kind="ExternalOutput").ap()

    xm = din("xm", [NBLK * 128, D]); xp = din("xp", [NBLK * 128, D]); flag_d = din("flag", [128, 1])
    xsm = din("xsm", [SB, D]); ck = din("ck", [SB, 128, 128]); cv = din("cv", [SB, 128, 128])
    sconv = din("sconv", [SB, 3, 1024]); sssm = din("sssm", [SB * 8, 64 * 128])
    rel_bias = din("rel_bias", [32, 8]); onehot_d = din("onehot", [32, 128])
    norm1_w = din("norm1_w", [1, D]); w_in = din("w_in", [D, DIN]); sinks = din("attn_sinks", [1, 8])
    conv_w = din("conv_w", [4, 1024]); conv_b = din("conv_b", [1, 1024]); dt_bias = din("dt_bias", [1, 8])
    A_log = din("A_log", [1, 8]); D_skip = din("D_skip", [1, 8]); ssm_norm_w = din("ssm_norm_w", [1, 512])
    w_out = din("w_out", [D, D]); norm2_w = din("norm2_w", [1, D]); w_gate = din("w_gate", [D, DFF])
    w_up = din("w_up", [D, DFF]); w_down = din("w_down", [DFF, D]); fnorm_w = din("final_norm_w", [1, D])

    y_m = dout("y_m", [NBLK * 128, D]); y_s = dout("y_s", [SB, D])
    nk_p = dout("nk_p", [128, 128]); nv_p = dout("nv_p", [128, 128]); ncv_p = dout("ncv_p", [3, 1024])
    nss_p = dout("nss_p", [512, 128])
    nk_s = dout("nk_s", [SB, 128, 128]); nv_s = dout("nv_s", [SB, 128, 128])
    ncv_s = dout("ncv_s", [SB, 3, 1024]); nss_s = dout("nss_s", [SB * 8, 64 * 128])
    scrS = nc.dram_tensor("scrS", [383, 8], F32).ap()
    scr1 = nc.dram_tensor("scr1", [SB, 512], F32).ap()
    scr2 = nc.dram_tensor("scr2", [SB, 8], F32).ap()
    scr3 = nc.dram_tensor("scr3", [SB, 512], F32).ap()
    scr4 = nc.dram_tensor("scr4", [SB * 8, 64], F32).ap()
    scrKV = nc.dram_tensor("scrKV", [SB, 256], BF16).ap()
    wgu_bf = nc.dram_tensor("wgu_bf", [NFC, 128, 2, 8, 128], BF16).ap()
    wd_bf = nc.dram_tensor("wd_bf", [DFF, D], BF16).ap()

    st = ExitStack()
    with st:
        def sb(name, shape, dt=F32, nparts=1, stack=st):
            return Tl(name, stack.enter_context(nc.sbuf_tensor(name, shape, dt)), nparts)

        ps = [Tl("ps%d" % i, st.enter_context(nc.psum_tensor("ps%d" % i, [128, 512], F32))) for i in range(8)]
        psrr = [0, 0, 0]
        psub = {}
        bank_mode = [None]
        bank_sub = [None]

        def bank():
            m = bank_mode[0]
            if m is None:
                b = ps[psrr[2] % 8]
                psrr[2] += 1
            elif bank_sub[0] is None:
                b = ps[m * 4 + psrr[m] % 4]
                psrr[m] += 1
            else:
                k = (m, bank_sub[0])
                psub[k] = psub.get(k, 0) + 1
                b = ps[m * 4 + bank_sub[0] * 2 + psub[k] % 2]
            return b

        identf = sb("identf", [128, 128]); identb = sb("identb", [128, 128], BF16)
        tri = sb("tri", [128, 128])
        striu = sb("striu", [128, 128])
        onesf = sb("onesf", [128, 128]); onesb = sb("onesb", [128, 64], BF16)
        negm = sb("negm", [128, 4, 128], BF16)
        expB = sb("expB", [128, 2, 128, 8])
        expBs = sb("expBs", [128, 8])
        w1T = sb("w1T", [128, 8]); w2b = sb("w2b", [128, D]); wfb = sb("wfb", [128, D]); wsT = sb("wsT", [128, 4])
        convdiag = sb("convdiag", [128, 8, 4, 128], BF16)
        cwT = sb("cwT", [128, 4, 8]); cbT = sb("cbT", [128, 8])
        A_b = sb("A_b", [128, 8]); dtb_b = sb("dtb_b", [128, 8]); D_b = sb("D_b", [128, 8]); esink = sb("esink", [128, 8])
        flag = sb("flag_sb", [128, 1])
        zero = sb("zero_sb", [128, 8])
        w_in_sb = sb("w_in_sb", [128, 8, WIN_W], BF16, nparts=8)
        w_out_a = sb("w_out_a", [64, 8, D], BF16); w_out_s = sb("w_out_s", [128, 4, D], BF16)
        NSL = 3
        wgu_sl = [sb("wgu%d" % i, [128, 2, 8, 128], BF16) for i in range(NSL)]
        wd_sl = [sb("wd%d" % i, [128, D], BF16) for i in range(NSL)]
        stateT = sb("stateT", [128, 512]); stateTb = sb("stateTb", [128, 512], BF16)
        small = {}

        def sm(name, w=8):
            if name not in small:
                small[name] = sb("sm_" + name, [128, w])
            return small[name]

        def cst(tl, fn):
            for f_ in (fn if isinstance(fn, (list, tuple)) else [fn]):
                P.add('pool', f_, reads=tl.all(), writes=tl.all())
        cst(identf, [lambda e: e.memset(identf.t[:], 1.0),
                     lambda e: e.affine_select(out=identf.t[:], in_=identf.t[:], pattern=[[-1, 128]], compare_op=ALU.is_equal, fill=0.0, base=0, channel_multiplier=1)])
        cst(tri, [lambda e: e.memset(tri.t[:], 1.0),
                  lambda e: e.affine_select(out=tri.t[:], in_=tri.t[:], pattern=[[1, 128]], compare_op=ALU.is_ge, fill=0.0, base=0, channel_multiplier=-1)])
        cst(striu, [lambda e: e.memset(striu.t[:], 1.0),
                    lambda e: e.affine_select(out=striu.t[:], in_=striu.t[:], pattern=[[-1, 128]], compare_op=ALU.is_gt, fill=0.0, base=0, channel_multiplier=1)])
        cst(onesf, lambda e: e.memset(onesf.t[:], 1.0))
        cst(onesb, lambda e: e.memset(onesb.t[:], 1.0))
        cst(zero, lambda e: e.memset(zero.t[:], 0.0))
        P.add('dve', lambda e: e.tensor_copy(out=identb.t[:], in_=identf.t[:]), reads=identf.all(), writes=identb.all())

        def bload(tl, src, width):
            P.dma('sp', 'cst', tl.t[:, 0:width], src[0:1, :].partition_broadcast(128), writes=tl.all())
        bload(w2b, norm2_w, D); bload(wfb, fnorm_w, D)
        P.dma('sp', 'cst', wsT.t[:], ssm_norm_w.rearrange("o (c p) -> p (o c)", p=128), writes=wsT.all(), allow_slow_non_contiguous=True)
        bload(A_b, A_log, 8); bload(dtb_b, dt_bias, 8); bload(D_b, D_skip, 8); bload(esink, sinks, 8)
        P.dma('sp', 'cst', flag.t[:], flag_d[:, :], writes=flag.all())
        P.dma('sp', 'cst', w1T.t[:], norm1_w.rearrange("o (k p) -> p (o k)", p=128), writes=w1T.all(), allow_slow_non_contiguous=True)
        for j in range(4):
            P.dma('sp', 'cst', cwT.t[:, j, :], conv_w[j:j + 1, :].rearrange("o (c p) -> p (o c)", p=128), writes=cwT.all(), allow_slow_non_contiguous=True)
        P.dma('sp', 'cst', cbT.t[:], conv_b.rearrange("o (c p) -> p (o c)", p=128), writes=cbT.all(), allow_slow_non_contiguous=True)
        P.add('act', lambda e: e.activation(out=A_b.t[:], in_=A_b.t[:], func=AF.Exp), reads=A_b.all(), writes=A_b.all())
        P.add('dve', lambda e: e.tensor_scalar(out=A_b.t[:], in0=A_b.t[:], scalar1=-1.0, scalar2=None, op0=ALU.mult), reads=A_b.all(), writes=A_b.all())
        P.add('act', lambda e: e.activation(out=esink.t[:], in_=esink.t[:], func=AF.Exp), reads=esink.all(), writes=esink.all())
        for c in range(8):
            for j in range(4):
                P.add('dve', (lambda c, j: lambda e: e.tensor_scalar(out=convdiag.t[:, c, j, :], in0=identf.t[:], scalar1=cwT.t[:, j, c:c + 1], scalar2=None, op0=ALU.mult))(c, j),
                      reads=identf.all() + cwT.all(), writes=convdiag.all())
        P.dma('pool', 'win', w_in_sb.t[:, :, 0:DIN], w_in.rearrange("(k p) n -> p k n", p=128), writes=w_in_sb.all())
        for k in range(8):
            P.add('dve', lambda e, k=k: e.tensor_scalar(out=w_in_sb.t[:, k, 0:DIN], in0=w_in_sb.t[:, k, 0:DIN], scalar1=w1T.t[:, k:k + 1], scalar2=None, op0=ALU.mult), reads=w_in_sb.all() + w1T.all(), writes=w_in_sb.all())
        for h in range(2):
            for dup in range(2):
                o0 = DIN + h * 128 + dup * 64
                P.add('dve', lambda e, o0=o0, h=h: e.tensor_copy(out=w_in_sb.t[:, :, o0:o0 + 64], in_=w_in_sb.t[:, :, 512 + h * 64:512 + (h + 1) * 64]), reads=w_in_sb.all(), writes=w_in_sb.all())
        P.dma('pool', 'wout', w_out_a.t[:, :, :], w_out[0:512, :].rearrange("(h d) n -> d h n", d=64), writes=w_out_a.all())
        P.dma('pool', 'wout', w_out_s.t[:, :, :], w_out[512:1024, :].rearrange("(c p) n -> p c n", p=128), writes=w_out_s.all())
        for c in range(4):
            P.add('dve', lambda e, c=c: e.tensor_scalar(out=w_out_s.t[:, c, :], in0=w_out_s.t[:, c, :], scalar1=wsT.t[:, c:c + 1], scalar2=None, op0=ALU.mult), reads=w_out_s.all() + wsT.all(), writes=w_out_s.all())
        def rmsnorm_T(xin_ap, xin_keys, nrow, wb, hn, hTdst, hT_keys, tag, cp='act'):
            ss = sm("ss_" + tag, 4)
            junk = hn
            P.add('act', lambda e: e.activation(out=junk.t[0:nrow, :], in_=xin_ap, func=AF.Square, accum_out=ss.t[0:nrow, 0:1]),
                  reads=xin_keys, writes=junk.all() + ss.all())
            P.add('act', lambda e: e.activation(out=ss.t[0:nrow, 1:2], in_=ss.t[0:nrow, 0:1], func=AF.Ln, bias=EPS, scale=1.0 / D), reads=ss.all(), writes=ss.all())
            P.add('act', lambda e: e.activation(out=ss.t[0:nrow, 2:3], in_=ss.t[0:nrow, 1:2], func=AF.Exp, scale=-0.5), reads=ss.all(), writes=ss.all())
            if wb is None:
                P.add('dve', lambda e: e.tensor_scalar(out=hn.t[0:nrow, :], in0=xin_ap, scalar1=ss.t[0:nrow, 2:3], scalar2=None, op0=ALU.mult),
                      reads=xin_keys + ss.all(), writes=hn.all())
            else:
                P.add('dve', lambda e: e.scalar_tensor_tensor(out=hn.t[0:nrow, :], in0=xin_ap, scalar=ss.t[0:nrow, 2:3], in1=wb.t[0:nrow, :], op0=ALU.mult, op1=ALU.mult),
                      reads=xin_keys + ss.all() + wb.all(), writes=hn.all())
            bk = bank()
            pv = bk.t[:].bitcast(BF16)
            P.add('pe', [(lambda k: lambda e: e.transpose(out=pv[:, k * 128:k * 128 + nrow], in_=hn.t[0:nrow, k * 128:(k + 1) * 128], identity=identb.t[0:nrow, 0:nrow]))(k) for k in range(8)],
                  reads=hn.all() + identb.all(), writes=bk.all())
            if cp == 'act':
                P.add('act', lambda e: e.activation(out=hTdst, in_=pv.rearrange("p (k t) -> p k t", k=8)[:, :, 0:nrow], func=AF.Copy), reads=bk.all(), writes=hT_keys)
            else:
                P.add('dve', lambda e: e.tensor_copy(out=hTdst, in_=pv.rearrange("p (k t) -> p k t", k=8)[:, :, 0:nrow]), reads=bk.all(), writes=hT_keys)
            return ss

        def softplus_dt(ps_ap, ps_keys, nrow, dt):
            P.add('dve', lambda e: e.tensor_tensor(out=dt.t[0:nrow, 8:16], in0=ps_ap, in1=dtb_b.t[0:nrow, :], op=ALU.add), reads=ps_keys + dtb_b.all(), writes=dt.all())
            P.add('act', lambda e: e.activation(out=dt.t[0:nrow, 8:16], in_=dt.t[0:nrow, 8:16], func=AF.Exp), reads=dt.all(), writes=dt.all())
            P.add('act', lambda e: e.activation(out=dt.t[0:nrow, 0:8], in_=dt.t[0:nrow, 8:16], func=AF.Ln, bias=1.0, scale=1.0), reads=dt.all(), writes=dt.all())

        def ffn(h2T, ntok, blocks, x1_ap_fn, x1_keys_fn, out_dma_fn, actT, tag, defer_tail=False):
            sgs = ffn_sg[tag]
            HC = NFC // 2
            nb = len(blocks)
            for fh in range(2):
                for ci in range(HC):
                    c = fh * HC + ci
                    wgu = wgu_sl[c % NSL]
                    P.dma('sp', 'wgu%d' % (c % NSL), wgu.t[:, :, :, :], wgu_bf[c], reads=WGU_KEYS, writes=wgu.all())
                    bg, bu = bank(), bank()
                    P.add('pe', [(lambda k, bg=bg, wgu=wgu: lambda e: e.matmul(bg.t[:, 0:ntok], lhsT=wgu.t[:, 0, k, :], rhs=h2T.t[:, k, 0:ntok], start=(k == 0), stop=(k == 7)))(k) for k in range(8)],
                          reads=wgu.all() + h2T.all(), writes=bg.all())
                    P.add('pe', [(lambda k, bu=bu, wgu=wgu: lambda e: e.matmul(bu.t[:, 0:ntok], lhsT=wgu.t[:, 1, k, :], rhs=h2T.t[:, k, 0:ntok], start=(k == 0), stop=(k == 7)))(k) for k in range(8)],
                          reads=wgu.all() + h2T.all(), writes=bu.all())
                    sg = sgs[c % 2]
                    P.add('act', lambda e, bg=bg, sg=sg: e.activation(out=sg.t[:, 0:ntok], in_=bg.t[:, 0:ntok], func=AF.Silu), reads=bg.all(), writes=sg.all())
                    P.add('dve', lambda e, bu=bu, sg=sg, ci=ci: e.tensor_tensor(out=actT.t[:, ci, 0:ntok], in0=sg.t[:, 0:ntok], in1=bu.t[:, 0:ntok], op=ALU.mult),
                          reads=bu.all() + sg.all(), writes=actT.p(ci))
                accs = [[bank(), bank()] for _ in range(nb)]
                for ci in range(HC):
                    c = fh * HC + ci
                    wd = wd_sl[c % NSL]
                    P.dma('sp', 'wd%d' % (c % NSL), wd.t[:, :], wd_bf[c * 128:(c + 1) * 128, :], reads=[('wd_bf', 0)], writes=wd.all())
                    fns = []
                    wr = []
                    for bi, (c0, nrow) in enumerate(blocks):
                        for half in range(2):
                            fns.append((lambda bi, half, c0, nrow, ci=ci, wd=wd, accs=accs: lambda e: e.matmul(accs[bi][half].t[0:nrow, :], lhsT=actT.t[:, ci, c0:c0 + nrow], rhs=wd.t[:, half * 512:(half + 1) * 512], start=(ci == 0), stop=(ci == HC - 1)))(bi, half, c0, nrow))
                            wr += accs[bi][half].all()
                    P.add('pe', fns, reads=wd.all() + actT.p(ci), writes=wr)
                if fh == 0:
                    for bi, (c0, nrow) in enumerate(blocks):
                        x1keys = x1_keys_fn(bi)
                        for half in range(2):
                            xa = x1_ap_fn(bi, half)
                            P.add('dve', lambda e, xa=xa, a=accs[bi][half], nrow=nrow: e.tensor_tensor(out=xa, in0=a.t[0:nrow, :], in1=xa, op=ALU.add),
                                  reads=accs[bi][half].all() + x1keys, writes=x1keys)
            for bi, (c0, nrow) in enumerate(blocks):
                x1keys = x1_keys_fn(bi)
                for half in range(2):
                    xa = x1_ap_fn(bi, half)
                    P.add('dve', lambda e, xa=xa, a=accs[bi][half], nrow=nrow: e.tensor_tensor(out=xa, in0=a.t[0:nrow, :], in1=xa, op=ALU.add),
                          reads=accs[bi][half].all() + x1keys, writes=x1keys)
            if defer_tail:
                P.capture()
            for bi, (c0, nrow) in enumerate(blocks):
                x1keys = x1_keys_fn(bi)
                ss = sm("ssf_" + tag, 4)
                full = x1_ap_fn(bi, None)
                if tag == "m":
                    jk = sgs[0]
                    jk_ap = jk.t[0:nrow, :].bitcast(BF16)
                else:
                    jk = hn_t[0]
                    jk_ap = jk.t[0:nrow, :]
                P.add('act', lambda e, full=full, nrow=nrow, jk_ap=jk_ap: e.activation(out=jk_ap, in_=full, func=AF.Square, accum_out=ss.t[0:nrow, 0:1]),
                      reads=x1keys, writes=jk.all() + ss.all())
                P.add('act', lambda e, nrow=nrow: e.activation(out=ss.t[0:nrow, 1:2], in_=ss.t[0:nrow, 0:1], func=AF.Ln, bias=EPS, scale=1.0 / D), reads=ss.all(), writes=ss.all())
                P.add('act', lambda e, nrow=nrow: e.activation(out=ss.t[0:nrow, 2:3], in_=ss.t[0:nrow, 1:2], func=AF.Exp, scale=-0.5), reads=ss.all(), writes=ss.all())
                P.add('dve', lambda e, full=full, nrow=nrow: e.scalar_tensor_tensor(out=full, in0=full, scalar=ss.t[0:nrow, 2:3], in1=wfb.t[0:nrow, :], op0=ALU.mult, op1=ALU.mult),
                      reads=x1keys + ss.all() + wfb.all(), writes=x1keys)
                out_dma_fn(bi, full, x1keys)
            if defer_tail:
                return P.end_capture()
            return None

        ffn_sg = {}
        hn_t = [None]
        WGU_KEYS = [('wgu_bf', i) for i in range(16)]
        xs_in = sb("xs_in", [SB, D]); h2T_s = sb("h2T_s", [128, 8, SB], BF16); actT_s = sb("actT_s", [128, NFC // 2, SB], BF16, nparts=NFC // 2)
        ffn_sg["s"] = [sb("sg_s%d" % i, [128, SB]) for i in range(2)]
        hn_sf = sb("hn_sf", [SB, D], BF16)
        for nm_, w_ in [("ss_s1", 4), ("ss_s2", 4), ("ssf_s", 4), ("ssg_s", 8), ("ss_m1", 4), ("ss_m2", 4), ("ssf_m", 4), ("ssg_m", 8), ("scal", 64)]:
            sm(nm_, w_)

        import os as _os
        for _i in range(int(_os.environ.get('K_DUMMY', '0'))):
            P.add('dve', lambda e: e.memset(zero.t[:], 0.0), writes=zero.all())
        P.marks['setup'] = len(P.ops)
        sst = ExitStack()
        with sst:
            def ssb(name, shape, dt=F32, nparts=1):
                return sb(name, shape, dt, nparts, stack=sst)
            h0 = [ssb("h0_%d" % i, [128, 16, 128]) for i in range(2)]; otmp = ssb("otmp", [128, 16, 128], nparts=2); kvst = otmp
            Kwb = ssb("Kwb", [128, SB, 128], BF16); Vwb = ssb("Vwb", [128, SB, 128], BF16)
            SelH = Tl("h0_1", h0[1].t, 1); SelHT = ssb("SelHT", [128, 8, SB])
            P.dma('sp', 'xs', xs_in.t[:, :], xsm[:, :], writes=xs_in.all())
            for hf, q in ((0, 'sp'), (1, 'act')):
                P.dma(q, 'kvst%d' % hf, kvst.t[0:127, hf * 8:(hf + 1) * 8, :], ck[hf * 8:(hf + 1) * 8, 1:128, :].rearrange("b p f -> p b f"), writes=kvst.p(hf))
            for hf, q in ((0, 'sp'), (1, 'act')):
                P.dma(q, 'kvsv%d' % hf, h0[0].t[0:127, hf * 8:(hf + 1) * 8, :], cv[hf * 8:(hf + 1) * 8, 1:128, :].rearrange("b p f -> p b f"), writes=h0[0].all())
            SelQ = ssb("SelQ", [SB, 4, 128]); SelQT = ssb("SelQT", [128, 4, SB])
            cbuf = ssb("cbuf", [128, 3, 256]); cw_b = ssb("cw_b", [128, 4, 256]); cb_b = ssb("cb_b", [128, 256])
            accq = ssb("accq", [128, 256]); tmpq = ssb("tmpq", [128, 256])
            P.add('dve', [lambda e: e.memset(cbuf.t[:, :, :], 0.0), lambda e: e.memset(cw_b.t[:, :, :], 0.0), lambda e: e.memset(cb_b.t[:, :], 0.0)],
                  writes=cbuf.all() + cw_b.all() + cb_b.all())
            for q4 in range(4):
                p0, cs0 = q4 * 32, q4 * 256
                P.dma('sp', 'sconv', cbuf.t[p0:p0 + SB, :, :], sconv[:, :, cs0:cs0 + 256], writes=cbuf.all())
                P.dma('sp', 'sconv', cw_b.t[p0:p0 + SB, :, :], bass.AP(conv_w.tensor, cs0, [[0, SB], [1024, 4], [1, 256]]), writes=cw_b.all())
                P.dma('sp', 'sconv', cb_b.t[p0:p0 + SB, :], conv_b[0:1, cs0:cs0 + 256].partition_broadcast(SB), writes=cb_b.all())
            antiI = ssb("antiI", [128, 128])
            cst(antiI, [lambda e: e.memset(antiI.t[:], 1.0),
                        lambda e: e.affine_select(out=antiI.t[:], in_=antiI.t[:], pattern=[[1, 128]], compare_op=ALU.is_equal, fill=0.0, base=-127, channel_multiplier=1)])
            cst(SelH, [lambda e: e.memset(SelH.t[0:SB, 0:8, :], 1.0),
                       lambda e: e.affine_select(out=SelH.t[0:SB, 0:8, :], in_=SelH.t[0:SB, 0:8, :], pattern=[[-1, 8], [1, 128]], compare_op=ALU.is_equal, fill=0.0, base=0, channel_multiplier=-8)])
            cst(SelHT, [lambda e: e.memset(SelHT.t[:, :, :], 1.0),
                       lambda e: e.affine_select(out=SelHT.t[:, :, :], in_=SelHT.t[:, :, :], pattern=[[-1, 8], [-8, SB]], compare_op=ALU.is_equal, fill=0.0, base=0, channel_multiplier=1)])
            cst(SelQ, [lambda e: e.memset(SelQ.t[:, :, :], 1.0),
                       lambda e: e.affine_select(out=SelQ.t[:, :, :], in_=SelQ.t[:, :, :], pattern=[[-32, 4], [1, 128]], compare_op=ALU.is_equal, fill=0.0, base=0, channel_multiplier=-1)])
            cst(SelQT, [lambda e: e.memset(SelQT.t[:, :, :], 1.0),
                       lambda e: e.affine_select(out=SelQT.t[:, :, :], in_=SelQT.t[:, :, :], pattern=[[-32, 4], [-1, SB]], compare_op=ALU.is_equal, fill=0.0, base=0, channel_multiplier=1)])
            sb_saved = sb
            sb = lambda name, shape, dt=F32, nparts=1, stack=sst: sb_saved(name, shape, dt, nparts, stack=sst)
            negf = sb("negf", [128, 128])
            cst(negf, [lambda e: e.memset(negf.t[:], 0.0),
                       lambda e: e.affine_select(out=negf.t[:], in_=negf.t[:], pattern=[[1, 128]], compare_op=ALU.is_ge, fill=NEG, base=0, channel_multiplier=-1)])
            P.add('dve', lambda e: e.tensor_copy(out=negm.t[:], in_=bc(negf.t[:, :].unsqueeze(1), [128, 4, 128])), reads=negf.all(), writes=negm.all())
            oh_sb = sb("oh_sb", [32, 128]); rb_sb = sb("rb_sb", [32, 8]); eb = sb("eb", [128, 8])
            P.dma('sp', 'cst', oh_sb.t[:], onehot_d[:, :], writes=oh_sb.all())
            P.dma('sp', 'cst', rb_sb.t[:], rel_bias[:, :], writes=rb_sb.all())
            b0 = bank()
            P.add('pe', lambda e: e.matmul(b0.t[:, 0:8], lhsT=oh_sb.t[:, :], rhs=rb_sb.t[:, :], start=True, stop=True), reads=oh_sb.all() + rb_sb.all(), writes=b0.all())
            P.add('act', lambda e: e.activation(out=eb.t[:], in_=b0.t[:, 0:8], func=AF.Exp), reads=b0.all(), writes=eb.all())
            b1 = bank()
            P.add('pe', lambda e: e.matmul(b1.t[:, 0:8], lhsT=antiI.t[:, :], rhs=eb.t[:, :], start=True, stop=True), reads=antiI.all() + eb.all(), writes=b1.all())
            P.add('dve', lambda e: e.tensor_copy(out=expBs.t[:], in_=b1.t[:, 0:8]), reads=b1.all(), writes=expBs.all())
            P.dma('sp', 'scrS', scrS[0:127, :], zero.t[0:127, :], reads=zero.all(), writes=[('scrS', 0)])
            P.dma('sp', 'scrS', scrS[255:383, :], zero.t[:, :], reads=zero.all(), writes=[('scrS', 0)])
            P.dma('sp', 'scrS', scrS[127:255, :], eb.t[:, :], reads=eb.all(), writes=[('scrS', 0)])
            sb = sb_saved
            hn_s = ssb("hn_s", [SB, D], BF16)
            hn_t[0] = hn_s
            hT_s = ssb("hT_s", [128, 8, SB], BF16)
            proj = ssb("proj_s", [SB, DIN]); projb = ssb("projb_s", [SB, 768], BF16)
            KT_s = ssb("KT_s", [64, SB * 2, 128], BF16); qT2 = ssb("qT2", [64, 8, SB], BF16)
            Es = ssb("Es", [128, 128]); Esb = ssb("Esb", [128, SB, 8], BF16)
            rec_s = ssb("rec_s", [64, 128]); aT_s = ssb("aT_s", [64, 8, SB], BF16)
            xc_s = ssb("xc_s", [SB, 1024]); tmp_s = ssb("tmp_s", [SB, 512])
            dt_s = ssb("dt_s", [128, 16]); dec_s = ssb("dec_s", [SB, 8]); xdt_s = ssb("xdt_s", [SB, 512])
            xdt_bh = ssb("xdt_bh", [128, 64]); dec_bh = ssb("dec_bh", [128, 1]); B_bh = ssb("B_bh", [128, 128]); C_bh = ssb("C_bh", [128, 128])
            y_bh = ssb("y_bh", [128, 64]); y_tok = ssb("y_tok", [SB, 512]); sz_s = ssb("sz_s", [SB, 512])
            yn_s = ssb("yn_s", [SB, 512], BF16); yT_s = ssb("yT_s", [128, 4, SB], BF16)
            x1_s = xs_in

            rmsnorm_T(xs_in.t[:, :], xs_in.all(), SB, None, hn_s, hT_s.t[:, :, :], hT_s.all(), "s1")
            colr = [(0, 512), (512, 1024), (1024, 1536), (1536, 2048), (2048, DIN)]
            for (c0, c1) in colr:
                bk = bank()
                P.add('pe', [(lambda k, bk=bk, c0=c0, c1=c1: lambda e: e.matmul(bk.t[0:SB, 0:c1 - c0], lhsT=hT_s.t[:, k, :], rhs=w_in_sb.t[:, k, c0:c1], start=(k == 0), stop=(k == 7)))(k) for k in range(8)],
                      reads=hT_s.all() + w_in_sb.all(), writes=bk.all())
                P.add('act', lambda e, bk=bk, c0=c0, c1=c1: e.activation(out=proj.t[:, c0:c1], in_=bk.t[0:SB, 0:c1 - c0], func=AF.Copy), reads=bk.all(), writes=proj.all())
            P.add('dve', lambda e: e.tensor_copy(out=projb.t[:, :], in_=proj.t[:, 0:768]), reads=proj.all(), writes=projb.all())
            P.dma('sp', 'skv0', scrKV[:, :], projb.t[:, 512:768], reads=projb.all(), writes=[('scrKV', 0)])
            P.dma('sp', 'skv', Kwb.t[127:128, :, :], scrKV[:, 0:128].rearrange("(o b) f -> o b f", o=1), reads=[('scrKV', 0)], writes=Kwb.all())
            P.dma('sp', 'skv', Vwb.t[127:128, :, :], scrKV[:, 128:256].rearrange("(o b) f -> o b f", o=1), reads=[('scrKV', 0)], writes=Vwb.all())
            P.dma('sp', 'sout', nk_s[:, 0:127, :], ck[:, 1:128, :])
            P.dma('sp', 'sout', nv_s[:, 0:127, :], cv[:, 1:128, :])
            P.dma('sp', 'sout', nk_s[:, 127, :], proj.t[:, 512:640], reads=proj.all())
            P.dma('sp', 'sout', nv_s[:, 127, :], proj.t[:, 640:768], reads=proj.all())
            P.dma('sp', 'sout', ncv_s[:, 0:2, :], sconv[:, 1:3, :])
            P.dma('sp', 'sout', ncv_s[:, 2, :], proj.t[:, 1280:2304], reads=proj.all())
            for hf in range(2):
                P.add('pool', lambda e, hf=hf: e.tensor_copy(out=Kwb.t[0:127, hf * 8:(hf + 1) * 8, :], in_=kvst.t[0:127, hf * 8:(hf + 1) * 8, :]), reads=kvst.p(hf), writes=Kwb.all())
            for hf in range(2):
                P.add('pool', lambda e, hf=hf: e.tensor_copy(out=Vwb.t[0:127, hf * 8:(hf + 1) * 8, :], in_=h0[0].t[0:127, hf * 8:(hf + 1) * 8, :]), reads=h0[0].all(), writes=Vwb.all())
            for gu, wsrc in enumerate((w_gate, w_up)):
                for k in range(8):
                    P.dma('pool', 'wcast', wgu_bf[:, :, gu, k, :].rearrange("c p j -> p c j"), wsrc[k * 128:(k + 1) * 128, :].rearrange("p (c j) -> p c j", j=128), writes=[('wgu_bf', gu * 8 + k)])
            P.dma('pool', 'wcast', wd_bf[:, :], w_down[:, :], writes=[('wd_bf', 0)])


            Tp = h0[0]
            P.dma('sp', 'expB', Tp.t[:, :, :].rearrange("p a n -> p (a n)"), bass.AP(scrS.tensor, 0, [[8, 128], [1, 2048]]), reads=[('scrS', 0)], writes=Tp.all())
            for q4 in range(4):
                bkx = bank()
                P.add('pe', lambda e, bkx=bkx, q4=q4: e.matmul(bkx.t[:, :], lhsT=antiI.t[:, :], rhs=Tp.t[:, :, :].rearrange("p a n -> p (a n)")[:, q4 * 512:(q4 + 1) * 512], start=True, stop=True), reads=antiI.all() + Tp.all(), writes=bkx.all())
                P.add('act', lambda e, bkx=bkx, q4=q4: e.activation(out=expB.t[:, :, :, :].rearrange("p a q h -> p (a q h)")[:, q4 * 512:(q4 + 1) * 512], in_=bkx.t[:, :], func=AF.Copy), reads=bkx.all(), writes=expB.all())
            bk = bank(); pv = bk.t[:].bitcast(BF16)
            P.add('pe', [(lambda hq, pv=pv: lambda e: e.transpose(out=pv[0:64, hq * SB:(hq + 1) * SB], in_=projb.t[:, hq * 64:(hq + 1) * 64], identity=identb.t[0:SB, 0:SB]))(hq) for hq in range(8)],
                  reads=projb.all() + identb.all(), writes=bk.all())
            P.add('act', lambda e, pv=pv: e.activation(out=qT2.t[:, :, :], in_=pv[0:64, 0:8 * SB].rearrange("p (h b) -> p h b", h=8), func=AF.Copy), reads=bk.all(), writes=qT2.all())
            for g4 in range(4):
                bk = bank(); pv = bk.t[:].bitcast(BF16)
                P.add('pe', [(lambda i, pv=pv, g4=g4: lambda e: e.transpose(out=pv[0:64, i * 128:(i + 1) * 128], in_=Kwb.t[:, (g4 * 8 + i) // 2, ((g4 * 8 + i) % 2) * 64:((g4 * 8 + i) % 2) * 64 + 64], identity=identb.t[:, :]))(i) for i in range(8)],
                      reads=Kwb.all() + identb.all(), writes=bk.all())
                P.add('dve', lambda e, pv=pv, g4=g4: e.tensor_copy(out=KT_s.t[:, g4 * 8:(g4 + 1) * 8, :], in_=pv[0:64, :].rearrange("p (i t) -> p i t", i=8)), reads=bk.all(), writes=KT_s.all())
            bsc = bank()
            P.add('pe', [(lambda b, h: lambda e: e.matmul(bsc.t[:, b * 8 + h * 4:b * 8 + h * 4 + 4], lhsT=KT_s.t[:, b * 2 + h, :], rhs=qT2.t[:, h * 4:(h + 1) * 4, b], start=True, stop=True))(b, h) for b in range(SB) for h in range(2)],
                  reads=KT_s.all() + qT2.all(), writes=bsc.all())
            P.add('act', lambda e: e.activation(out=Es.t[:, :], in_=bsc.t[:, 0:128], func=AF.Exp, scale=SCALE), reads=bsc.all(), writes=Es.all())
            P.add('dve', lambda e: e.tensor_tensor(out=Esb.t[:, :, :], in0=Es.t[:, :].rearrange("p (b h) -> p b h", b=SB), in1=bc(expBs.t[:, :].unsqueeze(1), [128, SB, 8]), op=ALU.mult),
                  reads=Es.all() + expBs.all(), writes=Esb.all())
            bden = bank(); bos = bank()
            P.add('pe', lambda e: e.matmul(bden.t[0:64, 0:128], lhsT=onesb.t[:, :], rhs=Esb.t[:, :, :].rearrange("p b h -> p (b h)"), start=True, stop=True), reads=onesb.all() + Esb.all(), writes=bden.all())
            P.add('pe', [(lambda b, h: lambda e: e.matmul(bos.t[0:64, b * 8 + h * 4:b * 8 + h * 4 + 4], lhsT=Vwb.t[:, b, h * 64:(h + 1) * 64], rhs=Esb.t[:, b, h * 4:(h + 1) * 4], start=True, stop=True))(b, h) for b in range(SB) for h in range(2)],
                  reads=Vwb.all() + Esb.all(), writes=bos.all())
            P.add('dve', lambda e: e.tensor_tensor(out=rec_s.t[:, :].rearrange("p (b h) -> p b h", b=SB), in0=bden.t[0:64, 0:128].rearrange("p (b h) -> p b h", b=SB), in1=bc(esink.t[0:64, :].unsqueeze(1), [64, SB, 8]), op=ALU.add),
                  reads=bden.all() + esink.all(), writes=rec_s.all())
            P.add('dve', lambda e: e.reciprocal(out=rec_s.t[:, :], in_=rec_s.t[:, :]), reads=rec_s.all(), writes=rec_s.all())
            P.add('dve', lambda e: e.tensor_tensor(out=aT_s.t[:, :, :].rearrange("p h b -> p b h"), in0=bos.t[0:64, 0:128].rearrange("p (b h) -> p b h", b=SB), in1=rec_s.t[:, :].rearrange("p (b h) -> p b h", b=SB), op=ALU.mult),
                  reads=bos.all() + rec_s.all(), writes=aT_s.all())
            bq = bank()
            P.add('pe', [(lambda q4: lambda e: e.matmul(bq.t[:, 0:256], lhsT=SelQ.t[:, q4, :], rhs=proj.t[:, 1280 + q4 * 256:1280 + (q4 + 1) * 256], start=(q4 == 0), stop=(q4 == 3)))(q4) for q4 in range(4)],
                  reads=SelQ.all() + proj.all(), writes=bq.all())
            P.add('dve', lambda e: e.tensor_tensor(out=accq.t[:, :], in0=bq.t[:, 0:256], in1=cw_b.t[:, 3, :], op=ALU.mult), reads=bq.all() + cw_b.all(), writes=accq.all())
            P.add('dve', lambda e: e.tensor_tensor(out=accq.t[:, :], in0=accq.t[:, :], in1=cb_b.t[:, :], op=ALU.add), reads=accq.all() + cb_b.all(), writes=accq.all())
            for j in range(3):
                P.add('dve', lambda e, j=j: e.tensor_tensor(out=tmpq.t[:, :], in0=cbuf.t[:, j, :], in1=cw_b.t[:, j, :], op=ALU.mult), reads=cbuf.all() + cw_b.all(), writes=tmpq.all())
                P.add('dve', lambda e: e.tensor_tensor(out=accq.t[:, :], in0=accq.t[:, :], in1=tmpq.t[:, :], op=ALU.add), reads=accq.all() + tmpq.all(), writes=accq.all())
            P.add('act', lambda e: e.activation(out=tmpq.t[:, :], in_=accq.t[:, :], func=AF.Silu), reads=accq.all(), writes=tmpq.all())
            bq2 = [bank(), bank()]
            P.add('pe', [(lambda q4: lambda e: e.matmul(bq2[q4 // 2].t[0:SB, (q4 % 2) * 256:(q4 % 2 + 1) * 256], lhsT=SelQT.t[:, q4, :], rhs=tmpq.t[:, :], start=True, stop=True))(q4) for q4 in range(4)],
                  reads=SelQT.all() + tmpq.all(), writes=bq2[0].all() + bq2[1].all())
            P.add('act', [(lambda i: lambda e: e.activation(out=xc_s.t[:, i * 512:(i + 1) * 512], in_=bq2[i].t[0:SB, :], func=AF.Copy))(i) for i in range(2)], reads=bq2[0].all() + bq2[1].all(), writes=xc_s.all())
            softplus_dt(proj.t[:, 2304:2312], proj.all(), SB, dt_s)
            P.add('dve', lambda e: e.tensor_tensor(out=dec_s.t[:, :], in0=dt_s.t[0:SB, 0:8], in1=A_b.t[0:SB, :], op=ALU.mult), reads=dt_s.all() + A_b.all(), writes=dec_s.all())
            P.add('act', lambda e: e.activation(out=dec_s.t[:, :], in_=dec_s.t[:, :], func=AF.Exp), reads=dec_s.all(), writes=dec_s.all())
            P.add('dve', lambda e: e.tensor_tensor(out=xdt_s.t[:, :].rearrange("p (h d) -> p h d", h=8), in0=xc_s.t[:, 0:512].rearrange("p (h d) -> p h d", h=8), in1=bc(dt_s.t[0:SB, 0:8].unsqueeze(2), [SB, 8, 64]), op=ALU.mult),
                  reads=xc_s.all() + dt_s.all(), writes=xdt_s.all())
            bsel = bank(); bsel2 = bank()
            P.add('pe', [(lambda h: lambda e: e.matmul(bsel.t[:, 0:64], lhsT=SelH.t[0:SB, h, :], rhs=xdt_s.t[:, h * 64:(h + 1) * 64], start=(h == 0), stop=(h == 7)))(h) for h in range(8)]
                  + [(lambda h: lambda e: e.matmul(bsel.t[:, 64:65], lhsT=SelH.t[0:SB, h, :], rhs=dec_s.t[:, h:h + 1], start=(h == 0), stop=(h == 7)))(h) for h in range(8)],
                  reads=SelH.all() + xdt_s.all() + dec_s.all(), writes=bsel.all())
            P.add('pe', [(lambda h: lambda e: e.matmul(bsel2.t[:, 0:128], lhsT=SelH.t[0:SB, h, :], rhs=xc_s.t[:, 512 + (h // 4) * 128:512 + (h // 4 + 1) * 128], start=(h == 0), stop=(h == 7)))(h) for h in range(8)]
                  + [(lambda h: lambda e: e.matmul(bsel2.t[:, 128:256], lhsT=SelH.t[0:SB, h, :], rhs=xc_s.t[:, 768 + (h // 4) * 128:768 + (h // 4 + 1) * 128], start=(h == 0), stop=(h == 7)))(h) for h in range(8)],
                  reads=SelH.all() + xc_s.all(), writes=bsel2.all())
            P.add('dve', [lambda e: e.tensor_copy(out=xdt_bh.t[:, :], in_=bsel.t[:, 0:64]), lambda e: e.tensor_copy(out=dec_bh.t[:, :], in_=bsel.t[:, 64:65])], reads=bsel.all(), writes=xdt_bh.all() + dec_bh.all())
            P.add('dve', [lambda e: e.tensor_copy(out=B_bh.t[:, :], in_=bsel2.t[:, 0:128]), lambda e: e.tensor_copy(out=C_bh.t[:, :], in_=bsel2.t[:, 128:256])], reads=bsel2.all(), writes=B_bh.all() + C_bh.all())
            for hf in range(4):
                hh = h0[hf % 2]; sse = 'dve'
                P.dma('sp', 'sh0_%d' % (hf % 2), hh.t[:, :, :].rearrange("p a n -> p (a n)"), sssm[:, hf * 2048:(hf + 1) * 2048], writes=hh.all())
                P.add(sse, lambda e, hf=hf: e.tensor_tensor(out=otmp.t[:, :, :], in0=bc(B_bh.t[:, :].unsqueeze(1), [128, 16, 128]), in1=bc(xdt_bh.t[:, hf * 16:(hf + 1) * 16].unsqueeze(2), [128, 16, 128]), op=ALU.mult),
                      reads=B_bh.all() + xdt_bh.all(), writes=otmp.all())
                P.add('dve', lambda e, hh=hh: e.scalar_tensor_tensor(out=hh.t[:, :, :], in0=hh.t[:, :, :], scalar=dec_bh.t[:, 0:1], in1=otmp.t[:, :, :], op0=ALU.mult, op1=ALU.add),
                      reads=hh.all() + dec_bh.all() + otmp.all(), writes=hh.all())
                P.dma('sp', 'snss%d' % (hf % 2), nss_s[:, hf * 2048:(hf + 1) * 2048], hh.t[:, :, :].rearrange("p a n -> p (a n)"), reads=hh.all())
                P.add(sse, lambda e, hh=hh: e.tensor_tensor(out=otmp.t[:, :, :], in0=hh.t[:, :, :], in1=bc(C_bh.t[:, :].unsqueeze(1), [128, 16, 128]), op=ALU.mult),
                      reads=hh.all() + C_bh.all(), writes=otmp.all())
                P.add('dve', lambda e, hf=hf: e.tensor_reduce(out=y_bh.t[:, hf * 16:(hf + 1) * 16], in_=otmp.t[:, :, :], axis=AX.X, op=ALU.add), reads=otmp.all(), writes=y_bh.all())
            bsel3 = bank()
            P.add('pe', [(lambda h: lambda e: e.matmul(bsel3.t[0:SB, h * 64:(h + 1) * 64], lhsT=SelHT.t[:, h, :], rhs=y_bh.t[:, :], start=True, stop=True))(h) for h in range(8)],
                  reads=SelHT.all() + y_bh.all(), writes=bsel3.all())
            P.add('dve', lambda e: e.tensor_copy(out=y_tok.t[:, :], in_=bsel3.t[0:SB, :]), reads=bsel3.all(), writes=y_tok.all())
            P.add('dve', lambda e: e.tensor_tensor(out=tmp_s.t[:, :].rearrange("p (h d) -> p h d", h=8), in0=xc_s.t[:, 0:512].rearrange("p (h d) -> p h d", h=8), in1=bc(D_b.t[0:SB, :].unsqueeze(2), [SB, 8, 64]), op=ALU.mult),
                  reads=xc_s.all() + D_b.all(), writes=tmp_s.all())
            P.add('dve', lambda e: e.tensor_tensor(out=y_tok.t[:, :], in0=y_tok.t[:, :], in1=tmp_s.t[:, :], op=ALU.add), reads=y_tok.all() + tmp_s.all(), writes=y_tok.all())
            P.add('act', lambda e: e.activation(out=sz_s.t[:, :], in_=proj.t[:, 768:1280], func=AF.Silu), reads=proj.all(), writes=sz_s.all())
            P.add('dve', lambda e: e.tensor_tensor(out=y_tok.t[:, :], in0=y_tok.t[:, :], in1=sz_s.t[:, :], op=ALU.mult), reads=y_tok.all() + sz_s.all(), writes=y_tok.all())

            def group_norm(y, nrow, yn, ssg):
                P.add('act', [(lambda g: lambda e: e.activation(out=yn.t[0:nrow, g * 256:(g + 1) * 256], in_=y.t[0:nrow, g * 256:(g + 1) * 256], func=AF.Square, accum_out=ssg.t[0:nrow, g:g + 1]))(g) for g in range(2)],
                      reads=y.all(), writes=yn.all() + ssg.all())
                P.add('act', lambda e: e.activation(out=ssg.t[0:nrow, 2:4], in_=ssg.t[0:nrow, 0:2], func=AF.Ln, bias=EPS, scale=1.0 / 256), reads=ssg.all(), writes=ssg.all())
                P.add('act', lambda e: e.activation(out=ssg.t[0:nrow, 4:6], in_=ssg.t[0:nrow, 2:4], func=AF.Exp, scale=-0.5), reads=ssg.all(), writes=ssg.all())
                P.add('dve', [(lambda g: lambda e: e.tensor_scalar(out=yn.t[0:nrow, g * 256:(g + 1) * 256], in0=y.t[0:nrow, g * 256:(g + 1) * 256], scalar1=ssg.t[0:nrow, 4 + g:5 + g], scalar2=None, op0=ALU.mult))(g) for g in range(2)],
                      reads=y.all() + ssg.all(), writes=yn.all())
            group_norm(y_tok, SB, yn_s, sm("ssg_s"))
            bk = bank(); pv = bk.t[:].bitcast(BF16)
            P.add('pe', [(lambda c, pv=pv: lambda e: e.transpose(out=pv[:, c * SB:(c + 1) * SB], in_=yn_s.t[:, c * 128:(c + 1) * 128], identity=identb.t[0:SB, 0:SB]))(c) for c in range(4)],
                  reads=yn_s.all() + identb.all(), writes=bk.all())
            P.add('act', lambda e, pv=pv: e.activation(out=yT_s.t[:, :, :], in_=pv[:, 0:4 * SB].rearrange("p (c b) -> p c b", c=4), func=AF.Copy), reads=bk.all(), writes=yT_s.all())

            def out_proj(aT, yT, nrow, xin_ap, xin_keys, x1_ap, x1_keys):
                for half in range(2):
                    bk = bank()
                    fns = [(lambda hq, bk=bk, half=half: lambda e: e.matmul(bk.t[0:nrow, :], lhsT=aT.t[:, hq, 0:nrow], rhs=w_out_a.t[:, hq, half * 512:(half + 1) * 512], start=(hq == 0), stop=False))(hq) for hq in range(8)]
                    fns += [(lambda c, bk=bk, half=half: lambda e: e.matmul(bk.t[0:nrow, :], lhsT=yT.t[:, c, 0:nrow], rhs=w_out_s.t[:, c, half * 512:(half + 1) * 512], start=False, stop=(c == 3)))(c) for c in range(4)]
                    P.add('pe', fns, reads=aT.all() + yT.all() + w_out_a.all() + w_out_s.all(), writes=bk.all())
                    P.add('dve', lambda e, bk=bk, half=half: e.tensor_tensor(out=x1_ap[:, half * 512:(half + 1) * 512], in0=bk.t[0:nrow, :], in1=xin_ap[:, half * 512:(half + 1) * 512], op=ALU.add),
                          reads=bk.all() + xin_keys, writes=x1_keys)
            out_proj(aT_s, yT_s, SB, xs_in.t[:, :], xs_in.all(), x1_s.t[:, :], x1_s.all())
            rmsnorm_T(x1_s.t[:, :], x1_s.all(), SB, w2b, hn_s, h2T_s.t[:, :, :], h2T_s.all(), "s2")

            P.barrier()
        P.marks['sample'] = len(P.ops)

        xin = [sb("xin%d" % i, [128, D]) for i in range(2)]
        hn = sb("hn", [128, D], BF16); hn_t[0] = hn
        hT = sb("hT", [128, 8, 128], BF16)
        qTs = [sb("qT%d" % i, [128, 4, 128], BF16) for i in range(2)]
        Kh = [sb("Kh%d" % i, [128, 2, 128], BF16) for i in range(3)]
        Vh = [sb("Vh%d" % i, [128, 2, 128], BF16) for i in range(3)]
        xs_tok = sb("xs_tok", [128, 512], BF16)
        xbcT = sb("xbcT", [128, 8, 131], BF16)
        xcT = sb("xcT", [128, 8, 128], BF16, nparts=8)
        Btok = sb("Btok", [128, 256], BF16)
        szts = [sb("szt%d" % i, [128, 512]) for i in range(2)]; dtts = [sb("dtt%d" % i, [128, 16]) for i in range(2)]
        scals = [sm("scal", 64), sb("scal1", [128, 64])]
        Lt = sb("Lt", [128, 8, 128]); MT = sb("MT", [128, 8, 128], BF16)
        xdt = sb("xdt", [128, 512], BF16); xdte = sb("xdte", [128, 512], BF16)
        yt = sb("yt", [128, 512]); ytmp = sb("ytmp", [128, 512]); yn = sb("yn", [128, 512], BF16); yT = sb("yT", [128, 4, 128], BF16)
        Eraw = sb("Eraw", [128, 4, 128]); ET = [sb("ET%d" % i, [128, 8, 128], BF16) for i in range(2)]
        rec = sb("rec", [64, 1, 512]); aT = sb("aT", [64, 8, 128], BF16)
        x1 = sb("x1", [128, GB, D], nparts=GB); h2T = sb("h2T", [128, 8, GB * 128], BF16, nparts=GB)
        actT = sb("actT", [128, NFC // 2, GB * 128], BF16, nparts=NFC // 2)
        kvout = ytmp; ncv = sb("ncv", [128, 8, 3]); ncvT = yt
        nss_sb = Lt
        ffn_sg["m"] = [yt] * 2

        P.add('dve', lambda e: e.memset(stateT.t[:, :], 0.0), writes=stateT.all())
        P.add('dve', lambda e: e.memset(stateTb.t[:, :], 0.0), writes=stateTb.all())
        P.add('dve', lambda e: e.memset(xbcT.t[:, :, :], 0.0), writes=xbcT.all())
        for i in range(3):
            P.add('dve', lambda e, i=i: e.memset(Vh[i].t[:, :, 64:128], 1.0), writes=Vh[i].all())

        blk_ctr = [0]

        def block(xsrc, t, mode):
            main = mode == 'main'
            gi = blk_ctr[0]
            blk_ctr[0] += 1
            bank_mode[0] = gi % 2
            bank_sub[0] = None
            xt = xin[gi % 2]
            qT = qTs[gi % 2]; szt = szts[gi % 2]; dtt = dtts[gi % 2]; scal = scals[gi % 2]
            pe2 = 'pool' if main else 'dve'
            P.dma('sp', 'xin%d' % (gi % 2), xt.t[:, :], xsrc[t * 128:(t + 1) * 128, :], writes=xt.all())
            rmsnorm_T(xt.t[:, :], xt.all(), 128, None, hn, hT.t[:, :, :], hT.all(), "m1", cp='act' if main else 'dve')
            chunks = []
            if main:
                chunks += [(c * 128, 'q', c) for c in range(4)]
            if main or mode == 'prelast':
                chunks += [(DIN + h * 128, 'k', h) for h in range(2)]
            nx = 8 if (main or mode == 'prelast') else 6
            chunks += [(1280 + c * 128, 'x', c) for c in range(nx)]
            slot = gi % 3
            i = 0
            while i < len(chunks):
                grp = chunks[i:i + 4]
                i += 4
                bk = bank()
                fns = []
                for gi_, (c0, kind, idx) in enumerate(grp):
                    fns += [(lambda k, gi_=gi_, c0=c0, bk=bk: lambda e: e.matmul(bk.t[:, gi_ * 128:(gi_ + 1) * 128], lhsT=w_in_sb.t[:, k, c0:c0 + 128], rhs=hT.t[:, k, :], start=(k == 0), stop=(k == 7)))(k) for k in range(8)]
                P.add('pe', fns, reads=w_in_sb.all() + hT.all(), writes=bk.all())
                j = 0
                while j < len(grp):
                    kind = grp[j][1]
                    j2 = j
                    while j2 < len(grp) and grp[j2][1] == kind:
                        j2 += 1
                    i0 = grp[j][2]
                    nn = j2 - j
                    src = bk.t[:, j * 128:j2 * 128].rearrange("p (c t) -> p c t", c=nn)
                    if kind == 'q':
                        P.add('act', lambda e, src=src, i0=i0, nn=nn: e.activation(out=qT.t[:, i0:i0 + nn, :], in_=src, func=AF.Copy), reads=bk.all(), writes=qT.all())
                    elif kind == 'k':
                        P.add('dve', lambda e, src=src, i0=i0, nn=nn: e.tensor_copy(out=Kh[slot].t[:, i0:i0 + nn, :], in_=src), reads=bk.all(), writes=Kh[slot].all())
                    elif main:
                        P.add('act', lambda e, src=src, i0=i0, nn=nn: e.activation(out=xbcT.t[:, i0:i0 + nn, 3:131], in_=src, func=AF.Copy), reads=bk.all(), writes=xbcT.all())
                        if t == NBLK - 1:
                            P.add('dve', lambda e, src=src, i0=i0, nn=nn: e.tensor_copy(out=ncv.t[:, i0:i0 + nn, :], in_=src[:, :, 125:128]), reads=bk.all(), writes=ncv.all())
                    else:
                        P.add('dve', lambda e, src=src, i0=i0, nn=nn: e.tensor_copy(out=xbcT.t[:, i0:i0 + nn, 3:131], in_=src), reads=bk.all(), writes=xbcT.all())
                    j = j2
            bdt = None
            if main:
                bA, bB = bank(), bank()
                P.add('pe', [(lambda k: lambda e: e.matmul(bA.t[:, :], lhsT=hT.t[:, k, :], rhs=w_in_sb.t[:, k, 640:1152], start=(k == 0), stop=(k == 7)))(k) for k in range(8)],
                      reads=w_in_sb.all() + hT.all(), writes=bA.all())
                P.add('pe', [(lambda k: lambda e: e.matmul(bB.t[:, 0:128], lhsT=hT.t[:, k, :], rhs=w_in_sb.t[:, k, 1152:1280], start=(k == 0), stop=(k == 7)))(k) for k in range(8)]
                      + [(lambda k: lambda e: e.matmul(bB.t[:, 128:136], lhsT=hT.t[:, k, :], rhs=w_in_sb.t[:, k, 2304:2312], start=(k == 0), stop=(k == 7)))(k) for k in range(8)],
                      reads=w_in_sb.all() + hT.all(), writes=bB.all())
                P.add('dve', lambda e: e.tensor_copy(out=Vh[slot].t[:, :, 0:64], in_=bA.t[:, 0:128].rearrange("p (h d) -> p h d", h=2)), reads=bA.all(), writes=Vh[slot].all())
                P.add('act', lambda e: e.activation(out=szt.t[:, 0:384], in_=bA.t[:, 128:512], func=AF.Silu), reads=bA.all(), writes=szt.all())
                P.add('act', lambda e: e.activation(out=szt.t[:, 384:512], in_=bB.t[:, 0:128], func=AF.Silu), reads=bB.all(), writes=szt.all())
                softplus_dt(bB.t[:, 128:136], bB.all(), 128, dtt)
                if t == NBLK - 1:
                    bKt = bank()
                    P.add('pe', [(lambda k: lambda e: e.matmul(bKt.t[:, 0:128], lhsT=hT.t[:, k, :], rhs=w_in_sb.t[:, k, 512:640], start=(k == 0), stop=(k == 7)))(k) for k in range(8)],
                          reads=w_in_sb.all() + hT.all(), writes=bKt.all())
                    P.add('dve', lambda e: e.tensor_copy(out=kvout.t[:, 0:128], in_=bKt.t[:, 0:128]), reads=bKt.all(), writes=kvout.all())
                    P.add('dve', lambda e: e.tensor_copy(out=kvout.t[:, 128:256], in_=bA.t[:, 0:128]), reads=bA.all(), writes=kvout.all())
                    P.dma('sp', 'pout', nk_p[:, :], kvout.t[:, 0:128], reads=kvout.all())
                    P.dma('sp', 'pout', nv_p[:, :], kvout.t[:, 128:256], reads=kvout.all())
            else:
                bB = bank()
                fns = [(lambda k: lambda e: e.matmul(bB.t[:, 128:136], lhsT=hT.t[:, k, :], rhs=w_in_sb.t[:, k, 2304:2312], start=(k == 0), stop=(k == 7)))(k) for k in range(8)]
                if mode == 'prelast':
                    fns += [(lambda k: lambda e: e.matmul(bB.t[:, 0:128], lhsT=hT.t[:, k, :], rhs=w_in_sb.t[:, k, 640:768], start=(k == 0), stop=(k == 7)))(k) for k in range(8)]
                P.add('pe', fns, reads=w_in_sb.all() + hT.all(), writes=bB.all())
                if mode == 'prelast':
                    P.add('dve', lambda e: e.tensor_copy(out=Vh[slot].t[:, :, 0:64], in_=bB.t[:, 0:128].rearrange("p (h d) -> p h d", h=2)), reads=bB.all(), writes=Vh[slot].all())
                softplus_dt(bB.t[:, 128:136], bB.all(), 128, dtt)
            if main:
                P.capture()
                bank_sub[0] = 0
            for c4 in range(0, nx, 4):
                bk = bank()
                ncc = min(4, nx - c4)
                fns = []
                for ci in range(ncc):
                    c = c4 + ci
                    fns += [(lambda j, c=c, ci=ci, bk=bk: lambda e: e.matmul(bk.t[:, ci * 128:(ci + 1) * 128], lhsT=convdiag.t[:, c, j, :], rhs=xbcT.t[:, c, j:j + 128], start=(j == 0), stop=(j == 3)))(j) for j in range(4)]
                P.add('pe', fns, reads=convdiag.all() + xbcT.all(), writes=bk.all())
                P.add('act', [(lambda ci, bk=bk, c4=c4: lambda e: e.activation(out=xcT.t[:, c4 + ci, :], in_=bk.t[:, ci * 128:(ci + 1) * 128], func=AF.Silu, bias=cbT.t[:, c4 + ci:c4 + ci + 1]))(ci) for ci in range(ncc)],
                      reads=bk.all() + cbT.all(), writes=xcT.p(*range(c4, c4 + ncc)))
            P.add(pe2, lambda e: e.tensor_copy(out=xbcT.t[:, :, 0:3], in_=xbcT.t[:, :, 128:131]), reads=xbcT.all(), writes=xbcT.all())
            bX = bank(); pX = bX.t[:].bitcast(BF16)
            P.add('pe', [(lambda c: lambda e: e.transpose(out=pX[:, c * 128:(c + 1) * 128], in_=xcT.t[:, c, :], identity=identb.t[:, :]))(c) for c in range(6)],
                  reads=xcT.p(0, 1, 2, 3, 4, 5) + identb.all(), writes=bX.all())
            if main:
                P.add('act', lambda e: e.activation(out=Btok.t[:, :], in_=pX[:, 512:768], func=AF.Copy), reads=bX.all(), writes=Btok.all())
                P.add('act', lambda e: e.activation(out=xs_tok.t[:, :], in_=pX[:, 0:512], func=AF.Copy), reads=bX.all(), writes=xs_tok.all())
            else:
                P.add('dve', lambda e: e.tensor_copy(out=Btok.t[:, :], in_=pX[:, 512:768]), reads=bX.all(), writes=Btok.all())
                P.add('dve', lambda e: e.tensor_copy(out=xs_tok.t[:, :], in_=pX[:, 0:512]), reads=bX.all(), writes=xs_tok.all())
            a_ = scal.t[:, 0:8]
            P.add('dve', lambda e: e.tensor_tensor(out=a_, in0=dtt.t[:, 0:8], in1=A_b.t[:, :], op=ALU.mult), reads=dtt.all() + A_b.all(), writes=scal.all())
            if main:
                P.add('pool', lambda e: e.tensor_tensor(out=Lt.t[:, :, :], in0=bc(tri.t[:, :].unsqueeze(1), [128, 8, 128]), in1=bc(a_.unsqueeze(2), [128, 8, 128]), op=ALU.mult),
                      reads=tri.all() + scal.all(), writes=Lt.all())
                bcs = bank()
                P.add('pe', lambda e: e.matmul(bcs.t[:, 0:8], lhsT=tri.t[:, :], rhs=a_, start=True, stop=True), reads=tri.all() + scal.all(), writes=bcs.all())
                P.add('dve', lambda e: e.tensor_scalar(out=scal.t[:, 8:16], in0=bcs.t[:, 0:8], scalar1=-1.0, scalar2=None, op0=ALU.mult), reads=bcs.all(), writes=scal.all())
                P.add('act', lambda e: e.activation(out=scal.t[:, 16:24], in_=bcs.t[:, 0:8], func=AF.Exp), reads=bcs.all(), writes=scal.all())
                bC0, bC1 = bank(), bank()
                for hf, bk in enumerate((bC0, bC1)):
                    P.add('pe', [lambda e, bk=bk, hf=hf: e.matmul(bk.t[:, :], lhsT=onesf.t[:, :], rhs=Lt.t[:, hf * 4:(hf + 1) * 4, :].rearrange("p h i -> p (h i)"), start=True, stop=False),
                                 lambda e, bk=bk: e.matmul(bk.t[:, :], lhsT=identb.t[:, :], rhs=negm.t[:, :, :].rearrange("p h i -> p (h i)"), start=False, stop=True)],
                          reads=onesf.all() + Lt.all() + identb.all() + negm.all(), writes=bk.all())
                for hf, bk in enumerate((bC0, bC1)):
                    P.add('act', lambda e, bk=bk, hf=hf: e.activation(out=scal.t[:, 24 + hf * 4:28 + hf * 4], in_=bk.t[:, :].rearrange("p (h i) -> p h i", h=4)[:, :, 127], func=AF.Exp), reads=bk.all(), writes=scal.all())
                    P.add('act', [(lambda hh, bk=bk, hf=hf: lambda e: e.activation(out=Lt.t[:, hf * 4 + hh, :], in_=bk.t[:, hh * 128:(hh + 1) * 128], func=AF.Exp, bias=scal.t[:, 8 + hf * 4 + hh:9 + hf * 4 + hh]))(hh) for hh in range(4)],
                          reads=bk.all() + scal.all(), writes=Lt.all())
                bCB = bank()
                P.add('pe', [(lambda g: lambda e: e.matmul(bCB.t[:, g * 128:(g + 1) * 128], lhsT=xcT.t[:, 4 + g, :], rhs=xcT.t[:, 6 + g, :], start=True, stop=True))(g) for g in range(2)],
                      reads=xcT.p(4, 5, 6, 7), writes=bCB.all())
                P.add('dve', lambda e: e.tensor_tensor(out=MT.t[:, :, :].rearrange("p (g r) i -> p g r i", g=2), in0=Lt.t[:, :, :].rearrange("p (g r) i -> p g r i", g=2),
                                                       in1=bc(bCB.t[:, 0:256].rearrange("p (g i) -> p g i", g=2).unsqueeze(2), [128, 2, 4, 128]), op=ALU.mult),
                      reads=Lt.all() + bCB.all(), writes=MT.all())
                dte_ap = Lt.t[:, :, 127]
                dte_keys = Lt.all()
                cd_ap = scal.t[:, 24:32]
            else:
                bsf = bank()
                P.add('pe', [lambda e: e.matmul(bsf.t[:, 0:8], lhsT=striu.t[:, :], rhs=a_, start=True, stop=True),
                             lambda e: e.matmul(bsf.t[:, 8:16], lhsT=onesf.t[:, :], rhs=a_, start=True, stop=True)], reads=striu.all() + onesf.all() + scal.all(), writes=bsf.all())
                P.add('act', lambda e: e.activation(out=scal.t[:, 16:32], in_=bsf.t[:, 0:16], func=AF.Exp), reads=bsf.all(), writes=scal.all())
                dte_ap = scal.t[:, 16:24]
                dte_keys = scal.all()
                cd_ap = scal.t[:, 24:32]
            P.add('dve', lambda e: e.tensor_tensor(out=scal.t[:, 32:40], in0=dtt.t[:, 0:8], in1=dte_ap, op=ALU.mult), reads=dtt.all() + dte_keys, writes=scal.all())
            pXs = xs_tok.t[:, :].rearrange("p (h d) -> p h d", h=8)
            P.add(pe2, lambda e: e.tensor_tensor(out=xdte.t[:, :].rearrange("p (h d) -> p h d", h=8), in0=pXs, in1=bc(scal.t[:, 32:40].unsqueeze(2), [128, 8, 64]), op=ALU.mult),
                  reads=xs_tok.all() + scal.all(), writes=xdte.all())
            if main:
                P.add('pool', lambda e: e.tensor_tensor(out=xdt.t[:, :].rearrange("p (h d) -> p h d", h=8), in0=pXs, in1=bc(dtt.t[:, 0:8].unsqueeze(2), [128, 8, 64]), op=ALU.mult),
                      reads=xs_tok.all() + dtt.all(), writes=xdt.all())
                bY, bYo = bank(), bank()
                P.add('pe', [(lambda h: lambda e: e.matmul(bY.t[:, h * 64:(h + 1) * 64], lhsT=MT.t[:, h, :], rhs=xdt.t[:, h * 64:(h + 1) * 64], start=True, stop=True))(h) for h in range(8)],
                      reads=MT.all() + xdt.all(), writes=bY.all())
                P.add('pe', [(lambda g: lambda e: e.matmul(bYo.t[:, g * 256:(g + 1) * 256], lhsT=xcT.t[:, 6 + g, :], rhs=stateTb.t[:, g * 256:(g + 1) * 256], start=True, stop=True))(g) for g in range(2)],
                      reads=xcT.p(6, 7) + stateTb.all(), writes=bYo.all())
                P.add('dve', lambda e: e.tensor_tensor(out=yt.t[:, :].rearrange("p (h d) -> p h d", h=8), in0=bYo.t[:, :].rearrange("p (h d) -> p h d", h=8), in1=bc(scal.t[:, 16:24].unsqueeze(2), [128, 8, 64]), op=ALU.mult),
                      reads=bYo.all() + scal.all(), writes=yt.all())
                P.add('dve', lambda e: e.tensor_tensor(out=yt.t[:, :], in0=yt.t[:, :], in1=bY.t[:, :], op=ALU.add), reads=yt.all() + bY.all(), writes=yt.all())
                P.add('pool', lambda e: e.tensor_tensor(out=ytmp.t[:, :].rearrange("p (h d) -> p h d", h=8), in0=pXs, in1=bc(D_b.t[:, :].unsqueeze(2), [128, 8, 64]), op=ALU.mult),
                      reads=xs_tok.all() + D_b.all(), writes=ytmp.all())
                P.add('dve', lambda e: e.tensor_tensor(out=yt.t[:, :], in0=yt.t[:, :], in1=ytmp.t[:, :], op=ALU.add), reads=yt.all() + ytmp.all(), writes=yt.all())
                P.add('dve', lambda e: e.tensor_tensor(out=yt.t[:, :], in0=yt.t[:, :], in1=szt.t[:, :], op=ALU.mult), reads=yt.all() + szt.all(), writes=yt.all())
                group_norm(yt, 128, yn, sm("ssg_m"))
                bk = bank(); pv = bk.t[:].bitcast(BF16)
                P.add('pe', [(lambda c, pv=pv: lambda e: e.transpose(out=pv[:, c * 128:(c + 1) * 128], in_=yn.t[:, c * 128:(c + 1) * 128], identity=identb.t[:, :]))(c) for c in range(4)],
                      reads=yn.all() + identb.all(), writes=bk.all())
                P.add('act', lambda e, pv=pv: e.activation(out=yT.t[:, :, :], in_=pv[:, 0:512].rearrange("p (c t) -> p c t", c=4), func=AF.Copy), reads=bk.all(), writes=yT.all())
            bS = bank()
            P.add('pe', [(lambda g: lambda e: e.matmul(bS.t[:, g * 256:(g + 1) * 256], lhsT=Btok.t[:, g * 128:(g + 1) * 128], rhs=xdte.t[:, g * 256:(g + 1) * 256], start=True, stop=True))(g) for g in range(2)],
                  reads=Btok.all() + xdte.all(), writes=bS.all())
            P.add(pe2, lambda e: e.tensor_tensor(out=stateT.t[:, :].rearrange("p (h d) -> p h d", h=8), in0=stateT.t[:, :].rearrange("p (h d) -> p h d", h=8), in1=bc(cd_ap.unsqueeze(2), [128, 8, 64]), op=ALU.mult),
                  reads=stateT.all() + scal.all(), writes=stateT.all())
            P.add('dve', lambda e: e.tensor_tensor(out=stateT.t[:, :], in0=stateT.t[:, :], in1=bS.t[:, :], op=ALU.add), reads=stateT.all() + bS.all(), writes=stateT.all())
            if mode == 'prelast':
                P.add('dve', lambda e: e.tensor_scalar(out=stateT.t[:, :], in0=stateT.t[:, :], scalar1=flag.t[:, 0:1], scalar2=None, op0=ALU.mult), reads=stateT.all() + flag.all(), writes=stateT.all())
                P.add('dve', lambda e: e.tensor_scalar(out=xbcT.t[:, :, 0:3], in0=xbcT.t[:, :, 0:3], scalar1=flag.t[:, 0:1], scalar2=None, op0=ALU.mult), reads=xbcT.all() + flag.all(), writes=xbcT.all())
            if main or mode == 'prelast':
                P.add('act', lambda e: e.activation(out=stateTb.t[:, :], in_=stateT.t[:, :], func=AF.Copy), reads=stateT.all(), writes=stateTb.all())
            if not main:
                return
            ssd_ops = P.end_capture()
            P.capture()
            bank_sub[0] = 1
            pslot = (gi - 1) % 3
            for kb, sl in enumerate((pslot, slot)):
                bS0, bS1 = bank(), bank()
                for par, bk in enumerate((bS0, bS1)):
                    P.add('pe', [(lambda hh, j, bk=bk, par=par, sl=sl: lambda e: e.matmul(bk.t[:, (hh * 2 + j) * 128:(hh * 2 + j + 1) * 128], lhsT=Kh[sl].t[par * 64:par * 64 + 64, hh, :], rhs=qT.t[par * 64:par * 64 + 64, hh * 2 + j, :], start=True, stop=True))(hh, j) for hh in range(2) for j in range(2)],
                          reads=Kh[sl].all() + qT.all(), writes=bk.all())
                ebi = 1 - kb
                for par, bk in enumerate((bS0, bS1)):
                    P.add('act', lambda e, bk=bk: e.activation(out=Eraw.t[:, :, :], in_=bk.t[:, :].rearrange("p (c q) -> p c q", c=4), func=AF.Exp, scale=SCALE), reads=bk.all(), writes=Eraw.all())
                    P.add('dve', lambda e, kb=kb, ebi=ebi, par=par: e.tensor_tensor(out=ET[kb].t[:, :, :].rearrange("p (c par) q -> p par c q", par=2)[:, par], in0=Eraw.t[:, :, :],
                                                                            in1=expB.t[:, ebi, :, :].rearrange("p q (c par) -> p par c q", par=2)[:, par], op=ALU.mult),
                          reads=Eraw.all() + expB.all(), writes=ET[kb].all())
                if kb == 0 and t == 0:
                    P.add('dve', lambda e: e.tensor_scalar(out=ET[0].t[:, :, :], in0=ET[0].t[:, :, :], scalar1=flag.t[:, 0:1], scalar2=None, op0=ALU.mult), reads=ET[0].all() + flag.all(), writes=ET[0].all())
            bO = [bank(), bank()]
            for hh in range(2):
                P.add('pe', [(lambda kb, hh=hh: lambda e: e.matmul(bO[hh].t[:, :], lhsT=Vh[(pslot, slot)[kb]].t[:, hh, :], rhs=ET[kb].t[:, hh * 4:(hh + 1) * 4, :].rearrange("p r q -> p (r q)"), start=(kb == 0), stop=(kb == 1)))(kb) for kb in range(2)],
                      reads=Vh[pslot].all() + Vh[slot].all() + ET[0].all() + ET[1].all(), writes=bO[hh].all())
                P.add('dve', lambda e, hh=hh: e.tensor_tensor(out=rec.t[:, 0, :].rearrange("p (r q) -> p r q", r=4), in0=bO[hh].t[64:128, :].rearrange("p (r q) -> p r q", r=4), in1=bc(esink.t[64:128, hh * 4:(hh + 1) * 4].unsqueeze(2), [64, 4, 128]), op=ALU.add),
                      reads=bO[hh].all() + esink.all(), writes=rec.all())
                P.add('dve', lambda e, hh=hh: e.reciprocal(out=rec.t[:, 0, :], in_=rec.t[:, 0, :]), reads=rec.all(), writes=rec.all())
                P.add('dve', lambda e, hh=hh: e.tensor_tensor(out=aT.t[:, hh * 4:(hh + 1) * 4, :].rearrange("p r q -> p (r q)"), in0=bO[hh].t[0:64, :], in1=rec.t[:, 0, :], op=ALU.mult),
                      reads=bO[hh].all() + rec.all(), writes=aT.all())
            att_ops = P.end_capture()
            bank_sub[0] = None
            P.ops.extend(Prog.merge(ssd_ops, att_ops))
            b_in_g = t % GB
            out_proj(aT, yT, 128, xt.t[:, :], xt.all(), x1.t[:, b_in_g, :], x1.p(b_in_g))
            rmsnorm_T(x1.t[:, b_in_g, :], x1.p(b_in_g), 128, w2b, hn, h2T.t[:, :, b_in_g * 128:(b_in_g + 1) * 128], h2T.p(b_in_g), "m2")

        for t in range(NBLK):
            P.capture()
            block(xp, t, 'prelast' if t == NBLK - 1 else 'pre')
            bank_mode[0] = None
            P.pipe_push(P.end_capture())
        P.pipe_drain()
        hn_t[0] = hn_sf
        def s_out(bi, full, keys):
            P.dma('sp', 'sout', y_s[:, :], full, reads=keys)
        P.capture()
        bank_mode[0] = 1
        ffn(h2T_s, SB, [(0, SB)], lambda bi, half: x1_s.t[:, :] if half is None else x1_s.t[:, half * 512:(half + 1) * 512], lambda bi: x1_s.all(), s_out, actT_s, "s")
        bank_mode[0] = None
        P.pipe_push(P.end_capture(), split=False)
        hn_t[0] = hn
        P.marks['prefix'] = len(P.ops)
        for t in range(NBLK):
            P.marks['main%d' % t] = len(P.ops)
            P.capture()
            block(xm, t, 'main')
            bank_mode[0] = None
            P.pipe_push(P.end_capture())
            if t % GB == GB - 1:
                g0 = t - (GB - 1)
                P.pipe_drain()

                def m_out(bi, full, keys, g0=g0):
                    P.dma('pool', 'yout%d' % bi, y_m[(g0 + bi) * 128:(g0 + bi + 1) * 128, :], full, reads=keys)
                tail = ffn(h2T, GB * 128, [(bi * 128, 128) for bi in range(GB)],
                           lambda bi, half: x1.t[:, bi, :] if half is None else x1.t[:, bi, half * 512:(half + 1) * 512],
                           lambda bi: x1.p(bi), m_out, actT, "m", defer_tail=True)
                P.pipe_push(tail, split=False)
        P.pipe_drain()
        P.marks['mainend'] = len(P.ops)
        bk = bank()
        P.add('pe', lambda e: e.matmul(bk.t[0:24, 0:128], lhsT=ncv.t[:, :, :].rearrange("p c j -> p (c j)"), rhs=identf.t[:, :], start=True, stop=True), reads=ncv.all() + identf.all(), writes=bk.all())
        P.add('dve', lambda e: e.tensor_copy(out=ncvT.t[0:24, 0:128], in_=bk.t[0:24, 0:128]), reads=bk.all(), writes=ncvT.all())
        for c in range(8):
            P.dma('sp', 'pout', ncv_p[:, c * 128:(c + 1) * 128], ncvT.t[c * 3:(c + 1) * 3, 0:128], reads=ncvT.all())
        bk2 = bank()
        P.add('pe', [(lambda c: lambda e: e.matmul(bk2.t[:, c * 128:(c + 1) * 128], lhsT=stateT.t[:, c * 128:(c + 1) * 128], rhs=identf.t[:, :], start=True, stop=True))(c) for c in range(4)],
              reads=stateT.all() + identf.all(), writes=bk2.all())
        P.add('dve', lambda e: e.tensor_copy(out=nss_sb.t[:, 0:4, :], in_=bk2.t[:, :].rearrange("p (c n) -> p c n", c=4)), reads=bk2.all(), writes=nss_sb.all())
        P.dma('sp', 'pout', nss_p.rearrange("(c p) n -> p c n", p=128), nss_sb.t[:, 0:4, :], reads=nss_sb.all())

        _P[0] = P
        with nc.Block() as blockctx:
            P.finish(st)
    return nc


def _bucket_onehot():
    n = np.arange(128)
    exact = 16
    nf = np.maximum(n, 1).astype(np.float32)
    large = exact + (np.log(nf / exact) / math.log(128 / exact) * (32 - exact)).astype(np.int32)
    bucket = np.where(n < exact, n, np.minimum(large, 31))
    oh = np.zeros((32, 128), np.float32)
    oh[bucket, n] = 1.0
    return oh


_NC = [None]
_P = [None]


def kernel(**inp):
    f = lambda a: np.ascontiguousarray(np.asarray(a, dtype=np.float32))
    x_prompt = f(inp["x_prompt"]); x_sample = f(inp["x_sample"])
    if _NC[0] is None:
        _NC[0] = build()
    nc = _NC[0]
    shared = {
        "rel_bias": f(inp["rel_bias"]), "onehot": _bucket_onehot(),
        "norm1_w": f(inp["norm1_w"]).reshape(1, D), "w_in": f(inp["w_in"])[0], "attn_sinks": f(inp["attn_sinks"]).reshape(1, 8),
        "conv_w": f(inp["conv_w"])[0], "conv_b": f(inp["conv_b"]).reshape(1, 1024), "dt_bias": f(inp["dt_bias"]).reshape(1, 8),
        "A_log": f(inp["A_log"]).reshape(1, 8), "D_skip": f(inp["D_skip"]).reshape(1, 8), "ssm_norm_w": f(inp["ssm_norm_w"]).reshape(1, 512),
        "w_out": f(inp["w_out"])[0], "norm2_w": f(inp["norm2_w"]).reshape(1, D), "w_gate": f(inp["w_gate"])[0], "w_up": f(inp["w_up"])[0],
        "w_down": f(inp["w_down"])[0], "final_norm_w": f(inp["final_norm_w"]).reshape(1, D),
    }
    ck = f(inp["cache_k"])[0].reshape(128, 128, 128); cv = f(inp["cache_v"])[0].reshape(128, 128, 128)
    sconv = f(inp["state_conv"])[0]; sssm = f(inp["state_ssm"])[0].reshape(128 * 8, 64 * 128)
    in_maps = []
    for c in range(NCORES):
        b, half = c // 2, c % 2
        m = dict(shared)
        m["xm"] = x_prompt[b, half * 2048:(half + 1) * 2048]
        m["xp"] = x_prompt[b, 0:2048] if half == 1 else np.zeros((2048, D), np.float32)
        m["flag"] = np.full((128, 1), float(half), np.float32)
        m["xsm"] = x_sample[c * SB:(c + 1) * SB, 0]
        m["ck"] = ck[c * SB:(c + 1) * SB]; m["cv"] = cv[c * SB:(c + 1) * SB]
        m["sconv"] = sconv[c * SB:(c + 1) * SB]; m["sssm"] = sssm[c * SB * 8:(c + 1) * SB * 8]
        in_maps.append({k: np.ascontiguousarray(v) for k, v in m.items()})
    res = run_bass_kernel_spmd(nc, in_maps, core_ids=list(range(NCORES))).results
    yp = np.zeros((4, 4096, D), np.float32)
    for c in range(NCORES):
        yp[c // 2, (c % 2) * 2048:(c % 2 + 1) * 2048] = res[c]["y_m"]
    ys = np.concatenate([res[c]["y_s"] for c in range(NCORES)], 0).reshape(128, 1, D)
    odd = [1, 3, 5, 7]
    nkp = np.stack([res[c]["nk_p"].reshape(128, 2, 64) for c in odd])[None]
    nvp = np.stack([res[c]["nv_p"].reshape(128, 2, 64) for c in odd])[None]
    ncp = np.stack([res[c]["ncv_p"] for c in odd])[None]
    nsp = np.stack([res[c]["nss_p"].reshape(8, 64, 128) for c in odd])[None]
    nks = np.concatenate([res[c]["nk_s"] for c in range(NCORES)], 0).reshape(1, 128, 128, 2, 64)
    nvs = np.concatenate([res[c]["nv_s"] for c in range(NCORES)], 0).reshape(1, 128, 128, 2, 64)
    ncs = np.concatenate([res[c]["ncv_s"] for c in range(NCORES)], 0)[None]
    nsss = np.concatenate([res[c]["nss_s"] for c in range(NCORES)], 0).reshape(1, 128, 8, 64, 128)
    return (yp, ys, nkp.astype(np.float32), nvp.astype(np.float32), ncp.astype(np.float32), nsp.astype(np.float32),
            nks, nvs, ncs, nsss)
```

```python
import math
from contextlib import ExitStack
import numpy as np
import concourse.bass as bass
import concourse.mybir as mybir
from concourse.bass_utils import run_bass_kernel_spmd

F32 = mybir.dt.float32
BF16 = mybir.dt.bfloat16
AF = mybir.ActivationFunctionType
ALU = mybir.AluOpType
AX = mybir.AxisListType

NCORES = 8
D = 1024
NBLK = 16
GB = 4
DFF = 2816
NFC = DFF // 128
DIN = 2312
WIN_W = DIN + 256
SCALE = 0.125
EPS = 1e-6
NEG = -30000.0
SB = 16


class Tl:
    def __init__(self, name, t, nparts=1):
        self.name, self.t, self.nparts = name, t, nparts

    def all(self):
        return [(self.name, i) for i in range(self.nparts)]

    def p(self, *idx):
        return [(self.name, i) for i in idx]


class Prog:
    def __init__(self, nc):
        self.nc = nc
        self.ops = []
        self.marks = {}
        self.eng = {'pe': nc.tensor, 'act': nc.scalar, 'dve': nc.vector, 'pool': nc.gpsimd, 'sp': nc.sync}

    def add(self, eng, fns, reads=(), writes=(), stream=None):
        if callable(fns):
            fns = [fns]
        import sys as _s
        fr = _s._getframe(1)
        if fr.f_code.co_name == 'dma':
            fr = fr.f_back
        self.ops.append(dict(eng=eng, fns=list(fns), reads=list(reads), writes=list(writes), stream=stream, barrier=False, line=fr.f_lineno))

    def dma(self, eng, stream, out, in_, reads=(), writes=(), **kw):
        self.add(eng, [lambda e: e.dma_start(out=out, in_=in_, **kw)], reads, writes, stream=stream)

    def capture(self):
        if not hasattr(self, '_cstack'):
            self._cstack = []
        self._cstack.append(self.ops)
        self.ops = []

    def end_capture(self):
        got = self.ops
        self.ops = self._cstack.pop()
        return got

    @staticmethod
    def merge(a, b):
        out = []
        i = j = 0
        while i < len(a) or j < len(b):
            if j >= len(b) or (i < len(a) and i * len(b) <= j * len(a)):
                out.append(a[i]); i += 1
            else:
                out.append(b[j]); j += 1
        return out

    def pipe_push(self, blk_ops, split=True, frac=0.44):
        cur = getattr(self, '_cur', [])
        if split:
            h = int(len(blk_ops) * frac)
            head, tail = blk_ops[:h], blk_ops[h:]
        else:
            head, tail = [], blk_ops
        i = j = 0
        a, b = len(cur), len(head)
        while i < a or j < b:
            if j >= b or (i < a and i * b <= j * a):
                self.ops.append(cur[i]); i += 1
            else:
                self.ops.append(head[j]); j += 1
        self._cur = tail

    def pipe_drain(self):
        self.ops.extend(getattr(self, '_cur', []))
        self._cur = []

    def barrier(self):
        for e in ('pe', 'act', 'dve', 'pool', 'sp'):
            self.ops.append(dict(eng=e, fns=[], reads=[], writes=[], stream=None, barrier=True))

    def finish(self, stack):
        nc = self.nc
        import os
        mx = os.environ.get("K_MAXOPS")
        if mx:
            mx = self.marks.get(mx, None) if not mx.isdigit() else int(mx)
            self.ops = self.ops[:mx]
        ops = self.ops
        n = len(ops)
        last_w, readers = {}, {}
        deps = [set() for _ in range(n)]
        for i, op in enumerate(ops):
            if op['barrier']:
                seen_e = {}
                for j in range(i - 1, -1, -1):
                    o = ops[j]
                    if o['barrier']:
                        continue
                    key = o['stream'] if o['stream'] else o['eng']
                    if key == 'wcast':
                        continue
                    if key not in seen_e:
                        seen_e[key] = j
                deps[i] = set(seen_e.values())
                continue
            dset = deps[i]
            for k in op['reads']:
                if k in last_w:
                    dset.add(last_w[k])
                if k[0].startswith('ps'):
                    for r in readers.get(k, ()):
                        if ops[r]['eng'] != op['eng']:
                            dset.add(r)
            for k in op['writes']:
                if k in last_w:
                    dset.add(last_w[k])
                for r in readers.get(k, ()):
                    dset.add(r)
            dset.discard(i)
            for k in op['reads']:
                readers.setdefault(k, []).append(i)
            for k in op['writes']:
                last_w[k] = i
                readers[k] = []
            if op['eng'] == 'pe' and op['stream'] is None:
                deps[i] = {j for j in dset if not (ops[j]['eng'] == 'pe' and ops[j]['stream'] is None)}
        needed = [False] * n
        for i in range(n):
            for j in deps[i]:
                needed[j] = True
        sems = {}

        def getsem(name):
            if name not in sems:
                sems[name] = stack.enter_context(nc.semaphore("s_" + name))
            return sems[name]
        counts = {}
        sig = [None] * n
        for i, op in enumerate(ops):
            if op['barrier'] or not op['fns']:
                continue
            if op['stream']:
                key = 'd_' + op['stream']
                counts[key] = counts.get(key, 0) + 16 * len(op['fns'])
                sig[i] = (key, counts[key])
            elif needed[i]:
                key = 'e_' + op['eng']
                counts[key] = counts.get(key, 0) + 1
                sig[i] = (key, counts[key])
        seen = {e: {} for e in self.eng}
        issued = {}
        for i, op in enumerate(ops):
            e = self.eng[op['eng']]
            waits = {}
            for j in deps[i]:
                if sig[j] is None:
                    continue
                k, v = sig[j]
                if k.startswith('d_'):
                    v = issued[k]
                if v > waits.get(k, 0):
                    waits[k] = v
            wl = [(k, v) for k, v in waits.items() if seen[op['eng']].get(k, 0) < v]
            for k, v in wl:
                seen[op['eng']][k] = v
            if op['barrier'] or not op['fns']:
                for k, v in wl:
                    e.wait_ge(getsem(k), v)
                continue
            attach = None
            if wl and op['stream'] is None:
                attach = wl[0]
                wl = wl[1:]
            for k, v in wl:
                e.wait_ge(getsem(k), v)
            ins = None
            for fi, fn in enumerate(op['fns']):
                ins = fn(e)
                if fi == 0 and attach is not None:
                    ins._wait_ge(getsem(attach[0]), attach[1])
                if op['stream']:
                    ins.then_inc(getsem(sig[i][0]), 16)
            if sig[i] is not None and not op['stream']:
                ins.then_inc(getsem(sig[i][0]), 1)
            if op['stream']:
                issued[sig[i][0]] = sig[i][1]
        sp = nc.sync
        for k, v in counts.items():
            if k.startswith('d_'):
                sp.wait_ge(getsem(k), v)


def bc(ap, shape):
    return ap.broadcast_to(shape)


def build():
    nc = bass.Bass("TRN2", target_bir_lowering=False)
    P = Prog(nc)

    def din(name, shape):
        return nc.dram_tensor(name, shape, F32, kind="ExternalInput").ap()

    def dout(name, shape):
        return nc.dram_tensor(name, shape, F32, kind="ExternalOutput").ap()

    xm = din("xm", [NBLK * 128, D]); xp = din("xp", [NBLK * 128, D]); flag_d = din("flag", [128, 1])
    xsm = din("xsm", [SB, D]); ck = din("ck", [SB, 128, 128]); cv = din("cv", [SB, 128, 128])
    sconv = din("sconv", [SB, 3, 1024]); sssm = din("sssm", [SB * 8, 64 * 128])
    rel_bias = din("rel_bias", [32, 8]); onehot_d = din("onehot", [32, 128])
    norm1_w = din("norm1_w", [1, D]); w_in = din("w_in", [D, DIN]); sinks = din("attn_sinks", [1, 8])
    conv_w = din("conv_w", [4, 1024]); conv_b = din("conv_b", [1, 1024]); dt_bias = din("dt_bias", [1, 8])
    A_log = din("A_log", [1, 8]); D_skip = din("D_skip", [1, 8]); ssm_norm_w = din("ssm_norm_w", [1, 512])
    w_out = din("w_out", [D, D]); norm2_w = din("norm2_w", [1, D]); w_gate = din("w_gate", [D, DFF])
    w_up = din("w_up", [D, DFF]); w_down = din("w_down", [DFF, D]); fnorm_w = din("final_norm_w", [1, D])

    y_m = dout("y_m", [NBLK * 128, D]); y_s = dout("y_s", [SB, D])
    nk_p = dout("nk_p", [128, 128]); nv_p = dout("nv_p", [128, 128]); ncv_p = dout("ncv_p", [3, 1024])
    nss_p = dout("nss_p", [512, 128])
    nk_s = dout("nk_s", [SB, 128, 128]); nv_s = dout("nv_s", [SB, 128, 128])
    ncv_s = dout("ncv_s", [SB, 3, 1024]); nss_s = dout("nss_s", [SB * 8, 64 * 128])
    scrS = nc.dram_tensor("scrS", [383, 8], F32).ap()
    scr1 = nc.dram_tensor("scr1", [SB, 512], F32).ap()
    scr2 = nc.dram_tensor("scr2", [SB, 8], F32).ap()
    scr3 = nc.dram_tensor("scr3", [SB, 512], F32).ap()
    scr4 = nc.dram_tensor("scr4", [SB * 8, 64], F32).ap()
    scrKV = nc.dram_tensor("scrKV", [SB, 256], BF16).ap()
    wgu_bf = nc.dram_tensor("wgu_bf", [NFC, 128, 2, 8, 128], BF16).ap()
    wd_bf = nc.dram_tensor("wd_bf", [DFF, D], BF16).ap()

    st = ExitStack()
    with st:
        def sb(name, shape, dt=F32, nparts=1, stack=st):
            return Tl(name, stack.enter_context(nc.sbuf_tensor(name, shape, dt)), nparts)

        ps = [Tl("ps%d" % i, st.enter_context(nc.psum_tensor("ps%d" % i, [128, 512], F32))) for i in range(8)]
        psrr = [0, 0, 0]
        psub = {}
        bank_mode = [None]
        bank_sub = [None]

        def bank():
            m = bank_mode[0]
            if m is None:
                b = ps[psrr[2] % 8]
                psrr[2] += 1
            elif bank_sub[0] is None:
                b = ps[m * 4 + psrr[m] % 4]
                psrr[m] += 1
            else:
                k = (m, bank_sub[0])
                psub[k] = psub.get(k, 0) + 1
                b = ps[m * 4 + bank_sub[0] * 2 + psub[k] % 2]
            return b

        identf = sb("identf", [128, 128]); identb = sb("identb", [128, 128], BF16)
        tri = sb("tri", [128, 128])
        striu = sb("striu", [128, 128])
        onesf = sb("onesf", [128, 128]); onesb = sb("onesb", [128, 64], BF16)
        negm = sb("negm", [128, 4, 128], BF16)
        expB = sb("expB", [128, 2, 128, 8])
        expBs = sb("expBs", [128, 8])
        w1T = sb("w1T", [128, 8]); w2b = sb("w2b", [128, D]); wfb = sb("wfb", [128, D]); wsT = sb("wsT", [128, 4])
        convdiag = sb("convdiag", [128, 8, 4, 128], BF16)
        cwT = sb("cwT", [128, 4, 8]); cbT = sb("cbT", [128, 8])
        A_b = sb("A_b", [128, 8]); dtb_b = sb("dtb_b", [128, 8]); D_b = sb("D_b", [128, 8]); esink = sb("esink", [128, 8])
        flag = sb("flag_sb", [128, 1])
        zero = sb("zero_sb", [128, 8])
        w_in_sb = sb("w_in_sb", [128, 8, WIN_W], BF16, nparts=8)
        w_out_a = sb("w_out_a", [64, 8, D], BF16); w_out_s = sb("w_out_s", [128, 4, D], BF16)
        NSL = 3
        wgu_sl = [sb("wgu%d" % i, [128, 2, 8, 128], BF16) for i in range(NSL)]
        wd_sl = [sb("wd%d" % i, [128, D], BF16) for i in range(NSL)]
        stateT = sb("stateT", [128, 512]); stateTb = sb("stateTb", [128, 512], BF16)
        small = {}

        def sm(name, w=8):
            if name not in small:
                small[name] = sb("sm_" + name, [128, w])
            return small[name]

        def cst(tl, fn):
            for f_ in (fn if isinstance(fn, (list, tuple)) else [fn]):
                P.add('pool', f_, reads=tl.all(), writes=tl.all())
        cst(identf, [lambda e: e.memset(identf.t[:], 1.0),
                     lambda e: e.affine_select(out=identf.t[:], in_=identf.t[:], pattern=[[-1, 128]], compare_op=ALU.is_equal, fill=0.0, base=0, channel_multiplier=1)])
        cst(tri, [lambda e: e.memset(tri.t[:], 1.0),
                  lambda e: e.affine_select(out=tri.t[:], in_=tri.t[:], pattern=[[1, 128]], compare_op=ALU.is_ge, fill=0.0, base=0, channel_multiplier=-1)])
        cst(striu, [lambda e: e.memset(striu.t[:], 1.0),
                    lambda e: e.affine_select(out=striu.t[:], in_=striu.t[:], pattern=[[-1, 128]], compare_op=ALU.is_gt, fill=0.0, base=0, channel_multiplier=1)])
        cst(onesf, lambda e: e.memset(onesf.t[:], 1.0))
        cst(onesb, lambda e: e.memset(onesb.t[:], 1.0))
        cst(zero, lambda e: e.memset(zero.t[:], 0.0))
        P.add('dve', lambda e: e.tensor_copy(out=identb.t[:], in_=identf.t[:]), reads=identf.all(), writes=identb.all())

        def bload(tl, src, width):
            P.dma('sp', 'cst', tl.t[:, 0:width], src[0:1, :].partition_broadcast(128), writes=tl.all())
        bload(w2b, norm2_w, D); bload(wfb, fnorm_w, D)
        P.dma('sp', 'cst', wsT.t[:], ssm_norm_w.rearrange("o (c p) -> p (o c)", p=128), writes=wsT.all(), allow_slow_non_contiguous=True)
        bload(A_b, A_log, 8); bload(dtb_b, dt_bias, 8); bload(D_b, D_skip, 8); bload(esink, sinks, 8)
        P.dma('sp', 'cst', flag.t[:], flag_d[:, :], writes=flag.all())
        P.dma('sp', 'cst', w1T.t[:], norm1_w.rearrange("o (k p) -> p (o k)", p=128), writes=w1T.all(), allow_slow_non_contiguous=True)
        for j in range(4):
            P.dma('sp', 'cst', cwT.t[:, j, :], conv_w[j:j + 1, :].rearrange("o (c p) -> p (o c)", p=128), writes=cwT.all(), allow_slow_non_contiguous=True)
        P.dma('sp', 'cst', cbT.t[:], conv_b.rearrange("o (c p) -> p (o c)", p=128), writes=cbT.all(), allow_slow_non_contiguous=True)
        P.add('act', lambda e: e.activation(out=A_b.t[:], in_=A_b.t[:], func=AF.Exp), reads=A_b.all(), writes=A_b.all())
        P.add('dve', lambda e: e.tensor_scalar(out=A_b.t[:], in0=A_b.t[:], scalar1=-1.0, scalar2=None, op0=ALU.mult), reads=A_b.all(), writes=A_b.all())
        P.add('act', lambda e: e.activation(out=esink.t[:], in_=esink.t[:], func=AF.Exp), reads=esink.all(), writes=esink.all())
        for c in range(8):
            for j in range(4):
                P.add('dve', (lambda c, j: lambda e: e.tensor_scalar(out=convdiag.t[:, c, j, :], in0=identf.t[:], scalar1=cwT.t[:, j, c:c + 1], scalar2=None, op0=ALU.mult))(c, j),
                      reads=identf.all() + cwT.all(), writes=convdiag.all())
        P.dma('pool', 'win', w_in_sb.t[:, :, 0:DIN], w_in.rearrange("(k p) n -> p k n", p=128), writes=w_in_sb.all())
        for k in range(8):
            P.add('dve', lambda e, k=k: e.tensor_scalar(out=w_in_sb.t[:, k, 0:DIN], in0=w_in_sb.t[:, k, 0:DIN], scalar1=w1T.t[:, k:k + 1], scalar2=None, op0=ALU.mult), reads=w_in_sb.all() + w1T.all(), writes=w_in_sb.all())
        for h in range(2):
            for dup in range(2):
                o0 = DIN + h * 128 + dup * 64
                P.add('dve', lambda e, o0=o0, h=h: e.tensor_copy(out=w_in_sb.t[:, :, o0:o0 + 64], in_=w_in_sb.t[:, :, 512 + h * 64:512 + (h + 1) * 64]), reads=w_in_sb.all(), writes=w_in_sb.all())
        P.dma('pool', 'wout', w_out_a.t[:, :, :], w_out[0:512, :].rearrange("(h d) n -> d h n", d=64), writes=w_out_a.all())
        P.dma('pool', 'wout', w_out_s.t[:, :, :], w_out[512:1024, :].rearrange("(c p) n -> p c n", p=128), writes=w_out_s.all())
        for c in range(4):
            P.add('dve', lambda e, c=c: e.tensor_scalar(out=w_out_s.t[:, c, :], in0=w_out_s.t[:, c, :], scalar1=wsT.t[:, c:c + 1], scalar2=None, op0=ALU.mult), reads=w_out_s.all() + wsT.all(), writes=w_out_s.all())
        def rmsnorm_T(xin_ap, xin_keys, nrow, wb, hn, hTdst, hT_keys, tag, cp='act'):
            ss = sm("ss_" + tag, 4)
            junk = hn
            P.add('act', lambda e: e.activation(out=junk.t[0:nrow, :], in_=xin_ap, func=AF.Square, accum_out=ss.t[0:nrow, 0:1]),
                  reads=xin_keys, writes=junk.all() + ss.all())
            P.add('act', lambda e: e.activation(out=ss.t[0:nrow, 1:2], in_=ss.t[0:nrow, 0:1], func=AF.Ln, bias=EPS, scale=1.0 / D), reads=ss.all(), writes=ss.all())
            P.add('act', lambda e: e.activation(out=ss.t[0:nrow, 2:3], in_=ss.t[0:nrow, 1:2], func=AF.Exp, scale=-0.5), reads=ss.all(), writes=ss.all())
            if wb is None:
                P.add('dve', lambda e: e.tensor_scalar(out=hn.t[0:nrow, :], in0=xin_ap, scalar1=ss.t[0:nrow, 2:3], scalar2=None, op0=ALU.mult),
                      reads=xin_keys + ss.all(), writes=hn.all())
            else:
                P.add('dve', lambda e: e.scalar_tensor_tensor(out=hn.t[0:nrow, :], in0=xin_ap, scalar=ss.t[0:nrow, 2:3], in1=wb.t[0:nrow, :], op0=ALU.mult, op1=ALU.mult),
                      reads=xin_keys + ss.all() + wb.all(), writes=hn.all())
            bk = bank()
            pv = bk.t[:].bitcast(BF16)
            P.add('pe', [(lambda k: lambda e: e.transpose(out=pv[:, k * 128:k * 128 + nrow], in_=hn.t[0:nrow, k * 128:(k + 1) * 128], identity=identb.t[0:nrow, 0:nrow]))(k) for k in range(8)],
                  reads=hn.all() + identb.all(), writes=bk.all())
            if cp == 'act':
                P.add('act', lambda e: e.activation(out=hTdst, in_=pv.rearrange("p (k t) -> p k t", k=8)[:, :, 0:nrow], func=AF.Copy), reads=bk.all(), writes=hT_keys)
            else:
                P.add('dve', lambda e: e.tensor_copy(out=hTdst, in_=pv.rearrange("p (k t) -> p k t", k=8)[:, :, 0:nrow]), reads=bk.all(), writes=hT_keys)
            return ss

        def softplus_dt(ps_ap, ps_keys, nrow, dt):
            P.add('dve', lambda e: e.tensor_tensor(out=dt.t[0:nrow, 8:16], in0=ps_ap, in1=dtb_b.t[0:nrow, :], op=ALU.add), reads=ps_keys + dtb_b.all(), writes=dt.all())
            P.add('act', lambda e: e.activation(out=dt.t[0:nrow, 8:16], in_=dt.t[0:nrow, 8:16], func=AF.Exp), reads=dt.all(), writes=dt.all())
            P.add('act', lambda e: e.activation(out=dt.t[0:nrow, 0:8], in_=dt.t[0:nrow, 8:16], func=AF.Ln, bias=1.0, scale=1.0), reads=dt.all(), writes=dt.all())

        def ffn(h2T, ntok, blocks, x1_ap_fn, x1_keys_fn, out_dma_fn, actT, tag, defer_tail=False):
            sgs = ffn_sg[tag]
            HC = NFC // 2
            nb = len(blocks)
            for fh in range(2):
                for ci in range(HC):
                    c = fh * HC + ci
                    wgu = wgu_sl[c % NSL]
                    P.dma('sp', 'wgu%d' % (c % NSL), wgu.t[:, :, :, :], wgu_bf[c], reads=WGU_KEYS, writes=wgu.all())
                    bg, bu = bank(), bank()
                    P.add('pe', [(lambda k, bg=bg, wgu=wgu: lambda e: e.matmul(bg.t[:, 0:ntok], lhsT=wgu.t[:, 0, k, :], rhs=h2T.t[:, k, 0:ntok], start=(k == 0), stop=(k == 7)))(k) for k in range(8)],
                          reads=wgu.all() + h2T.all(), writes=bg.all())
                    P.add('pe', [(lambda k, bu=bu, wgu=wgu: lambda e: e.matmul(bu.t[:, 0:ntok], lhsT=wgu.t[:, 1, k, :], rhs=h2T.t[:, k, 0:ntok], start=(k == 0), stop=(k == 7)))(k) for k in range(8)],
                          reads=wgu.all() + h2T.all(), writes=bu.all())
                    sg = sgs[c % 2]
                    P.add('act', lambda e, bg=bg, sg=sg: e.activation(out=sg.t[:, 0:ntok], in_=bg.t[:, 0:ntok], func=AF.Silu), reads=bg.all(), writes=sg.all())
                    P.add('dve', lambda e, bu=bu, sg=sg, ci=ci: e.tensor_tensor(out=actT.t[:, ci, 0:ntok], in0=sg.t[:, 0:ntok], in1=bu.t[:, 0:ntok], op=ALU.mult),
                          reads=bu.all() + sg.all(), writes=actT.p(ci))
                accs = [[bank(), bank()] for _ in range(nb)]
                for ci in range(HC):
                    c = fh * HC + ci
                    wd = wd_sl[c % NSL]
                    P.dma('sp', 'wd%d' % (c % NSL), wd.t[:, :], wd_bf[c * 128:(c + 1) * 128, :], reads=[('wd_bf', 0)], writes=wd.all())
                    fns = []
                    wr = []
                    for bi, (c0, nrow) in enumerate(blocks):
                        for half in range(2):
                            fns.append((lambda bi, half, c0, nrow, ci=ci, wd=wd, accs=accs: lambda e: e.matmul(accs[bi][half].t[0:nrow, :], lhsT=actT.t[:, ci, c0:c0 + nrow], rhs=wd.t[:, half * 512:(half + 1) * 512], start=(ci == 0), stop=(ci == HC - 1)))(bi, half, c0, nrow))
                            wr += accs[bi][half].all()
                    P.add('pe', fns, reads=wd.all() + actT.p(ci), writes=wr)
                if fh == 0:
                    for bi, (c0, nrow) in enumerate(blocks):
                        x1keys = x1_keys_fn(bi)
                        for half in range(2):
                            xa = x1_ap_fn(bi, half)
                            P.add('dve', lambda e, xa=xa, a=accs[bi][half], nrow=nrow: e.tensor_tensor(out=xa, in0=a.t[0:nrow, :], in1=xa, op=ALU.add),
                                  reads=accs[bi][half].all() + x1keys, writes=x1keys)
            for bi, (c0, nrow) in enumerate(blocks):
                x1keys = x1_keys_fn(bi)
                for half in range(2):
                    xa = x1_ap_fn(bi, half)
                    P.add('dve', lambda e, xa=xa, a=accs[bi][half], nrow=nrow: e.tensor_tensor(out=xa, in0=a.t[0:nrow, :], in1=xa, op=ALU.add),
                          reads=accs[bi][half].all() + x1keys, writes=x1keys)
            if defer_tail:
                P.capture()
            for bi, (c0, nrow) in enumerate(blocks):
                x1keys = x1_keys_fn(bi)
                ss = sm("ssf_" + tag, 4)
                full = x1_ap_fn(bi, None)
                if tag == "m":
                    jk = sgs[0]
                    jk_ap = jk.t[0:nrow, :].bitcast(BF16)
                else:
                    jk = hn_t[0]
                    jk_ap = jk.t[0:nrow, :]
                P.add('act', lambda e, full=full, nrow=nrow, jk_ap=jk_ap: e.activation(out=jk_ap, in_=full, func=AF.Square, accum_out=ss.t[0:nrow, 0:1]),
                      reads=x1keys, writes=jk.all() + ss.all())
                P.add('act', lambda e, nrow=nrow: e.activation(out=ss.t[0:nrow, 1:2], in_=ss.t[0:nrow, 0:1], func=AF.Ln, bias=EPS, scale=1.0 / D), reads=ss.all(), writes=ss.all())
                P.add('act', lambda e, nrow=nrow: e.activation(out=ss.t[0:nrow, 2:3], in_=ss.t[0:nrow, 1:2], func=AF.Exp, scale=-0.5), reads=ss.all(), writes=ss.all())
                P.add('dve', lambda e, full=full, nrow=nrow: e.scalar_tensor_tensor(out=full, in0=full, scalar=ss.t[0:nrow, 2:3], in1=wfb.t[0:nrow, :], op0=ALU.mult, op1=ALU.mult),
                      reads=x1keys + ss.all() + wfb.all(), writes=x1keys)
                out_dma_fn(bi, full, x1keys)
            if defer_tail:
                return P.end_capture()
            return None

        ffn_sg = {}
        hn_t = [None]
        WGU_KEYS = [('wgu_bf', i) for i in range(16)]
        xs_in = sb("xs_in", [SB, D]); h2T_s = sb("h2T_s", [128, 8, SB], BF16); actT_s = sb("actT_s", [128, NFC // 2, SB], BF16, nparts=NFC // 2)
        ffn_sg["s"] = [sb("sg_s%d" % i, [128, SB]) for i in range(2)]
        hn_sf = sb("hn_sf", [SB, D], BF16)
        for nm_, w_ in [("ss_s1", 4), ("ss_s2", 4), ("ssf_s", 4), ("ssg_s", 8), ("ss_m1", 4), ("ss_m2", 4), ("ssf_m", 4), ("ssg_m", 8), ("scal", 64)]:
            sm(nm_, w_)

        import os as _os
        for _i in range(int(_os.environ.get('K_DUMMY', '0'))):
            P.add('dve', lambda e: e.memset(zero.t[:], 0.0), writes=zero.all())
        P.marks['setup'] = len(P.ops)
        sst = ExitStack()
        with sst:
            def ssb(name, shape, dt=F32, nparts=1):
                return sb(name, shape, dt, nparts, stack=sst)
            h0 = [ssb("h0_%d" % i, [128, 16, 128]) for i in range(2)]; otmp = ssb("otmp", [128, 16, 128], nparts=2); kvst = otmp
            Kwb = ssb("Kwb", [128, SB, 128], BF16); Vwb = ssb("Vwb", [128, SB, 128], BF16)
            SelH = Tl("h0_1", h0[1].t, 1); SelHT = ssb("SelHT", [128, 8, SB])
            P.dma('sp', 'xs', xs_in.t[:, :], xsm[:, :], writes=xs_in.all())
            for hf, q in ((0, 'sp'), (1, 'act')):
                P.dma(q, 'kvst%d' % hf, kvst.t[0:127, hf * 8:(hf + 1) * 8, :], ck[hf * 8:(hf + 1) * 8, 1:128, :].rearrange("b p f -> p b f"), writes=kvst.p(hf))
            for hf, q in ((0, 'sp'), (1, 'act')):
                P.dma(q, 'kvsv%d' % hf, h0[0].t[0:127, hf * 8:(hf + 1) * 8, :], cv[hf * 8:(hf + 1) * 8, 1:128, :].rearrange("b p f -> p b f"), writes=h0[0].all())
            SelQ = ssb("SelQ", [SB, 4, 128]); SelQT = ssb("SelQT", [128, 4, SB])
            cbuf = ssb("cbuf", [128, 3, 256]); cw_b = ssb("cw_b", [128, 4, 256]); cb_b = ssb("cb_b", [128, 256])
            accq = ssb("accq", [128, 256]); tmpq = ssb("tmpq", [128, 256])
            P.add('dve', [lambda e: e.memset(cbuf.t[:, :, :], 0.0), lambda e: e.memset(cw_b.t[:, :, :], 0.0), lambda e: e.memset(cb_b.t[:, :], 0.0)],
                  writes=cbuf.all() + cw_b.all() + cb_b.all())
            for q4 in range(4):
                p0, cs0 = q4 * 32, q4 * 256
                P.dma('sp', 'sconv', cbuf.t[p0:p0 + SB, :, :], sconv[:, :, cs0:cs0 + 256], writes=cbuf.all())
                P.dma('sp', 'sconv', cw_b.t[p0:p0 + SB, :, :], bass.AP(conv_w.tensor, cs0, [[0, SB], [1024, 4], [1, 256]]), writes=cw_b.all())
                P.dma('sp', 'sconv', cb_b.t[p0:p0 + SB, :], conv_b[0:1, cs0:cs0 + 256].partition_broadcast(SB), writes=cb_b.all())
            antiI = ssb("antiI", [128, 128])
            cst(antiI, [lambda e: e.memset(antiI.t[:], 1.0),
                        lambda e: e.affine_select(out=antiI.t[:], in_=antiI.t[:], pattern=[[1, 128]], compare_op=ALU.is_equal, fill=0.0, base=-127, channel_multiplier=1)])
            cst(SelH, [lambda e: e.memset(SelH.t[0:SB, 0:8, :], 1.0),
                       lambda e: e.affine_select(out=SelH.t[0:SB, 0:8, :], in_=SelH.t[0:SB, 0:8, :], pattern=[[-1, 8], [1, 128]], compare_op=ALU.is_equal, fill=0.0, base=0, channel_multiplier=-8)])
            cst(SelHT, [lambda e: e.memset(SelHT.t[:, :, :], 1.0),
                       lambda e: e.affine_select(out=SelHT.t[:, :, :], in_=SelHT.t[:, :, :], pattern=[[-1, 8], [-8, SB]], compare_op=ALU.is_equal, fill=0.0, base=0, channel_multiplier=1)])
            cst(SelQ, [lambda e: e.memset(SelQ.t[:, :, :], 1.0),
                       lambda e: e.affine_select(out=SelQ.t[:, :, :], in_=SelQ.t[:, :, :], pattern=[[-32, 4], [1, 128]], compare_op=ALU.is_equal, fill=0.0, base=0, channel_multiplier=-1)])
            cst(SelQT, [lambda e: e.memset(SelQT.t[:, :, :], 1.0),
                       lambda e: e.affine_select(out=SelQT.t[:, :, :], in_=SelQT.t[:, :, :], pattern=[[-32, 4], [-1, SB]], compare_op=ALU.is_equal, fill=0.0, base=0, channel_multiplier=1)])
            sb_saved = sb
            sb = lambda name, shape, dt=F32, nparts=1, stack=sst: sb_saved(name, shape, dt, nparts, stack=sst)
            negf = sb("negf", [128, 128])
            cst(negf, [lambda e: e.memset(negf.t[:], 0.0),
                       lambda e: e.affine_select(out=negf.t[:], in_=negf.t[:], pattern=[[1, 128]], compare_op=ALU.is_ge, fill=NEG, base=0, channel_multiplier=-1)])
            P.add('dve', lambda e: e.tensor_copy(out=negm.t[:], in_=bc(negf.t[:, :].unsqueeze(1), [128, 4, 128])), reads=negf.all(), writes=negm.all())
            oh_sb = sb("oh_sb", [32, 128]); rb_sb = sb("rb_sb", [32, 8]); eb = sb("eb", [128, 8])
            P.dma('sp', 'cst', oh_sb.t[:], onehot_d[:, :], writes=oh_sb.all())
            P.dma('sp', 'cst', rb_sb.t[:], rel_bias[:, :], writes=rb_sb.all())
            b0 = bank()
            P.add('pe', lambda e: e.matmul(b0.t[:, 0:8], lhsT=oh_sb.t[:, :], rhs=rb_sb.t[:, :], start=True, stop=True), reads=oh_sb.all() + rb_sb.all(), writes=b0.all())
            P.add('act', lambda e: e.activation(out=eb.t[:], in_=b0.t[:, 0:8], func=AF.Exp), reads=b0.all(), writes=eb.all())
            b1 = bank()
            P.add('pe', lambda e: e.matmul(b1.t[:, 0:8], lhsT=antiI.t[:, :], rhs=eb.t[:, :], start=True, stop=True), reads=antiI.all() + eb.all(), writes=b1.all())
            P.add('dve', lambda e: e.tensor_copy(out=expBs.t[:], in_=b1.t[:, 0:8]), reads=b1.all(), writes=expBs.all())
            P.dma('sp', 'scrS', scrS[0:127, :], zero.t[0:127, :], reads=zero.all(), writes=[('scrS', 0)])
            P.dma('sp', 'scrS', scrS[255:383, :], zero.t[:, :], reads=zero.all(), writes=[('scrS', 0)])
            P.dma('sp', 'scrS', scrS[127:255, :], eb.t[:, :], reads=eb.all(), writes=[('scrS', 0)])
            sb = sb_saved
            hn_s = ssb("hn_s", [SB, D], BF16)
            hn_t[0] = hn_s
            hT_s = ssb("hT_s", [128, 8, SB], BF16)
            proj = ssb("proj_s", [SB, DIN]); projb = ssb("projb_s", [SB, 768], BF16)
            KT_s = ssb("KT_s", [64, SB * 2, 128], BF16); qT2 = ssb("qT2", [64, 8, SB], BF16)
            Es = ssb("Es", [128, 128]); Esb = ssb("Esb", [128, SB, 8], BF16)
            rec_s = ssb("rec_s", [64, 128]); aT_s = ssb("aT_s", [64, 8, SB], BF16)
            xc_s = ssb("xc_s", [SB, 1024]); tmp_s = ssb("tmp_s", [SB, 512])
            dt_s = ssb("dt_s", [128, 16]); dec_s = ssb("dec_s", [SB, 8]); xdt_s = ssb("xdt_s", [SB, 512])
            xdt_bh = ssb("xdt_bh", [128, 64]); dec_bh = ssb("dec_bh", [128, 1]); B_bh = ssb("B_bh", [128, 128]); C_bh = ssb("C_bh", [128, 128])
            y_bh = ssb("y_bh", [128, 64]); y_tok = ssb("y_tok", [SB, 512]); sz_s = ssb("sz_s", [SB, 512])
            yn_s = ssb("yn_s", [SB, 512], BF16); yT_s = ssb("yT_s", [128, 4, SB], BF16)
            x1_s = xs_in

            rmsnorm_T(xs_in.t[:, :], xs_in.all(), SB, None, hn_s, hT_s.t[:, :, :], hT_s.all(), "s1")
            colr = [(0, 512), (512, 1024), (1024, 1536), (1536, 2048), (2048, DIN)]
            for (c0, c1) in colr:
                bk = bank()
                P.add('pe', [(lambda k, bk=bk, c0=c0, c1=c1: lambda e: e.matmul(bk.t[0:SB, 0:c1 - c0], lhsT=hT_s.t[:, k, :], rhs=w_in_sb.t[:, k, c0:c1], start=(k == 0), stop=(k == 7)))(k) for k in range(8)],
                      reads=hT_s.all() + w_in_sb.all(), writes=bk.all())
                P.add('act', lambda e, bk=bk, c0=c0, c1=c1: e.activation(out=proj.t[:, c0:c1], in_=bk.t[0:SB, 0:c1 - c0], func=AF.Copy), reads=bk.all(), writes=proj.all())
            P.add('dve', lambda e: e.tensor_copy(out=projb.t[:, :], in_=proj.t[:, 0:768]), reads=proj.all(), writes=projb.all())
            P.dma('sp', 'skv0', scrKV[:, :], projb.t[:, 512:768], reads=projb.all(), writes=[('scrKV', 0)])
            P.dma('sp', 'skv', Kwb.t[127:128, :, :], scrKV[:, 0:128].rearrange("(o b) f -> o b f", o=1), reads=[('scrKV', 0)], writes=Kwb.all())
            P.dma('sp', 'skv', Vwb.t[127:128, :, :], scrKV[:, 128:256].rearrange("(o b) f -> o b f", o=1), reads=[('scrKV', 0)], writes=Vwb.all())
            P.dma('sp', 'sout', nk_s[:, 0:127, :], ck[:, 1:128, :])
            P.dma('sp', 'sout', nv_s[:, 0:127, :], cv[:, 1:128, :])
            P.dma('sp', 'sout', nk_s[:, 127, :], proj.t[:, 512:640], reads=proj.all())
            P.dma('sp', 'sout', nv_s[:, 127, :], proj.t[:, 640:768], reads=proj.all())
            P.dma('sp', 'sout', ncv_s[:, 0:2, :], sconv[:, 1:3, :])
            P.dma('sp', 'sout', ncv_s[:, 2, :], proj.t[:, 1280:2304], reads=proj.all())
            for hf in range(2):
                P.add('pool', lambda e, hf=hf: e.tensor_copy(out=Kwb.t[0:127, hf * 8:(hf + 1) * 8, :], in_=kvst.t[0:127, hf * 8:(hf + 1) * 8, :]), reads=kvst.p(hf), writes=Kwb.all())
            for hf in range(2):
                P.add('pool', lambda e, hf=hf: e.tensor_copy(out=Vwb.t[0:127, hf * 8:(hf + 1) * 8, :], in_=h0[0].t[0:127, hf * 8:(hf + 1) * 8, :]), reads=h0[0].all(), writes=Vwb.all())
            for gu, wsrc in enumerate((w_gate, w_up)):
                for k in range(8):
                    P.dma('pool', 'wcast', wgu_bf[:, :, gu, k, :].rearrange("c p j -> p c j"), wsrc[k * 128:(k + 1) * 128, :].rearrange("p (c j) -> p c j", j=128), writes=[('wgu_bf', gu * 8 + k)])
            P.dma('pool', 'wcast', wd_bf[:, :], w_down[:, :], writes=[('wd_bf', 0)])


            Tp = h0[0]
            P.dma('sp', 'expB', Tp.t[:, :, :].rearrange("p a n -> p (a n)"), bass.AP(scrS.tensor, 0, [[8, 128], [1, 2048]]), reads=[('scrS', 0)], writes=Tp.all())
            for q4 in range(4):
                bkx = bank()
                P.add('pe', lambda e, bkx=bkx, q4=q4: e.matmul(bkx.t[:, :], lhsT=antiI.t[:, :], rhs=Tp.t[:, :, :].rearrange("p a n -> p (a n)")[:, q4 * 512:(q4 + 1) * 512], start=True, stop=True), reads=antiI.all() + Tp.all(), writes=bkx.all())
                P.add('act', lambda e, bkx=bkx, q4=q4: e.activation(out=expB.t[:, :, :, :].rearrange("p a q h -> p (a q h)")[:, q4 * 512:(q4 + 1) * 512], in_=bkx.t[:, :], func=AF.Copy), reads=bkx.all(), writes=expB.all())
            bk = bank(); pv = bk.t[:].bitcast(BF16)
            P.add('pe', [(lambda hq, pv=pv: lambda e: e.transpose(out=pv[0:64, hq * SB:(hq + 1) * SB], in_=projb.t[:, hq * 64:(hq + 1) * 64], identity=identb.t[0:SB, 0:SB]))(hq) for hq in range(8)],
                  reads=projb.all() + identb.all(), writes=bk.all())
            P.add('act', lambda e, pv=pv: e.activation(out=qT2.t[:, :, :], in_=pv[0:64, 0:8 * SB].rearrange("p (h b) -> p h b", h=8), func=AF.Copy), reads=bk.all(), writes=qT2.all())
            for g4 in range(4):
                bk = bank(); pv = bk.t[:].bitcast(BF16)
                P.add('pe', [(lambda i, pv=pv, g4=g4: lambda e: e.transpose(out=pv[0:64, i * 128:(i + 1) * 128], in_=Kwb.t[:, (g4 * 8 + i) // 2, ((g4 * 8 + i) % 2) * 64:((g4 * 8 + i) % 2) * 64 + 64], identity=identb.t[:, :]))(i) for i in range(8)],
                      reads=Kwb.all() + identb.all(), writes=bk.all())
                P.add('dve', lambda e, pv=pv, g4=g4: e.tensor_copy(out=KT_s.t[:, g4 * 8:(g4 + 1) * 8, :], in_=pv[0:64, :].rearrange("p (i t) -> p i t", i=8)), reads=bk.all(), writes=KT_s.all())
            bsc = bank()
            P.add('pe', [(lambda b, h: lambda e: e.matmul(bsc.t[:, b * 8 + h * 4:b * 8 + h * 4 + 4], lhsT=KT_s.t[:, b * 2 + h, :], rhs=qT2.t[:, h * 4:(h + 1) * 4, b], start=True, stop=True))(b, h) for b in range(SB) for h in range(2)],
                  reads=KT_s.all() + qT2.all(), writes=bsc.all())
            P.add('act', lambda e: e.activation(out=Es.t[:, :], in_=bsc.t[:, 0:128], func=AF.Exp, scale=SCALE), reads=bsc.all(), writes=Es.all())
            P.add('dve', lambda e: e.tensor_tensor(out=Esb.t[:, :, :], in0=Es.t[:, :].rearrange("p (b h) -> p b h", b=SB), in1=bc(expBs.t[:, :].unsqueeze(1), [128, SB, 8]), op=ALU.mult),
                  reads=Es.all() + expBs.all(), writes=Esb.all())
            bden = bank(); bos = bank()
            P.add('pe', lambda e: e.matmul(bden.t[0:64, 0:128], lhsT=onesb.t[:, :], rhs=Esb.t[:, :, :].rearrange("p b h -> p (b h)"), start=True, stop=True), reads=onesb.all() + Esb.all(), writes=bden.all())
            P.add('pe', [(lambda b, h: lambda e: e.matmul(bos.t[0:64, b * 8 + h * 4:b * 8 + h * 4 + 4], lhsT=Vwb.t[:, b, h * 64:(h + 1) * 64], rhs=Esb.t[:, b, h * 4:(h + 1) * 4], start=True, stop=True))(b, h) for b in range(SB) for h in range(2)],
                  reads=Vwb.all() + Esb.all(), writes=bos.all())
            P.add('dve', lambda e: e.tensor_tensor(out=rec_s.t[:, :].rearrange("p (b h) -> p b h", b=SB), in0=bden.t[0:64, 0:128].rearrange("p (b h) -> p b h", b=SB), in1=bc(esink.t[0:64, :].unsqueeze(1), [64, SB, 8]), op=ALU.add),
                  reads=bden.all() + esink.all(), writes=rec_s.all())
            P.add('dve', lambda e: e.reciprocal(out=rec_s.t[:, :], in_=rec_s.t[:, :]), reads=rec_s.all(), writes=rec_s.all())
            P.add('dve', lambda e: e.tensor_tensor(out=aT_s.t[:, :, :].rearrange("p h b -> p b h"), in0=bos.t[0:64, 0:128].rearrange("p (b h) -> p b h", b=SB), in1=rec_s.t[:, :].rearrange("p (b h) -> p b h", b=SB), op=ALU.mult),
                  reads=bos.all() + rec_s.all(), writes=aT_s.all())
            bq = bank()
            P.add('pe', [(lambda q4: lambda e: e.matmul(bq.t[:, 0:256], lhsT=SelQ.t[:, q4, :], rhs=proj.t[:, 1280 + q4 * 256:1280 + (q4 + 1) * 256], start=(q4 == 0), stop=(q4 == 3)))(q4) for q4 in range(4)],
                  reads=SelQ.all() + proj.all(), writes=bq.all())
            P.add('dve', lambda e: e.tensor_tensor(out=accq.t[:, :], in0=bq.t[:, 0:256], in1=cw_b.t[:, 3, :], op=ALU.mult), reads=bq.all() + cw_b.all(), writes=accq.all())
            P.add('dve', lambda e: e.tensor_tensor(out=accq.t[:, :], in0=accq.t[:, :], in1=cb_b.t[:, :], op=ALU.add), reads=accq.all() + cb_b.all(), writes=accq.all())
            for j in range(3):
                P.add('dve', lambda e, j=j: e.tensor_tensor(out=tmpq.t[:, :], in0=cbuf.t[:, j, :], in1=cw_b.t[:, j, :], op=ALU.mult), reads=cbuf.all() + cw_b.all(), writes=tmpq.all())
                P.add('dve', lambda e: e.tensor_tensor(out=accq.t[:, :], in0=accq.t[:, :], in1=tmpq.t[:, :], op=ALU.add), reads=accq.all() + tmpq.all(), writes=accq.all())
            P.add('act', lambda e: e.activation(out=tmpq.t[:, :], in_=accq.t[:, :], func=AF.Silu), reads=accq.all(), writes=tmpq.all())
            bq2 = [bank(), bank()]
            P.add('pe', [(lambda q4: lambda e: e.matmul(bq2[q4 // 2].t[0:SB, (q4 % 2) * 256:(q4 % 2 + 1) * 256], lhsT=SelQT.t[:, q4, :], rhs=tmpq.t[:, :], start=True, stop=True))(q4) for q4 in range(4)],
                  reads=SelQT.all() + tmpq.all(), writes=bq2[0].all() + bq2[1].all())
            P.add('act', [(lambda i: lambda e: e.activation(out=xc_s.t[:, i * 512:(i + 1) * 512], in_=bq2[i].t[0:SB, :], func=AF.Copy))(i) for i in range(2)], reads=bq2[0].all() + bq2[1].all(), writes=xc_s.all())
            softplus_dt(proj.t[:, 2304:2312], proj.all(), SB, dt_s)
            P.add('dve', lambda e: e.tensor_tensor(out=dec_s.t[:, :], in0=dt_s.t[0:SB, 0:8], in1=A_b.t[0:SB, :], op=ALU.mult), reads=dt_s.all() + A_b.all(), writes=dec_s.all())
            P.add('act', lambda e: e.activation(out=dec_s.t[:, :], in_=dec_s.t[:, :], func=AF.Exp), reads=dec_s.all(), writes=dec_s.all())
            P.add('dve', lambda e: e.tensor_tensor(out=xdt_s.t[:, :].rearrange("p (h d) -> p h d", h=8), in0=xc_s.t[:, 0:512].rearrange("p (h d) -> p h d", h=8), in1=bc(dt_s.t[0:SB, 0:8].unsqueeze(2), [SB, 8, 64]), op=ALU.mult),
                  reads=xc_s.all() + dt_s.all(), writes=xdt_s.all())
            bsel = bank(); bsel2 = bank()
            P.add('pe', [(lambda h: lambda e: e.matmul(bsel.t[:, 0:64], lhsT=SelH.t[0:SB, h, :], rhs=xdt_s.t[:, h * 64:(h + 1) * 64], start=(h == 0), stop=(h == 7)))(h) for h in range(8)]
                  + [(lambda h: lambda e: e.matmul(bsel.t[:, 64:65], lhsT=SelH.t[0:SB, h, :], rhs=dec_s.t[:, h:h + 1], start=(h == 0), stop=(h == 7)))(h) for h in range(8)],
                  reads=SelH.all() + xdt_s.all() + dec_s.all(), writes=bsel.all())
            P.add('pe', [(lambda h: lambda e: e.matmul(bsel2.t[:, 0:128], lhsT=SelH.t[0:SB, h, :], rhs=xc_s.t[:, 512 + (h // 4) * 128:512 + (h // 4 + 1) * 128], start=(h == 0), stop=(h == 7)))(h) for h in range(8)]
                  + [(lambda h: lambda e: e.matmul(bsel2.t[:, 128:256], lhsT=SelH.t[0:SB, h, :], rhs=xc_s.t[:, 768 + (h // 4) * 128:768 + (h // 4 + 1) * 128], start=(h == 0), stop=(h == 7)))(h) for h in range(8)],
                  reads=SelH.all() + xc_s.all(), writes=bsel2.all())
            P.add('dve', [lambda e: e.tensor_copy(out=xdt_bh.t[:, :], in_=bsel.t[:, 0:64]), lambda e: e.tensor_copy(out=dec_bh.t[:, :], in_=bsel.t[:, 64:65])], reads=bsel.all(), writes=xdt_bh.all() + dec_bh.all())
            P.add('dve', [lambda e: e.tensor_copy(out=B_bh.t[:, :], in_=bsel2.t[:, 0:128]), lambda e: e.tensor_copy(out=C_bh.t[:, :], in_=bsel2.t[:, 128:256])], reads=bsel2.all(), writes=B_bh.all() + C_bh.all())
            for hf in range(4):
                hh = h0[hf % 2]; sse = 'dve'
                P.dma('sp', 'sh0_%d' % (hf % 2), hh.t[:, :, :].rearrange("p a n -> p (a n)"), sssm[:, hf * 2048:(hf + 1) * 2048], writes=hh.all())
                P.add(sse, lambda e, hf=hf: e.tensor_tensor(out=otmp.t[:, :, :], in0=bc(B_bh.t[:, :].unsqueeze(1), [128, 16, 128]), in1=bc(xdt_bh.t[:, hf * 16:(hf + 1) * 16].unsqueeze(2), [128, 16, 128]), op=ALU.mult),
                      reads=B_bh.all() + xdt_bh.all(), writes=otmp.all())
                P.add('dve', lambda e, hh=hh: e.scalar_tensor_tensor(out=hh.t[:, :, :], in0=hh.t[:, :, :], scalar=dec_bh.t[:, 0:1], in1=otmp.t[:, :, :], op0=ALU.mult, op1=ALU.add),
                      reads=hh.all() + dec_bh.all() + otmp.all(), writes=hh.all())
                P.dma('sp', 'snss%d' % (hf % 2), nss_s[:, hf * 2048:(hf + 1) * 2048], hh.t[:, :, :].rearrange("p a n -> p (a n)"), reads=hh.all())
                P.add(sse, lambda e, hh=hh: e.tensor_tensor(out=otmp.t[:, :, :], in0=hh.t[:, :, :], in1=bc(C_bh.t[:, :].unsqueeze(1), [128, 16, 128]), op=ALU.mult),
                      reads=hh.all() + C_bh.all(), writes=otmp.all())
                P.add('dve', lambda e, hf=hf: e.tensor_reduce(out=y_bh.t[:, hf * 16:(hf + 1) * 16], in_=otmp.t[:, :, :], axis=AX.X, op=ALU.add), reads=otmp.all(), writes=y_bh.all())
            bsel3 = bank()
            P.add('pe', [(lambda h: lambda e: e.matmul(bsel3.t[0:SB, h * 64:(h + 1) * 64], lhsT=SelHT.t[:, h, :], rhs=y_bh.t[:, :], start=True, stop=True))(h) for h in range(8)],
                  reads=SelHT.all() + y_bh.all(), writes=bsel3.all())
            P.add('dve', lambda e: e.tensor_copy(out=y_tok.t[:, :], in_=bsel3.t[0:SB, :]), reads=bsel3.all(), writes=y_tok.all())
            P.add('dve', lambda e: e.tensor_tensor(out=tmp_s.t[:, :].rearrange("p (h d) -> p h d", h=8), in0=xc_s.t[:, 0:512].rearrange("p (h d) -> p h d", h=8), in1=bc(D_b.t[0:SB, :].unsqueeze(2), [SB, 8, 64]), op=ALU.mult),
                  reads=xc_s.all() + D_b.all(), writes=tmp_s.all())
            P.add('dve', lambda e: e.tensor_tensor(out=y_tok.t[:, :], in0=y_tok.t[:, :], in1=tmp_s.t[:, :], op=ALU.add), reads=y_tok.all() + tmp_s.all(), writes=y_tok.all())
            P.add('act', lambda e: e.activation(out=sz_s.t[:, :], in_=proj.t[:, 768:1280], func=AF.Silu), reads=proj.all(), writes=sz_s.all())
            P.add('dve', lambda e: e.tensor_tensor(out=y_tok.t[:, :], in0=y_tok.t[:, :], in1=sz_s.t[:, :], op=ALU.mult), reads=y_tok.all() + sz_s.all(), writes=y_tok.all())

            def group_norm(y, nrow, yn, ssg):
                P.add('act', [(lambda g: lambda e: e.activation(out=yn.t[0:nrow, g * 256:(g + 1) * 256], in_=y.t[0:nrow, g * 256:(g + 1) * 256], func=AF.Square, accum_out=ssg.t[0:nrow, g:g + 1]))(g) for g in range(2)],
                      reads=y.all(), writes=yn.all() + ssg.all())
                P.add('act', lambda e: e.activation(out=ssg.t[0:nrow, 2:4], in_=ssg.t[0:nrow, 0:2], func=AF.Ln, bias=EPS, scale=1.0 / 256), reads=ssg.all(), writes=ssg.all())
                P.add('act', lambda e: e.activation(out=ssg.t[0:nrow, 4:6], in_=ssg.t[0:nrow, 2:4], func=AF.Exp, scale=-0.5), reads=ssg.all(), writes=ssg.all())
                P.add('dve', [(lambda g: lambda e: e.tensor_scalar(out=yn.t[0:nrow, g * 256:(g + 1) * 256], in0=y.t[0:nrow, g * 256:(g + 1) * 256], scalar1=ssg.t[0:nrow, 4 + g:5 + g], scalar2=None, op0=ALU.mult))(g) for g in range(2)],
                      reads=y.all() + ssg.all(), writes=yn.all())
            group_norm(y_tok, SB, yn_s, sm("ssg_s"))
            bk = bank(); pv = bk.t[:].bitcast(BF16)
            P.add('pe', [(lambda c, pv=pv: lambda e: e.transpose(out=pv[:, c * SB:(c + 1) * SB], in_=yn_s.t[:, c * 128:(c + 1) * 128], identity=identb.t[0:SB, 0:SB]))(c) for c in range(4)],
                  reads=yn_s.all() + identb.all(), writes=bk.all())
            P.add('act', lambda e, pv=pv: e.activation(out=yT_s.t[:, :, :], in_=pv[:, 0:4 * SB].rearrange("p (c b) -> p c b", c=4), func=AF.Copy), reads=bk.all(), writes=yT_s.all())

            def out_proj(aT, yT, nrow, xin_ap, xin_keys, x1_ap, x1_keys):
                for half in range(2):
                    bk = bank()
                    fns = [(lambda hq, bk=bk, half=half: lambda e: e.matmul(bk.t[0:nrow, :], lhsT=aT.t[:, hq, 0:nrow], rhs=w_out_a.t[:, hq, half * 512:(half + 1) * 512], start=(hq == 0), stop=False))(hq) for hq in range(8)]
                    fns += [(lambda c, bk=bk, half=half: lambda e: e.matmul(bk.t[0:nrow, :], lhsT=yT.t[:, c, 0:nrow], rhs=w_out_s.t[:, c, half * 512:(half + 1) * 512], start=False, stop=(c == 3)))(c) for c in range(4)]
                    P.add('pe', fns, reads=aT.all() + yT.all() + w_out_a.all() + w_out_s.all(), writes=bk.all())
                    P.add('dve', lambda e, bk=bk, half=half: e.tensor_tensor(out=x1_ap[:, half * 512:(half + 1) * 512], in0=bk.t[0:nrow, :], in1=xin_ap[:, half * 512:(half + 1) * 512], op=ALU.add),
                          reads=bk.all() + xin_keys, writes=x1_keys)
            out_proj(aT_s, yT_s, SB, xs_in.t[:, :], xs_in.all(), x1_s.t[:, :], x1_s.all())
            rmsnorm_T(x1_s.t[:, :], x1_s.all(), SB, w2b, hn_s, h2T_s.t[:, :, :], h2T_s.all(), "s2")

            P.barrier()
        P.marks['sample'] = len(P.ops)

        xin = [sb("xin%d" % i, [128, D]) for i in range(2)]
        hn = sb("hn", [128, D], BF16); hn_t[0] = hn
        hT = sb("hT", [128, 8, 128], BF16)
        qTs = [sb("qT%d" % i, [128, 4, 128], BF16) for i in range(2)]
        Kh = [sb("Kh%d" % i, [128, 2, 128], BF16) for i in range(3)]
        Vh = [sb("Vh%d" % i, [128, 2, 128], BF16) for i in range(3)]
        xs_tok = sb("xs_tok", [128, 512], BF16)
        xbcT = sb("xbcT", [128, 8, 131], BF16)
        xcT = sb("xcT", [128, 8, 128], BF16, nparts=8)
        Btok = sb("Btok", [128, 256], BF16)
        szts = [sb("szt%d" % i, [128, 512]) for i in range(2)]; dtts = [sb("dtt%d" % i, [128, 16]) for i in range(2)]
        scals = [sm("scal", 64), sb("scal1", [128, 64])]
        Lt = sb("Lt", [128, 8, 128]); MT = sb("MT", [128, 8, 128], BF16)
        xdt = sb("xdt", [128, 512], BF16); xdte = sb("xdte", [128, 512], BF16)
        yt = sb("yt", [128, 512]); ytmp = sb("ytmp", [128, 512]); yn = sb("yn", [128, 512], BF16); yT = sb("yT", [128, 4, 128], BF16)
        Eraw = sb("Eraw", [128, 4, 128]); ET = [sb("ET%d" % i, [128, 8, 128], BF16) for i in range(2)]
        rec = sb("rec", [64, 1, 512]); aT = sb("aT", [64, 8, 128], BF16)
        x1 = sb("x1", [128, GB, D], nparts=GB); h2T = sb("h2T", [128, 8, GB * 128], BF16, nparts=GB)
        actT = sb("actT", [128, NFC // 2, GB * 128], BF16, nparts=NFC // 2)
        kvout = ytmp; ncv = sb("ncv", [128, 8, 3]); ncvT = yt
        nss_sb = Lt
        ffn_sg["m"] = [yt] * 2

        P.add('dve', lambda e: e.memset(stateT.t[:, :], 0.0), writes=stateT.all())
        P.add('dve', lambda e: e.memset(stateTb.t[:, :], 0.0), writes=stateTb.all())
        P.add('dve', lambda e: e.memset(xbcT.t[:, :, :], 0.0), writes=xbcT.all())
        for i in range(3):
            P.add('dve', lambda e, i=i: e.memset(Vh[i].t[:, :, 64:128], 1.0), writes=Vh[i].all())

        blk_ctr = [0]

        def block(xsrc, t, mode):
            main = mode == 'main'
            gi = blk_ctr[0]
            blk_ctr[0] += 1
            bank_mode[0] = gi % 2
            bank_sub[0] = None
            xt = xin[gi % 2]
            qT = qTs[gi % 2]; szt = szts[gi % 2]; dtt = dtts[gi % 2]; scal = scals[gi % 2]
            pe2 = 'pool' if main else 'dve'
            P.dma('sp', 'xin%d' % (gi % 2), xt.t[:, :], xsrc[t * 128:(t + 1) * 128, :], writes=xt.all())
            rmsnorm_T(xt.t[:, :], xt.all(), 128, None, hn, hT.t[:, :, :], hT.all(), "m1", cp='act' if main else 'dve')
            chunks = []
            if main:
                chunks += [(c * 128, 'q', c) for c in range(4)]
            if main or mode == 'prelast':
                chunks += [(DIN + h * 128, 'k', h) for h in range(2)]
            nx = 8 if (main or mode == 'prelast') else 6
            chunks += [(1280 + c * 128, 'x', c) for c in range(nx)]
            slot = gi % 3
            i = 0
            while i < len(chunks):
                grp = chunks[i:i + 4]
                i += 4
                bk = bank()
                fns = []
                for gi_, (c0, kind, idx) in enumerate(grp):
                    fns += [(lambda k, gi_=gi_, c0=c0, bk=bk: lambda e: e.matmul(bk.t[:, gi_ * 128:(gi_ + 1) * 128], lhsT=w_in_sb.t[:, k, c0:c0 + 128], rhs=hT.t[:, k, :], start=(k == 0), stop=(k == 7)))(k) for k in range(8)]
                P.add('pe', fns, reads=w_in_sb.all() + hT.all(), writes=bk.all())
                j = 0
                while j < len(grp):
                    kind = grp[j][1]
                    j2 = j
                    while j2 < len(grp) and grp[j2][1] == kind:
                        j2 += 1
                    i0 = grp[j][2]
                    nn = j2 - j
                    src = bk.t[:, j * 128:j2 * 128].rearrange("p (c t) -> p c t", c=nn)
                    if kind == 'q':
                        P.add('act', lambda e, src=src, i0=i0, nn=nn: e.activation(out=qT.t[:, i0:i0 + nn, :], in_=src, func=AF.Copy), reads=bk.all(), writes=qT.all())
                    elif kind == 'k':
                        P.add('dve', lambda e, src=src, i0=i0, nn=nn: e.tensor_copy(out=Kh[slot].t[:, i0:i0 + nn, :], in_=src), reads=bk.all(), writes=Kh[slot].all())
                    elif main:
                        P.add('act', lambda e, src=src, i0=i0, nn=nn: e.activation(out=xbcT.t[:, i0:i0 + nn, 3:131], in_=src, func=AF.Copy), reads=bk.all(), writes=xbcT.all())
                        if t == NBLK - 1:
                            P.add('dve', lambda e, src=src, i0=i0, nn=nn: e.tensor_copy(out=ncv.t[:, i0:i0 + nn, :], in_=src[:, :, 125:128]), reads=bk.all(), writes=ncv.all())
                    else:
                        P.add('dve', lambda e, src=src, i0=i0, nn=nn: e.tensor_copy(out=xbcT.t[:, i0:i0 + nn, 3:131], in_=src), reads=bk.all(), writes=xbcT.all())
                    j = j2
            bdt = None
            if main:
                bA, bB = bank(), bank()
                P.add('pe', [(lambda k: lambda e: e.matmul(bA.t[:, :], lhsT=hT.t[:, k, :], rhs=w_in_sb.t[:, k, 640:1152], start=(k == 0), stop=(k == 7)))(k) for k in range(8)],
                      reads=w_in_sb.all() + hT.all(), writes=bA.all())
                P.add('pe', [(lambda k: lambda e: e.matmul(bB.t[:, 0:128], lhsT=hT.t[:, k, :], rhs=w_in_sb.t[:, k, 1152:1280], start=(k == 0), stop=(k == 7)))(k) for k in range(8)]
                      + [(lambda k: lambda e: e.matmul(bB.t[:, 128:136], lhsT=hT.t[:, k, :], rhs=w_in_sb.t[:, k, 2304:2312], start=(k == 0), stop=(k == 7)))(k) for k in range(8)],
                      reads=w_in_sb.all() + hT.all(), writes=bB.all())
                P.add('dve', lambda e: e.tensor_copy(out=Vh[slot].t[:, :, 0:64], in_=bA.t[:, 0:128].rearrange("p (h d) -> p h d", h=2)), reads=bA.all(), writes=Vh[slot].all())
                P.add('act', lambda e: e.activation(out=szt.t[:, 0:384], in_=bA.t[:, 128:512], func=AF.Silu), reads=bA.all(), writes=szt.all())
                P.add('act', lambda e: e.activation(out=szt.t[:, 384:512], in_=bB.t[:, 0:128], func=AF.Silu), reads=bB.all(), writes=szt.all())
                softplus_dt(bB.t[:, 128:136], bB.all(), 128, dtt)
                if t == NBLK - 1:
                    bKt = bank()
                    P.add('pe', [(lambda k: lambda e: e.matmul(bKt.t[:, 0:128], lhsT=hT.t[:, k, :], rhs=w_in_sb.t[:, k, 512:640], start=(k == 0), stop=(k == 7)))(k) for k in range(8)],
                          reads=w_in_sb.all() + hT.all(), writes=bKt.all())
                    P.add('dve', lambda e: e.tensor_copy(out=kvout.t[:, 0:128], in_=bKt.t[:, 0:128]), reads=bKt.all(), writes=kvout.all())
                    P.add('dve', lambda e: e.tensor_copy(out=kvout.t[:, 128:256], in_=bA.t[:, 0:128]), reads=bA.all(), writes=kvout.all())
                    P.dma('sp', 'pout', nk_p[:, :], kvout.t[:, 0:128], reads=kvout.all())
                    P.dma('sp', 'pout', nv_p[:, :], kvout.t[:, 128:256], reads=kvout.all())
            else:
                bB = bank()
                fns = [(lambda k: lambda e: e.matmul(bB.t[:, 128:136], lhsT=hT.t[:, k, :], rhs=w_in_sb.t[:, k, 2304:2312], start=(k == 0), stop=(k == 7)))(k) for k in range(8)]
                if mode == 'prelast':
                    fns += [(lambda k: lambda e: e.matmul(bB.t[:, 0:128], lhsT=hT.t[:, k, :], rhs=w_in_sb.t[:, k, 640:768], start=(k == 0), stop=(k == 7)))(k) for k in range(8)]
                P.add('pe', fns, reads=w_in_sb.all() + hT.all(), writes=bB.all())
                if mode == 'prelast':
                    P.add('dve', lambda e: e.tensor_copy(out=Vh[slot].t[:, :, 0:64], in_=bB.t[:, 0:128].rearrange("p (h d) -> p h d", h=2)), reads=bB.all(), writes=Vh[slot].all())
                softplus_dt(bB.t[:, 128:136], bB.all(), 128, dtt)
            if main:
                P.capture()
                bank_sub[0] = 0
            for c4 in range(0, nx, 4):
                bk = bank()
                ncc = min(4, nx - c4)
                fns = []
                for ci in range(ncc):
                    c = c4 + ci
                    fns += [(lambda j, c=c, ci=ci, bk=bk: lambda e: e.matmul(bk.t[:, ci * 128:(ci + 1) * 128], lhsT=convdiag.t[:, c, j, :], rhs=xbcT.t[:, c, j:j + 128], start=(j == 0), stop=(j == 3)))(j) for j in range(4)]
                P.add('pe', fns, reads=convdiag.all() + xbcT.all(), writes=bk.all())
                P.add('act', [(lambda ci, bk=bk, c4=c4: lambda e: e.activation(out=xcT.t[:, c4 + ci, :], in_=bk.t[:, ci * 128:(ci + 1) * 128], func=AF.Silu, bias=cbT.t[:, c4 + ci:c4 + ci + 1]))(ci) for ci in range(ncc)],
                      reads=bk.all() + cbT.all(), writes=xcT.p(*range(c4, c4 + ncc)))
            P.add(pe2, lambda e: e.tensor_copy(out=xbcT.t[:, :, 0:3], in_=xbcT.t[:, :, 128:131]), reads=xbcT.all(), writes=xbcT.all())
            bX = bank(); pX = bX.t[:].bitcast(BF16)
            P.add('pe', [(lambda c: lambda e: e.transpose(out=pX[:, c * 128:(c + 1) * 128], in_=xcT.t[:, c, :], identity=identb.t[:, :]))(c) for c in range(6)],
                  reads=xcT.p(0, 1, 2, 3, 4, 5) + identb.all(), writes=bX.all())
            if main:
                P.add('act', lambda e: e.activation(out=Btok.t[:, :], in_=pX[:, 512:768], func=AF.Copy), reads=bX.all(), writes=Btok.all())
                P.add('act', lambda e: e.activation(out=xs_tok.t[:, :], in_=pX[:, 0:512], func=AF.Copy), reads=bX.all(), writes=xs_tok.all())
            else:
                P.add('dve', lambda e: e.tensor_copy(out=Btok.t[:, :], in_=pX[:, 512:768]), reads=bX.all(), writes=Btok.all())
                P.add('dve', lambda e: e.tensor_copy(out=xs_tok.t[:, :], in_=pX[:, 0:512]), reads=bX.all(), writes=xs_tok.all())
            a_ = scal.t[:, 0:8]
            P.add('dve', lambda e: e.tensor_tensor(out=a_, in0=dtt.t[:, 0:8], in1=A_b.t[:, :], op=ALU.mult), reads=dtt.all() + A_b.all(), writes=scal.all())
            if main:
                P.add('pool', lambda e: e.tensor_tensor(out=Lt.t[:, :, :], in0=bc(tri.t[:, :].unsqueeze(1), [128, 8, 128]), in1=bc(a_.unsqueeze(2), [128, 8, 128]), op=ALU.mult),
                      reads=tri.all() + scal.all(), writes=Lt.all())
                bcs = bank()
                P.add('pe', lambda e: e.matmul(bcs.t[:, 0:8], lhsT=tri.t[:, :], rhs=a_, start=True, stop=True), reads=tri.all() + scal.all(), writes=bcs.all())
                P.add('dve', lambda e: e.tensor_scalar(out=scal.t[:, 8:16], in0=bcs.t[:, 0:8], scalar1=-1.0, scalar2=None, op0=ALU.mult), reads=bcs.all(), writes=scal.all())
                P.add('act', lambda e: e.activation(out=scal.t[:, 16:24], in_=bcs.t[:, 0:8], func=AF.Exp), reads=bcs.all(), writes=scal.all())
                bC0, bC1 = bank(), bank()
                for hf, bk in enumerate((bC0, bC1)):
                    P.add('pe', [lambda e, bk=bk, hf=hf: e.matmul(bk.t[:, :], lhsT=onesf.t[:, :], rhs=Lt.t[:, hf * 4:(hf + 1) * 4, :].rearrange("p h i -> p (h i)"), start=True, stop=False),
                                 lambda e, bk=bk: e.matmul(bk.t[:, :], lhsT=identb.t[:, :], rhs=negm.t[:, :, :].rearrange("p h i -> p (h i)"), start=False, stop=True)],
                          reads=onesf.all() + Lt.all() + identb.all() + negm.all(), writes=bk.all())
                for hf, bk in enumerate((bC0, bC1)):
                    P.add('act', lambda e, bk=bk, hf=hf: e.activation(out=scal.t[:, 24 + hf * 4:28 + hf * 4], in_=bk.t[:, :].rearrange("p (h i) -> p h i", h=4)[:, :, 127], func=AF.Exp), reads=bk.all(), writes=scal.all())
                    P.add('act', [(lambda hh, bk=bk, hf=hf: lambda e: e.activation(out=Lt.t[:, hf * 4 + hh, :], in_=bk.t[:, hh * 128:(hh + 1) * 128], func=AF.Exp, bias=scal.t[:, 8 + hf * 4 + hh:9 + hf * 4 + hh]))(hh) for hh in range(4)],
                          reads=bk.all() + scal.all(), writes=Lt.all())
                bCB = bank()
                P.add('pe', [(lambda g: lambda e: e.matmul(bCB.t[:, g * 128:(g + 1) * 128], lhsT=xcT.t[:, 4 + g, :], rhs=xcT.t[:, 6 + g, :], start=True, stop=True))(g) for g in range(2)],
                      reads=xcT.p(4, 5, 6, 7), writes=bCB.all())
                P.add('dve', lambda e: e.tensor_tensor(out=MT.t[:, :, :].rearrange("p (g r) i -> p g r i", g=2), in0=Lt.t[:, :, :].rearrange("p (g r) i -> p g r i", g=2),
                                                       in1=bc(bCB.t[:, 0:256].rearrange("p (g i) -> p g i", g=2).unsqueeze(2), [128, 2, 4, 128]), op=ALU.mult),
                      reads=Lt.all() + bCB.all(), writes=MT.all())
                dte_ap = Lt.t[:, :, 127]
                dte_keys = Lt.all()
                cd_ap = scal.t[:, 24:32]
            else:
                bsf = bank()
                P.add('pe', [lambda e: e.matmul(bsf.t[:, 0:8], lhsT=striu.t[:, :], rhs=a_, start=True, stop=True),
                             lambda e: e.matmul(bsf.t[:, 8:16], lhsT=onesf.t[:, :], rhs=a_, start=True, stop=True)], reads=striu.all() + onesf.all() + scal.all(), writes=bsf.all())
                P.add('act', lambda e: e.activation(out=scal.t[:, 16:32], in_=bsf.t[:, 0:16], func=AF.Exp), reads=bsf.all(), writes=scal.all())
                dte_ap = scal.t[:, 16:24]
                dte_keys = scal.all()
                cd_ap = scal.t[:, 24:32]
            P.add('dve', lambda e: e.tensor_tensor(out=scal.t[:, 32:40], in0=dtt.t[:, 0:8], in1=dte_ap, op=ALU.mult), reads=dtt.all() + dte_keys, writes=scal.all())
            pXs = xs_tok.t[:, :].rearrange("p (h d) -> p h d", h=8)
            P.add(pe2, lambda e: e.tensor_tensor(out=xdte.t[:, :].rearrange("p (h d) -> p h d", h=8), in0=pXs, in1=bc(scal.t[:, 32:40].unsqueeze(2), [128, 8, 64]), op=ALU.mult),
                  reads=xs_tok.all() + scal.all(), writes=xdte.all())
            if main:
                P.add('pool', lambda e: e.tensor_tensor(out=xdt.t[:, :].rearrange("p (h d) -> p h d", h=8), in0=pXs, in1=bc(dtt.t[:, 0:8].unsqueeze(2), [128, 8, 64]), op=ALU.mult),
                      reads=xs_tok.all() + dtt.all(), writes=xdt.all())
                bY, bYo = bank(), bank()
                P.add('pe', [(lambda h: lambda e: e.matmul(bY.t[:, h * 64:(h + 1) * 64], lhsT=MT.t[:, h, :], rhs=xdt.t[:, h * 64:(h + 1) * 64], start=True, stop=True))(h) for h in range(8)],
                      reads=MT.all() + xdt.all(), writes=bY.all())
                P.add('pe', [(lambda g: lambda e: e.matmul(bYo.t[:, g * 256:(g + 1) * 256], lhsT=xcT.t[:, 6 + g, :], rhs=stateTb.t[:, g * 256:(g + 1) * 256], start=True, stop=True))(g) for g in range(2)],
                      reads=xcT.p(6, 7) + stateTb.all(), writes=bYo.all())
                P.add('dve', lambda e: e.tensor_tensor(out=yt.t[:, :].rearrange("p (h d) -> p h d", h=8), in0=bYo.t[:, :].rearrange("p (h d) -> p h d", h=8), in1=bc(scal.t[:, 16:24].unsqueeze(2), [128, 8, 64]), op=ALU.mult),
                      reads=bYo.all() + scal.all(), writes=yt.all())
                P.add('dve', lambda e: e.tensor_tensor(out=yt.t[:, :], in0=yt.t[:, :], in1=bY.t[:, :], op=ALU.add), reads=yt.all() + bY.all(), writes=yt.all())
                P.add('pool', lambda e: e.tensor_tensor(out=ytmp.t[:, :].rearrange("p (h d) -> p h d", h=8), in0=pXs, in1=bc(D_b.t[:, :].unsqueeze(2), [128, 8, 64]), op=ALU.mult),
                      reads=xs_tok.all() + D_b.all(), writes=ytmp.all())
                P.add('dve', lambda e: e.tensor_tensor(out=yt.t[:, :], in0=yt.t[:, :], in1=ytmp.t[:, :], op=ALU.add), reads=yt.all() + ytmp.all(), writes=yt.all())
                P.add('dve', lambda e: e.tensor_tensor(out=yt.t[:, :], in0=yt.t[:, :], in1=szt.t[:, :], op=ALU.mult), reads=yt.all() + szt.all(), writes=yt.all())
                group_norm(yt, 128, yn, sm("ssg_m"))
                bk = bank(); pv = bk.t[:].bitcast(BF16)
                P.add('pe', [(lambda c, pv=pv: lambda e: e.transpose(out=pv[:, c * 128:(c + 1) * 128], in_=yn.t[:, c * 128:(c + 1) * 128], identity=identb.t[:, :]))(c) for c in range(4)],
                      reads=yn.all() + identb.all(), writes=bk.all())
                P.add('act', lambda e, pv=pv: e.activation(out=yT.t[:, :, :], in_=pv[:, 0:512].rearrange("p (c t) -> p c t", c=4), func=AF.Copy), reads=bk.all(), writes=yT.all())
            bS = bank()
            P.add('pe', [(lambda g: lambda e: e.matmul(bS.t[:, g * 256:(g + 1) * 256], lhsT=Btok.t[:, g * 128:(g + 1) * 128], rhs=xdte.t[:, g * 256:(g + 1) * 256], start=True, stop=True))(g) for g in range(2)],
                  reads=Btok.all() + xdte.all(), writes=bS.all())
            P.add(pe2, lambda e: e.tensor_tensor(out=stateT.t[:, :].rearrange("p (h d) -> p h d", h=8), in0=stateT.t[:, :].rearrange("p (h d) -> p h d", h=8), in1=bc(cd_ap.unsqueeze(2), [128, 8, 64]), op=ALU.mult),
                  reads=stateT.all() + scal.all(), writes=stateT.all())
            P.add('dve', lambda e: e.tensor_tensor(out=stateT.t[:, :], in0=stateT.t[:, :], in1=bS.t[:, :], op=ALU.add), reads=stateT.all() + bS.all(), writes=stateT.all())
            if mode == 'prelast':
                P.add('dve', lambda e: e.tensor_scalar(out=stateT.t[:, :], in0=stateT.t[:, :], scalar1=flag.t[:, 0:1], scalar2=None, op0=ALU.mult), reads=stateT.all() + flag.all(), writes=stateT.all())
                P.add('dve', lambda e: e.tensor_scalar(out=xbcT.t[:, :, 0:3], in0=xbcT.t[:, :, 0:3], scalar1=flag.t[:, 0:1], scalar2=None, op0=ALU.mult), reads=xbcT.all() + flag.all(), writes=xbcT.all())
            if main or mode == 'prelast':
                P.add('act', lambda e: e.activation(out=stateTb.t[:, :], in_=stateT.t[:, :], func=AF.Copy), reads=stateT.all(), writes=stateTb.all())
            if not main:
                return
            ssd_ops = P.end_capture()
            P.capture()
            bank_sub[0] = 1
            pslot = (gi - 1) % 3
            for kb, sl in enumerate((pslot, slot)):
                bS0, bS1 = bank(), bank()
                for par, bk in enumerate((bS0, bS1)):
                    P.add('pe', [(lambda hh, j, bk=bk, par=par, sl=sl: lambda e: e.matmul(bk.t[:, (hh * 2 + j) * 128:(hh * 2 + j + 1) * 128], lhsT=Kh[sl].t[par * 64:par * 64 + 64, hh, :], rhs=qT.t[par * 64:par * 64 + 64, hh * 2 + j, :], start=True, stop=True))(hh, j) for hh in range(2) for j in range(2)],
                          reads=Kh[sl].all() + qT.all(), writes=bk.all())
                ebi = 1 - kb
                for par, bk in enumerate((bS0, bS1)):
                    P.add('act', lambda e, bk=bk: e.activation(out=Eraw.t[:, :, :], in_=bk.t[:, :].rearrange("p (c q) -> p c q", c=4), func=AF.Exp, scale=SCALE), reads=bk.all(), writes=Eraw.all())
                    P.add('dve', lambda e, kb=kb, ebi=ebi, par=par: e.tensor_tensor(out=ET[kb].t[:, :, :].rearrange("p (c par) q -> p par c q", par=2)[:, par], in0=Eraw.t[:, :, :],
                                                                            in1=expB.t[:, ebi, :, :].rearrange("p q (c par) -> p par c q", par=2)[:, par], op=ALU.mult),
                          reads=Eraw.all() + expB.all(), writes=ET[kb].all())
                if kb == 0 and t == 0:
                    P.add('dve', lambda e: e.tensor_scalar(out=ET[0].t[:, :, :], in0=ET[0].t[:, :, :], scalar1=flag.t[:, 0:1], scalar2=None, op0=ALU.mult), reads=ET[0].all() + flag.all(), writes=ET[0].all())
            bO = [bank(), bank()]
            for hh in range(2):
                P.add('pe', [(lambda kb, hh=hh: lambda e: e.matmul(bO[hh].t[:, :], lhsT=Vh[(pslot, slot)[kb]].t[:, hh, :], rhs=ET[kb].t[:, hh * 4:(hh + 1) * 4, :].rearrange("p r q -> p (r q)"), start=(kb == 0), stop=(kb == 1)))(kb) for kb in range(2)],
                      reads=Vh[pslot].all() + Vh[slot].all() + ET[0].all() + ET[1].all(), writes=bO[hh].all())
                P.add('dve', lambda e, hh=hh: e.tensor_tensor(out=rec.t[:, 0, :].rearrange("p (r q) -> p r q", r=4), in0=bO[hh].t[64:128, :].rearrange("p (r q) -> p r q", r=4), in1=bc(esink.t[64:128, hh * 4:(hh + 1) * 4].unsqueeze(2), [64, 4, 128]), op=ALU.add),
                      reads=bO[hh].all() + esink.all(), writes=rec.all())
                P.add('dve', lambda e, hh=hh: e.reciprocal(out=rec.t[:, 0, :], in_=rec.t[:, 0, :]), reads=rec.all(), writes=rec.all())
                P.add('dve', lambda e, hh=hh: e.tensor_tensor(out=aT.t[:, hh * 4:(hh + 1) * 4, :].rearrange("p r q -> p (r q)"), in0=bO[hh].t[0:64, :], in1=rec.t[:, 0, :], op=ALU.mult),
                      reads=bO[hh].all() + rec.all(), writes=aT.all())
            att_ops = P.end_capture()
            bank_sub[0] = None
            P.ops.extend(Prog.merge(ssd_ops, att_ops))
            b_in_g = t % GB
            out_proj(aT, yT, 128, xt.t[:, :], xt.all(), x1.t[:, b_in_g, :], x1.p(b_in_g))
            rmsnorm_T(x1.t[:, b_in_g, :], x1.p(b_in_g), 128, w2b, hn, h2T.t[:, :, b_in_g * 128:(b_in_g + 1) * 128], h2T.p(b_in_g), "m2")

        for t in range(NBLK):
            P.capture()
            block(xp, t, 'prelast' if t == NBLK - 1 else 'pre')
            bank_mode[0] = None
            P.pipe_push(P.end_capture())
        P.pipe_drain()
        hn_t[0] = hn_sf
        def s_out(bi, full, keys):
            P.dma('sp', 'sout', y_s[:, :], full, reads=keys)
        P.capture()
        bank_mode[0] = 1
        ffn(h2T_s, SB, [(0, SB)], lambda bi, half: x1_s.t[:, :] if half is None else x1_s.t[:, half * 512:(half + 1) * 512], lambda bi: x1_s.all(), s_out, actT_s, "s")
        bank_mode[0] = None
        P.pipe_push(P.end_capture(), split=False)
        hn_t[0] = hn
        P.marks['prefix'] = len(P.ops)
        for t in range(NBLK):
            P.marks['main%d' % t] = len(P.ops)
            P.capture()
            block(xm, t, 'main')
            bank_mode[0] = None
            P.pipe_push(P.end_capture())
            if t % GB == GB - 1:
                g0 = t - (GB - 1)
                P.pipe_drain()

                def m_out(bi, full, keys, g0=g0):
                    P.dma('sp', 'yout%d' % bi, y_m[(g0 + bi) * 128:(g0 + bi + 1) * 128, :], full, reads=keys)
                tail = ffn(h2T, GB * 128, [(bi * 128, 128) for bi in range(GB)],
                           lambda bi, half: x1.t[:, bi, :] if half is None else x1.t[:, bi, half * 512:(half + 1) * 512],
                           lambda bi: x1.p(bi), m_out, actT, "m", defer_tail=True)
                P.pipe_push(tail, split=False)
        P.pipe_drain()
        P.marks['mainend'] = len(P.ops)
        bk = bank()
        P.add('pe', lambda e: e.matmul(bk.t[0:24, 0:128], lhsT=ncv.t[:, :, :].rearrange("p c j -> p (c j)"), rhs=identf.t[:, :], start=True, stop=True), reads=ncv.all() + identf.all(), writes=bk.all())
        P.add('dve', lambda e: e.tensor_copy(out=ncvT.t[0:24, 0:128], in_=bk.t[0:24, 0:128]), reads=bk.all(), writes=ncvT.all())
        for c in range(8):
            P.dma('sp', 'pout', ncv_p[:, c * 128:(c + 1) * 128], ncvT.t[c * 3:(c + 1) * 3, 0:128], reads=ncvT.all())
        bk2 = bank()
        P.add('pe', [(lambda c: lambda e: e.matmul(bk2.t[:, c * 128:(c + 1) * 128], lhsT=stateT.t[:, c * 128:(c + 1) * 128], rhs=identf.t[:, :], start=True, stop=True))(c) for c in range(4)],
              reads=stateT.all() + identf.all(), writes=bk2.all())
        P.add('dve', lambda e: e.tensor_copy(out=nss_sb.t[:, 0:4, :], in_=bk2.t[:, :].rearrange("p (c n) -> p c n", c=4)), reads=bk2.all(), writes=nss_sb.all())
        P.dma('sp', 'pout', nss_p.rearrange("(c p) n -> p c n", p=128), nss_sb.t[:, 0:4, :], reads=nss_sb.all())

        _P[0] = P
        with nc.Block() as blockctx:
            P.finish(st)
    return nc


def _bucket_onehot():
    n = np.arange(128)
    exact = 16
    nf = np.maximum(n, 1).astype(np.float32)
    large = exact + (np.log(nf / exact) / math.log(128 / exact) * (32 - exact)).astype(np.int32)
    bucket = np.where(n < exact, n, np.minimum(large, 31))
    oh = np.zeros((32, 128), np.float32)
    oh[bucket, n] = 1.0
    return oh


_NC = [None]
_P = [None]


def kernel(**inp):
    f = lambda a: np.ascontiguousarray(np.asarray(a, dtype=np.float32))
    x_prompt = f(inp["x_prompt"]); x_sample = f(inp["x_sample"])
    if _NC[0] is None:
        _NC[0] = build()
    nc = _NC[0]
    shared = {
        "rel_bias": f(inp["rel_bias"]), "onehot": _bucket_onehot(),
        "norm1_w": f(inp["norm1_w"]).reshape(1, D), "w_in": f(inp["w_in"])[0], "attn_sinks": f(inp["attn_sinks"]).reshape(1, 8),
        "conv_w": f(inp["conv_w"])[0], "conv_b": f(inp["conv_b"]).reshape(1, 1024), "dt_bias": f(inp["dt_bias"]).reshape(1, 8),
        "A_log": f(inp["A_log"]).reshape(1, 8), "D_skip": f(inp["D_skip"]).reshape(1, 8), "ssm_norm_w": f(inp["ssm_norm_w"]).reshape(1, 512),
        "w_out": f(inp["w_out"])[0], "norm2_w": f(inp["norm2_w"]).reshape(1, D), "w_gate": f(inp["w_gate"])[0], "w_up": f(inp["w_up"])[0],
        "w_down": f(inp["w_down"])[0], "final_norm_w": f(inp["final_norm_w"]).reshape(1, D),
    }
    ck = f(inp["cache_k"])[0].reshape(128, 128, 128); cv = f(inp["cache_v"])[0].reshape(128, 128, 128)
    sconv = f(inp["state_conv"])[0]; sssm = f(inp["state_ssm"])[0].reshape(128 * 8, 64 * 128)
    in_maps = []
    for c in range(NCORES):
        b, half = c // 2, c % 2
        m = dict(shared)
        m["xm"] = x_prompt[b, half * 2048:(half + 1) * 2048]
        m["xp"] = x_prompt[b, 0:2048] if half == 1 else np.zeros((2048, D), np.float32)
        m["flag"] = np.full((128, 1), float(half), np.float32)
        m["xsm"] = x_sample[c * SB:(c + 1) * SB, 0]
        m["ck"] = ck[c * SB:(c + 1) * SB]; m["cv"] = cv[c * SB:(c + 1) * SB]
        m["sconv"] = sconv[c * SB:(c + 1) * SB]; m["sssm"] = sssm[c * SB * 8:(c + 1) * SB * 8]
        in_maps.append({k: np.ascontiguousarray(v) for k, v in m.items()})
    res = run_bass_kernel_spmd(nc, in_maps, core_ids=list(range(NCORES))).results
    yp = np.zeros((4, 4096, D), np.float32)
    for c in range(NCORES):
        yp[c // 2, (c % 2) * 2048:(c % 2 + 1) * 2048] = res[c]["y_m"]
    ys = np.concatenate([res[c]["y_s"] for c in range(NCORES)], 0).reshape(128, 1, D)
    odd = [1, 3, 5, 7]
    nkp = np.stack([res[c]["nk_p"].reshape(128, 2, 64) for c in odd])[None]
    nvp = np.stack([res[c]["nv_p"].reshape(128, 2, 64) for c in odd])[None]
    ncp = np.stack([res[c]["ncv_p"] for c in odd])[None]
    nsp = np.stack([res[c]["nss_p"].reshape(8, 64, 128) for c in odd])[None]
    nks = np.concatenate([res[c]["nk_s"] for c in range(NCORES)], 0).reshape(1, 128, 128, 2, 64)
    nvs = np.concatenate([res[c]["nv_s"] for c in range(NCORES)], 0).reshape(1, 128, 128, 2, 64)
    ncs = np.concatenate([res[c]["ncv_s"] for c in range(NCORES)], 0)[None]
    nsss = np.concatenate([res[c]["nss_s"] for c in range(NCORES)], 0).reshape(1, 128, 8, 64, 128)
    return (yp, ys, nkp.astype(np.float32), nvp.astype(np.float32), ncp.astype(np.float32), nsp.astype(np.float32),
            nks, nvs, ncs, nsss)
```

```python
import math
from contextlib import ExitStack
import numpy as np
import concourse.bass as bass
import concourse.mybir as mybir
from concourse.bass_utils import run_bass_kernel_spmd

F32 = mybir.dt.float32
BF16 = mybir.dt.bfloat16
AF = mybir.ActivationFunctionType
ALU = mybir.AluOpType
AX = mybir.AxisListType

NCORES = 8
D = 1024
NBLK = 16
GB = 4
DFF = 2816
NFC = DFF // 128
DIN = 2312
WIN_W = DIN + 256
SCALE = 0.125
EPS = 1e-6
NEG = -30000.0
SB = 16


class Tl:
    def __init__(self, name, t, nparts=1):
        self.name, self.t, self.nparts = name, t, nparts

    def all(self):
        return [(self.name, i) for i in range(self.nparts)]

    def p(self, *idx):
        return [(self.name, i) for i in idx]


class Prog:
    def __init__(self, nc):
        self.nc = nc
        self.ops = []
        self.marks = {}
        self.eng = {'pe': nc.tensor, 'act': nc.scalar, 'dve': nc.vector, 'pool': nc.gpsimd, 'sp': nc.sync}

    def add(self, eng, fns, reads=(), writes=(), stream=None):
        if callable(fns):
            fns = [fns]
        import sys as _s
        fr = _s._getframe(1)
        if fr.f_code.co_name == 'dma':
            fr = fr.f_back
        self.ops.append(dict(eng=eng, fns=list(fns), reads=list(reads), writes=list(writes), stream=stream, barrier=False, line=fr.f_lineno))

    def dma(self, eng, stream, out, in_, reads=(), writes=(), **kw):
        self.add(eng, [lambda e: e.dma_start(out=out, in_=in_, **kw)], reads, writes, stream=stream)

    def capture(self):
        if not hasattr(self, '_cstack'):
            self._cstack = []
        self._cstack.append(self.ops)
        self.ops = []

    def end_capture(self):
        got = self.ops
        self.ops = self._cstack.pop()
        return got

    @staticmethod
    def merge(a, b):
        out = []
        i = j = 0
        while i < len(a) or j < len(b):
            if j >= len(b) or (i < len(a) and i * len(b) <= j * len(a)):
                out.append(a[i]); i += 1
            else:
                out.append(b[j]); j += 1
        return out

    def pipe_push(self, blk_ops, split=True, frac=0.44):
        cur = getattr(self, '_cur', [])
        if split:
            h = int(len(blk_ops) * frac)
            head, tail = blk_ops[:h], blk_ops[h:]
        else:
            head, tail = [], blk_ops
        i = j = 0
        a, b = len(cur), len(head)
        while i < a or j < b:
            if j >= b or (i < a and i * b <= j * a):
                self.ops.append(cur[i]); i += 1
            else:
                self.ops.append(head[j]); j += 1
        self._cur = tail

    def pipe_drain(self):
        self.ops.extend(getattr(self, '_cur', []))
        self._cur = []

    def barrier(self):
        for e in ('pe', 'act', 'dve', 'pool', 'sp'):
            self.ops.append(dict(eng=e, fns=[], reads=[], writes=[], stream=None, barrier=True))

    def finish(self, stack):
        nc = self.nc
        import os
        mx = os.environ.get("K_MAXOPS")
        if mx:
            mx = self.marks.get(mx, None) if not mx.isdigit() else int(mx)
            self.ops = self.ops[:mx]
        ops = self.ops
        n = len(ops)
        last_w, readers = {}, {}
        deps = [set() for _ in range(n)]
        for i, op in enumerate(ops):
            if op['barrier']:
                seen_e = {}
                for j in range(i - 1, -1, -1):
                    o = ops[j]
                    if o['barrier']:
                        continue
                    key = o['stream'] if o['stream'] else o['eng']
                    if key == 'wcast':
                        continue
                    if key not in seen_e:
                        seen_e[key] = j
                deps[i] = set(seen_e.values())
                continue
            dset = deps[i]
            for k in op['reads']:
                if k in last_w:
                    dset.add(last_w[k])
                if k[0].startswith('ps'):
                    for r in readers.get(k, ()):
                        if ops[r]['eng'] != op['eng']:
                            dset.add(r)
            for k in op['writes']:
                if k in last_w:
                    dset.add(last_w[k])
                for r in readers.get(k, ()):
                    dset.add(r)
            dset.discard(i)
            for k in op['reads']:
                readers.setdefault(k, []).append(i)
            for k in op['writes']:
                last_w[k] = i
                readers[k] = []
            if op['eng'] == 'pe' and op['stream'] is None:
                deps[i] = {j for j in dset if not (ops[j]['eng'] == 'pe' and ops[j]['stream'] is None)}
        needed = [False] * n
        for i in range(n):
            for j in deps[i]:
                needed[j] = True
        sems = {}

        def getsem(name):
            if name not in sems:
                sems[name] = stack.enter_context(nc.semaphore("s_" + name))
            return sems[name]
        counts = {}
        sig = [None] * n
        for i, op in enumerate(ops):
            if op['barrier'] or not op['fns']:
                continue
            if op['stream']:
                key = 'd_' + op['stream']
                counts[key] = counts.get(key, 0) + 16 * len(op['fns'])
                sig[i] = (key, counts[key])
            elif needed[i]:
                key = 'e_' + op['eng']
                counts[key] = counts.get(key, 0) + 1
                sig[i] = (key, counts[key])
        seen = {e: {} for e in self.eng}
        issued = {}
        for i, op in enumerate(ops):
            e = self.eng[op['eng']]
            waits = {}
            for j in deps[i]:
                if sig[j] is None:
                    continue
                k, v = sig[j]
                if k.startswith('d_'):
                    v = issued[k]
                if v > waits.get(k, 0):
                    waits[k] = v
            wl = [(k, v) for k, v in waits.items() if seen[op['eng']].get(k, 0) < v]
            for k, v in wl:
                seen[op['eng']][k] = v
            if op['barrier'] or not op['fns']:
                for k, v in wl:
                    e.wait_ge(getsem(k), v)
                continue
            attach = None
            if wl and op['stream'] is None:
                attach = wl[0]
                wl = wl[1:]
            for k, v in wl:
                e.wait_ge(getsem(k), v)
            ins = None
            for fi, fn in enumerate(op['fns']):
                ins = fn(e)
                if fi == 0 and attach is not None:
                    ins._wait_ge(getsem(attach[0]), attach[1])
                if op['stream']:
                    ins.then_inc(getsem(sig[i][0]), 16)
            if sig[i] is not None and not op['stream']:
                ins.then_inc(getsem(sig[i][0]), 1)
            if op['stream']:
                issued[sig[i][0]] = sig[i][1]
        sp = nc.sync
        for k, v in counts.items():
            if k.startswith('d_'):
                sp.wait_ge(getsem(k), v)


def bc(ap, shape):
    return ap.broadcast_to(shape)


def build():
    nc = bass.Bass("TRN2", target_bir_lowering=False)
    P = Prog(nc)

    def din(name, shape):
        return nc.dram_tensor(name, shape, F32, kind="ExternalInput").ap()

    def dout(name, shape):
        return nc.dram_tensor(name, shape, F32, kind="ExternalOutput").ap()

    xm = din("xm", [NBLK * 128, D]); xp = din("xp", [NBLK * 128, D]); flag_d = din("flag", [128, 1])
    xsm = din("xsm", [SB, D]); ck = din("ck", [SB, 128, 128]); cv = din("cv", [SB, 128, 128])
    sconv = din("sconv", [SB, 3, 1024]); sssm = din("sssm", [SB * 8, 64 * 128])
    rel_bias = din("rel_bias", [32, 8]); onehot_d = din("onehot", [32, 128])
    norm1_w = din("norm1_w", [1, D]); w_in = din("w_in", [D, DIN]); sinks = din("attn_sinks", [1, 8])
    conv_w = din("conv_w", [4, 1024]); conv_b = din("conv_b", [1, 1024]); dt_bias = din("dt_bias", [1, 8])
    A_log = din("A_log", [1, 8]); D_skip = din("D_skip", [1, 8]); ssm_norm_w = din("ssm_norm_w", [1, 512])
    w_out = din("w_out", [D, D]); norm2_w = din("norm2_w", [1, D]); w_gate = din("w_gate", [D, DFF])
    w_up = din("w_up", [D, DFF]); w_down = din("w_down", [DFF, D]); fnorm_w = din("final_norm_w", [1, D])

    y_m = dout("y_m", [NBLK * 128, D]); y_s = dout("y_s", [SB, D])
    nk_p = dout("nk_p", [128, 128]); nv_p = dout("nv_p", [128, 128]); ncv_p = dout("ncv_p", [3, 1024])
    nss_p = dout("nss_p", [512, 128])
    nk_s = dout("nk_s", [SB, 128, 128]); nv_s = dout("nv_s", [SB, 128, 128])
    ncv_s = dout("ncv_s", [SB, 3, 1024]); nss_s = dout("nss_s", [SB * 8, 64 * 128])
    scrS = nc.dram_tensor("scrS", [383, 8], F32).ap()
    scr1 = nc.dram_tensor("scr1", [SB, 512], F32).ap()
    scr2 = nc.dram_tensor("scr2", [SB, 8], F32).ap()
    scr3 = nc.dram_tensor("scr3", [SB, 512], F32).ap()
    scr4 = nc.dram_tensor("scr4", [SB * 8, 64], F32).ap()
    scrKV = nc.dram_tensor("scrKV", [SB, 256], BF16).ap()
    wgu_bf = nc.dram_tensor("wgu_bf", [NFC, 128, 2, 8, 128], BF16).ap()
    wd_bf = nc.dram_tensor("wd_bf", [DFF, D], BF16).ap()

    st = ExitStack()
    with st:
        def sb(name, shape, dt=F32, nparts=1, stack=st):
            return Tl(name, stack.enter_context(nc.sbuf_tensor(name, shape, dt)), nparts)

        ps = [Tl("ps%d" % i, st.enter_context(nc.psum_tensor("ps%d" % i, [128, 512], F32))) for i in range(8)]
        psrr = [0, 0, 0]
        psub = {}
        bank_mode = [None]
        bank_sub = [None]

        def bank():
            m = bank_mode[0]
            if m is None:
                b = ps[psrr[2] % 8]
                psrr[2] += 1
            elif bank_sub[0] is None:
                b = ps[m * 4 + psrr[m] % 4]
                psrr[m] += 1
            else:
                k = (m, bank_sub[0])
                psub[k] = psub.get(k, 0) + 1
                b = ps[m * 4 + bank_sub[0] * 2 + psub[k] % 2]
            return b

        identf = sb("identf", [128, 128]); identb = sb("identb", [128, 128], BF16)
        tri = sb("tri", [128, 128])
        striu = sb("striu", [128, 128])
        onesf = sb("onesf", [128, 128]); onesb = sb("onesb", [128, 64], BF16)
        negm = sb("negm", [128, 4, 128], BF16)
        expB = sb("expB", [128, 2, 128, 8])
        expBs = sb("expBs", [128, 8])
        w1T = sb("w1T", [128, 8]); w2b = sb("w2b", [128, D]); wfb = sb("wfb", [128, D]); wsT = sb("wsT", [128, 4])
        convdiag = sb("convdiag", [128, 8, 4, 128], BF16)
        cwT = sb("cwT", [128, 4, 8]); cbT = sb("cbT", [128, 8])
        A_b = sb("A_b", [128, 8]); dtb_b = sb("dtb_b", [128, 8]); D_b = sb("D_b", [128, 8]); esink = sb("esink", [128, 8])
        flag = sb("flag_sb", [128, 1])
        zero = sb("zero_sb", [128, 8])
        w_in_sb = sb("w_in_sb", [128, 8, WIN_W], BF16, nparts=8)
        w_out_a = sb("w_out_a", [64, 8, D], BF16); w_out_s = sb("w_out_s", [128, 4, D], BF16)
        NSL = 3
        wgu_sl = [sb("wgu%d" % i, [128, 2, 8, 128], BF16) for i in range(NSL)]
        wd_sl = [sb("wd%d" % i, [128, D], BF16) for i in range(NSL)]
        stateT = sb("stateT", [128, 512]); stateTb = sb("stateTb", [128, 512], BF16)
        small = {}

        def sm(name, w=8):
            if name not in small:
                small[name] = sb("sm_" + name, [128, w])
            return small[name]

        def cst(tl, fn):
            for f_ in (fn if isinstance(fn, (list, tuple)) else [fn]):
                P.add('pool', f_, reads=tl.all(), writes=tl.all())
        cst(identf, [lambda e: e.memset(identf.t[:], 1.0),
                     lambda e: e.affine_select(out=identf.t[:], in_=identf.t[:], pattern=[[-1, 128]], compare_op=ALU.is_equal, fill=0.0, base=0, channel_multiplier=1)])
        cst(tri, [lambda e: e.memset(tri.t[:], 1.0),
                  lambda e: e.affine_select(out=tri.t[:], in_=tri.t[:], pattern=[[1, 128]], compare_op=ALU.is_ge, fill=0.0, base=0, channel_multiplier=-1)])
        cst(striu, [lambda e: e.memset(striu.t[:], 1.0),
                    lambda e: e.affine_select(out=striu.t[:], in_=striu.t[:], pattern=[[-1, 128]], compare_op=ALU.is_gt, fill=0.0, base=0, channel_multiplier=1)])
        cst(onesf, lambda e: e.memset(onesf.t[:], 1.0))
        cst(onesb, lambda e: e.memset(onesb.t[:], 1.0))
        cst(zero, lambda e: e.memset(zero.t[:], 0.0))
        P.add('dve', lambda e: e.tensor_copy(out=identb.t[:], in_=identf.t[:]), reads=identf.all(), writes=identb.all())

        def bload(tl, src, width):
            P.dma('sp', 'cst', tl.t[:, 0:width], src[0:1, :].partition_broadcast(128), writes=tl.all())
        bload(w2b, norm2_w, D); bload(wfb, fnorm_w, D)
        P.dma('sp', 'cst', wsT.t[:], ssm_norm_w.rearrange("o (c p) -> p (o c)", p=128), writes=wsT.all(), allow_slow_non_contiguous=True)
        bload(A_b, A_log, 8); bload(dtb_b, dt_bias, 8); bload(D_b, D_skip, 8); bload(esink, sinks, 8)
        P.dma('sp', 'cst', flag.t[:], flag_d[:, :], writes=flag.all())
        P.dma('sp', 'cst', w1T.t[:], norm1_w.rearrange("o (k p) -> p (o k)", p=128), writes=w1T.all(), allow_slow_non_contiguous=True)
        for j in range(4):
            P.dma('sp', 'cst', cwT.t[:, j, :], conv_w[j:j + 1, :].rearrange("o (c p) -> p (o c)", p=128), writes=cwT.all(), allow_slow_non_contiguous=True)
        P.dma('sp', 'cst', cbT.t[:], conv_b.rearrange("o (c p) -> p (o c)", p=128), writes=cbT.all(), allow_slow_non_contiguous=True)
        P.add('act', lambda e: e.activation(out=A_b.t[:], in_=A_b.t[:], func=AF.Exp), reads=A_b.all(), writes=A_b.all())
        P.add('dve', lambda e: e.tensor_scalar(out=A_b.t[:], in0=A_b.t[:], scalar1=-1.0, scalar2=None, op0=ALU.mult), reads=A_b.all(), writes=A_b.all())
        P.add('act', lambda e: e.activation(out=esink.t[:], in_=esink.t[:], func=AF.Exp), reads=esink.all(), writes=esink.all())
        for c in range(8):
            for j in range(4):
                P.add('dve', (lambda c, j: lambda e: e.tensor_scalar(out=convdiag.t[:, c, j, :], in0=identf.t[:], scalar1=cwT.t[:, j, c:c + 1], scalar2=None, op0=ALU.mult))(c, j),
                      reads=identf.all() + cwT.all(), writes=convdiag.all())
        P.dma('pool', 'win', w_in_sb.t[:, :, 0:DIN], w_in.rearrange("(k p) n -> p k n", p=128), writes=w_in_sb.all())
        for k in range(8):
            P.add('dve', lambda e, k=k: e.tensor_scalar(out=w_in_sb.t[:, k, 0:DIN], in0=w_in_sb.t[:, k, 0:DIN], scalar1=w1T.t[:, k:k + 1], scalar2=None, op0=ALU.mult), reads=w_in_sb.all() + w1T.all(), writes=w_in_sb.all())
        for h in range(2):
            for dup in range(2):
                o0 = DIN + h * 128 + dup * 64
                P.add('dve', lambda e, o0=o0, h=h: e.tensor_copy(out=w_in_sb.t[:, :, o0:o0 + 64], in_=w_in_sb.t[:, :, 512 + h * 64:512 + (h + 1) * 64]), reads=w_in_sb.all(), writes=w_in_sb.all())
        P.dma('pool', 'wout', w_out_a.t[:, :, :], w_out[0:512, :].rearrange("(h d) n -> d h n", d=64), writes=w_out_a.all())
        P.dma('pool', 'wout', w_out_s.t[:, :, :], w_out[512:1024, :].rearrange("(c p) n -> p c n", p=128), writes=w_out_s.all())
        for c in range(4):
            P.add('dve', lambda e, c=c: e.tensor_scalar(out=w_out_s.t[:, c, :], in0=w_out_s.t[:, c, :], scalar1=wsT.t[:, c:c + 1], scalar2=None, op0=ALU.mult), reads=w_out_s.all() + wsT.all(), writes=w_out_s.all())
        def rmsnorm_T(xin_ap, xin_keys, nrow, wb, hn, hTdst, hT_keys, tag, cp='act'):
            ss = sm("ss_" + tag, 4)
            junk = hn
            P.add('act', lambda e: e.activation(out=junk.t[0:nrow, :], in_=xin_ap, func=AF.Square, accum_out=ss.t[0:nrow, 0:1]),
                  reads=xin_keys, writes=junk.all() + ss.all())
            P.add('act', lambda e: e.activation(out=ss.t[0:nrow, 1:2], in_=ss.t[0:nrow, 0:1], func=AF.Ln, bias=EPS, scale=1.0 / D), reads=ss.all(), writes=ss.all())
            P.add('act', lambda e: e.activation(out=ss.t[0:nrow, 2:3], in_=ss.t[0:nrow, 1:2], func=AF.Exp, scale=-0.5), reads=ss.all(), writes=ss.all())
            if wb is None:
                P.add('dve', lambda e: e.tensor_scalar(out=hn.t[0:nrow, :], in0=xin_ap, scalar1=ss.t[0:nrow, 2:3], scalar2=None, op0=ALU.mult),
                      reads=xin_keys + ss.all(), writes=hn.all())
            else:
                P.add('dve', lambda e: e.scalar_tensor_tensor(out=hn.t[0:nrow, :], in0=xin_ap, scalar=ss.t[0:nrow, 2:3], in1=wb.t[0:nrow, :], op0=ALU.mult, op1=ALU.mult),
                      reads=xin_keys + ss.all() + wb.all(), writes=hn.all())
            bk = bank()
            pv = bk.t[:].bitcast(BF16)
            P.add('pe', [(lambda k: lambda e: e.transpose(out=pv[:, k * 128:k * 128 + nrow], in_=hn.t[0:nrow, k * 128:(k + 1) * 128], identity=identb.t[0:nrow, 0:nrow]))(k) for k in range(8)],
                  reads=hn.all() + identb.all(), writes=bk.all())
            if cp == 'act':
                P.add('act', lambda e: e.activation(out=hTdst, in_=pv.rearrange("p (k t) -> p k t", k=8)[:, :, 0:nrow], func=AF.Copy), reads=bk.all(), writes=hT_keys)
            else:
                P.add('dve', lambda e: e.tensor_copy(out=hTdst, in_=pv.rearrange("p (k t) -> p k t", k=8)[:, :, 0:nrow]), reads=bk.all(), writes=hT_keys)
            return ss

        def softplus_dt(ps_ap, ps_keys, nrow, dt):
            P.add('dve', lambda e: e.tensor_tensor(out=dt.t[0:nrow, 8:16], in0=ps_ap, in1=dtb_b.t[0:nrow, :], op=ALU.add), reads=ps_keys + dtb_b.all(), writes=dt.all())
            P.add('act', lambda e: e.activation(out=dt.t[0:nrow, 8:16], in_=dt.t[0:nrow, 8:16], func=AF.Exp), reads=dt.all(), writes=dt.all())
            P.add('act', lambda e: e.activation(out=dt.t[0:nrow, 0:8], in_=dt.t[0:nrow, 8:16], func=AF.Ln, bias=1.0, scale=1.0), reads=dt.all(), writes=dt.all())

        def ffn(h2T, ntok, blocks, x1_ap_fn, x1_keys_fn, out_dma_fn, actT, tag, defer_tail=False):
            sgs = ffn_sg[tag]
            HC = NFC // 2
            nb = len(blocks)
            for fh in range(2):
                for ci in range(HC):
                    c = fh * HC + ci
                    wgu = wgu_sl[c % NSL]
                    P.dma('sp', 'wgu%d' % (c % NSL), wgu.t[:, :, :, :], wgu_bf[c], reads=WGU_KEYS, writes=wgu.all())
                    bg, bu = bank(), bank()
                    P.add('pe', [(lambda k, bg=bg, wgu=wgu: lambda e: e.matmul(bg.t[:, 0:ntok], lhsT=wgu.t[:, 0, k, :], rhs=h2T.t[:, k, 0:ntok], start=(k == 0), stop=(k == 7)))(k) for k in range(8)],
                          reads=wgu.all() + h2T.all(), writes=bg.all())
                    P.add('pe', [(lambda k, bu=bu, wgu=wgu: lambda e: e.matmul(bu.t[:, 0:ntok], lhsT=wgu.t[:, 1, k, :], rhs=h2T.t[:, k, 0:ntok], start=(k == 0), stop=(k == 7)))(k) for k in range(8)],
                          reads=wgu.all() + h2T.all(), writes=bu.all())
                    sg = sgs[c % 2]
                    P.add('act', lambda e, bg=bg, sg=sg: e.activation(out=sg.t[:, 0:ntok], in_=bg.t[:, 0:ntok], func=AF.Silu), reads=bg.all(), writes=sg.all())
                    P.add('dve', lambda e, bu=bu, sg=sg, ci=ci: e.tensor_tensor(out=actT.t[:, ci, 0:ntok], in0=sg.t[:, 0:ntok], in1=bu.t[:, 0:ntok], op=ALU.mult),
                          reads=bu.all() + sg.all(), writes=actT.p(ci))
                accs = [[bank(), bank()] for _ in range(nb)]
                for ci in range(HC):
                    c = fh * HC + ci
                    wd = wd_sl[c % NSL]
                    P.dma('sp', 'wd%d' % (c % NSL), wd.t[:, :], wd_bf[c * 128:(c + 1) * 128, :], reads=[('wd_bf', 0)], writes=wd.all())
                    fns = []
                    wr = []
                    for bi, (c0, nrow) in enumerate(blocks):
                        for half in range(2):
                            fns.append((lambda bi, half, c0, nrow, ci=ci, wd=wd, accs=accs: lambda e: e.matmul(accs[bi][half].t[0:nrow, :], lhsT=actT.t[:, ci, c0:c0 + nrow], rhs=wd.t[:, half * 512:(half + 1) * 512], start=(ci == 0), stop=(ci == HC - 1)))(bi, half, c0, nrow))
                            wr += accs[bi][half].all()
                    P.add('pe', fns, reads=wd.all() + actT.p(ci), writes=wr)
                if fh == 0:
                    for bi, (c0, nrow) in enumerate(blocks):
                        x1keys = x1_keys_fn(bi)
                        for half in range(2):
                            xa = x1_ap_fn(bi, half)
                            P.add('dve', lambda e, xa=xa, a=accs[bi][half], nrow=nrow: e.tensor_tensor(out=xa, in0=a.t[0:nrow, :], in1=xa, op=ALU.add),
                                  reads=accs[bi][half].all() + x1keys, writes=x1keys)
            for bi, (c0, nrow) in enumerate(blocks):
                x1keys = x1_keys_fn(bi)
                for half in range(2):
                    xa = x1_ap_fn(bi, half)
                    P.add('dve', lambda e, xa=xa, a=accs[bi][half], nrow=nrow: e.tensor_tensor(out=xa, in0=a.t[0:nrow, :], in1=xa, op=ALU.add),
                          reads=accs[bi][half].all() + x1keys, writes=x1keys)
            if defer_tail:
                P.capture()
            for bi, (c0, nrow) in enumerate(blocks):
                x1keys = x1_keys_fn(bi)
                ss = sm("ssf_" + tag, 4)
                full = x1_ap_fn(bi, None)
                if tag == "m":
                    jk = sgs[0]
                    jk_ap = jk.t[0:nrow, :].bitcast(BF16)
                else:
                    jk = hn_t[0]
                    jk_ap = jk.t[0:nrow, :]
                P.add('act', lambda e, full=full, nrow=nrow, jk_ap=jk_ap: e.activation(out=jk_ap, in_=full, func=AF.Square, accum_out=ss.t[0:nrow, 0:1]),
                      reads=x1keys, writes=jk.all() + ss.all())
                P.add('act', lambda e, nrow=nrow: e.activation(out=ss.t[0:nrow, 1:2], in_=ss.t[0:nrow, 0:1], func=AF.Ln, bias=EPS, scale=1.0 / D), reads=ss.all(), writes=ss.all())
                P.add('act', lambda e, nrow=nrow: e.activation(out=ss.t[0:nrow, 2:3], in_=ss.t[0:nrow, 1:2], func=AF.Exp, scale=-0.5), reads=ss.all(), writes=ss.all())
                P.add('dve', lambda e, full=full, nrow=nrow: e.scalar_tensor_tensor(out=full, in0=full, scalar=ss.t[0:nrow, 2:3], in1=wfb.t[0:nrow, :], op0=ALU.mult, op1=ALU.mult),
                      reads=x1keys + ss.all() + wfb.all(), writes=x1keys)
                out_dma_fn(bi, full, x1keys)
            if defer_tail:
                return P.end_capture()
            return None

        ffn_sg = {}
        hn_t = [None]
        WGU_KEYS = [('wgu_bf', i) for i in range(16)]
        xs_in = sb("xs_in", [SB, D]); h2T_s = sb("h2T_s", [128, 8, SB], BF16); actT_s = sb("actT_s", [128, NFC // 2, SB], BF16, nparts=NFC // 2)
        ffn_sg["s"] = [sb("sg_s%d" % i, [128, SB]) for i in range(2)]
        hn_sf = sb("hn_sf", [SB, D], BF16)
        for nm_, w_ in [("ss_s1", 4), ("ss_s2", 4), ("ssf_s", 4), ("ssg_s", 8), ("ss_m1", 4), ("ss_m2", 4), ("ssf_m", 4), ("ssg_m", 8), ("scal", 64)]:
            sm(nm_, w_)

        import os as _os
        for _i in range(int(_os.environ.get('K_DUMMY', '0'))):
            P.add('dve', lambda e: e.memset(zero.t[:], 0.0), writes=zero.all())
        P.marks['setup'] = len(P.ops)
        sst = ExitStack()
        with sst:
            def ssb(name, shape, dt=F32, nparts=1):
                return sb(name, shape, dt, nparts, stack=sst)
            h0 = [ssb("h0_%d" % i, [128, 16, 128]) for i in range(2)]; otmp = ssb("otmp", [128, 16, 128], nparts=2); kvst = otmp
            Kwb = ssb("Kwb", [128, SB, 128], BF16); Vwb = ssb("Vwb", [128, SB, 128], BF16)
            SelH = Tl("h0_1", h0[1].t, 1); SelHT = ssb("SelHT", [128, 8, SB])
            P.dma('sp', 'xs', xs_in.t[:, :], xsm[:, :], writes=xs_in.all())
            for hf, q in ((0, 'sp'), (1, 'act')):
                P.dma(q, 'kvst%d' % hf, kvst.t[0:127, hf * 8:(hf + 1) * 8, :], ck[hf * 8:(hf + 1) * 8, 1:128, :].rearrange("b p f -> p b f"), writes=kvst.p(hf))
            for hf, q in ((0, 'sp'), (1, 'act')):
                P.dma(q, 'kvsv%d' % hf, h0[0].t[0:127, hf * 8:(hf + 1) * 8, :], cv[hf * 8:(hf + 1) * 8, 1:128, :].rearrange("b p f -> p b f"), writes=h0[0].all())
            SelQ = ssb("SelQ", [SB, 4, 128]); SelQT = ssb("SelQT", [128, 4, SB])
            cbuf = ssb("cbuf", [128, 3, 256]); cw_b = ssb("cw_b", [128, 4, 256]); cb_b = ssb("cb_b", [128, 256])
            accq = ssb("accq", [128, 256]); tmpq = ssb("tmpq", [128, 256])
            P.add('dve', [lambda e: e.memset(cbuf.t[:, :, :], 0.0), lambda e: e.memset(cw_b.t[:, :, :], 0.0), lambda e: e.memset(cb_b.t[:, :], 0.0)],
                  writes=cbuf.all() + cw_b.all() + cb_b.all())
            for q4 in range(4):
                p0, cs0 = q4 * 32, q4 * 256
                P.dma('sp', 'sconv', cbuf.t[p0:p0 + SB, :, :], sconv[:, :, cs0:cs0 + 256], writes=cbuf.all())
                P.dma('sp', 'sconv', cw_b.t[p0:p0 + SB, :, :], bass.AP(conv_w.tensor, cs0, [[0, SB], [1024, 4], [1, 256]]), writes=cw_b.all())
                P.dma('sp', 'sconv', cb_b.t[p0:p0 + SB, :], conv_b[0:1, cs0:cs0 + 256].partition_broadcast(SB), writes=cb_b.all())
            antiI = ssb("antiI", [128, 128])
            cst(antiI, [lambda e: e.memset(antiI.t[:], 1.0),
                        lambda e: e.affine_select(out=antiI.t[:], in_=antiI.t[:], pattern=[[1, 128]], compare_op=ALU.is_equal, fill=0.0, base=-127, channel_multiplier=1)])
            cst(SelH, [lambda e: e.memset(SelH.t[0:SB, 0:8, :], 1.0),
                       lambda e: e.affine_select(out=SelH.t[0:SB, 0:8, :], in_=SelH.t[0:SB, 0:8, :], pattern=[[-1, 8], [1, 128]], compare_op=ALU.is_equal, fill=0.0, base=0, channel_multiplier=-8)])
            cst(SelHT, [lambda e: e.memset(SelHT.t[:, :, :], 1.0),
                       lambda e: e.affine_select(out=SelHT.t[:, :, :], in_=SelHT.t[:, :, :], pattern=[[-1, 8], [-8, SB]], compare_op=ALU.is_equal, fill=0.0, base=0, channel_multiplier=1)])
            cst(SelQ, [lambda e: e.memset(SelQ.t[:, :, :], 1.0),
                       lambda e: e.affine_select(out=SelQ.t[:, :, :], in_=SelQ.t[:, :, :], pattern=[[-32, 4], [1, 128]], compare_op=ALU.is_equal, fill=0.0, base=0, channel_multiplier=-1)])
            cst(SelQT, [lambda e: e.memset(SelQT.t[:, :, :], 1.0),
                       lambda e: e.affine_select(out=SelQT.t[:, :, :], in_=SelQT.t[:, :, :], pattern=[[-32, 4], [-1, SB]], compare_op=ALU.is_equal, fill=0.0, base=0, channel_multiplier=1)])
            sb_saved = sb
            sb = lambda name, shape, dt=F32, nparts=1, stack=sst: sb_saved(name, shape, dt, nparts, stack=sst)
            negf = sb("negf", [128, 128])
            cst(negf, [lambda e: e.memset(negf.t[:], 0.0),
                       lambda e: e.affine_select(out=negf.t[:], in_=negf.t[:], pattern=[[1, 128]], compare_op=ALU.is_ge, fill=NEG, base=0, channel_multiplier=-1)])
            P.add('dve', lambda e: e.tensor_copy(out=negm.t[:], in_=bc(negf.t[:, :].unsqueeze(1), [128, 4, 128])), reads=negf.all(), writes=negm.all())
            oh_sb = sb("oh_sb", [32, 128]); rb_sb = sb("rb_sb", [32, 8]); eb = sb("eb", [128, 8])
            P.dma('sp', 'cst', oh_sb.t[:], onehot_d[:, :], writes=oh_sb.all())
            P.dma('sp', 'cst', rb_sb.t[:], rel_bias[:, :], writes=rb_sb.all())
            b0 = bank()
            P.add('pe', lambda e: e.matmul(b0.t[:, 0:8], lhsT=oh_sb.t[:, :], rhs=rb_sb.t[:, :], start=True, stop=True), reads=oh_sb.all() + rb_sb.all(), writes=b0.all())
            P.add('act', lambda e: e.activation(out=eb.t[:], in_=b0.t[:, 0:8], func=AF.Exp), reads=b0.all(), writes=eb.all())
            b1 = bank()
            P.add('pe', lambda e: e.matmul(b1.t[:, 0:8], lhsT=antiI.t[:, :], rhs=eb.t[:, :], start=True, stop=True), reads=antiI.all() + eb.all(), writes=b1.all())
            P.add('dve', lambda e: e.tensor_copy(out=expBs.t[:], in_=b1.t[:, 0:8]), reads=b1.all(), writes=expBs.all())
            P.dma('sp', 'scrS', scrS[0:127, :], zero.t[0:127, :], reads=zero.all(), writes=[('scrS', 0)])
            P.dma('sp', 'scrS', scrS[255:383, :], zero.t[:, :], reads=zero.all(), writes=[('scrS', 0)])
            P.dma('sp', 'scrS', scrS[127:255, :], eb.t[:, :], reads=eb.all(), writes=[('scrS', 0)])
            sb = sb_saved
            hn_s = ssb("hn_s", [SB, D], BF16)
            hn_t[0] = hn_s
            hT_s = ssb("hT_s", [128, 8, SB], BF16)
            proj = ssb("proj_s", [SB, DIN]); projb = ssb("projb_s", [SB, 768], BF16)
            KT_s = ssb("KT_s", [64, SB * 2, 128], BF16); qT2 = ssb("qT2", [64, 8, SB], BF16)
            Es = ssb("Es", [128, 128]); Esb = ssb("Esb", [128, SB, 8], BF16)
            rec_s = ssb("rec_s", [64, 128]); aT_s = ssb("aT_s", [64, 8, SB], BF16)
            xc_s = ssb("xc_s", [SB, 1024]); tmp_s = ssb("tmp_s", [SB, 512])
            dt_s = ssb("dt_s", [128, 16]); dec_s = ssb("dec_s", [SB, 8]); xdt_s = ssb("xdt_s", [SB, 512])
            xdt_bh = ssb("xdt_bh", [128, 64]); dec_bh = ssb("dec_bh", [128, 1]); B_bh = ssb("B_bh", [128, 128]); C_bh = ssb("C_bh", [128, 128])
            y_bh = ssb("y_bh", [128, 64]); y_tok = ssb("y_tok", [SB, 512]); sz_s = ssb("sz_s", [SB, 512])
            yn_s = ssb("yn_s", [SB, 512], BF16); yT_s = ssb("yT_s", [128, 4, SB], BF16)
            x1_s = xs_in

            rmsnorm_T(xs_in.t[:, :], xs_in.all(), SB, None, hn_s, hT_s.t[:, :, :], hT_s.all(), "s1")
            colr = [(0, 512), (512, 1024), (1024, 1536), (1536, 2048), (2048, DIN)]
            for (c0, c1) in colr:
                bk = bank()
                P.add('pe', [(lambda k, bk=bk, c0=c0, c1=c1: lambda e: e.matmul(bk.t[0:SB, 0:c1 - c0], lhsT=hT_s.t[:, k, :], rhs=w_in_sb.t[:, k, c0:c1], start=(k == 0), stop=(k == 7)))(k) for k in range(8)],
                      reads=hT_s.all() + w_in_sb.all(), writes=bk.all())
                P.add('act', lambda e, bk=bk, c0=c0, c1=c1: e.activation(out=proj.t[:, c0:c1], in_=bk.t[0:SB, 0:c1 - c0], func=AF.Copy), reads=bk.all(), writes=proj.all())
            P.add('dve', lambda e: e.tensor_copy(out=projb.t[:, :], in_=proj.t[:, 0:768]), reads=proj.all(), writes=projb.all())
            P.dma('sp', 'skv0', scrKV[:, :], projb.t[:, 512:768], reads=projb.all(), writes=[('scrKV', 0)])
            P.dma('sp', 'skv', Kwb.t[127:128, :, :], scrKV[:, 0:128].rearrange("(o b) f -> o b f", o=1), reads=[('scrKV', 0)], writes=Kwb.all())
            P.dma('sp', 'skv', Vwb.t[127:128, :, :], scrKV[:, 128:256].rearrange("(o b) f -> o b f", o=1), reads=[('scrKV', 0)], writes=Vwb.all())
            P.dma('sp', 'sout', nk_s[:, 0:127, :], ck[:, 1:128, :])
            P.dma('sp', 'sout', nv_s[:, 0:127, :], cv[:, 1:128, :])
            P.dma('sp', 'sout', nk_s[:, 127, :], proj.t[:, 512:640], reads=proj.all())
            P.dma('sp', 'sout', nv_s[:, 127, :], proj.t[:, 640:768], reads=proj.all())
            P.dma('sp', 'sout', ncv_s[:, 0:2, :], sconv[:, 1:3, :])
            P.dma('sp', 'sout', ncv_s[:, 2, :], proj.t[:, 1280:2304], reads=proj.all())
            for hf in range(2):
                P.add('pool', lambda e, hf=hf: e.tensor_copy(out=Kwb.t[0:127, hf * 8:(hf + 1) * 8, :], in_=kvst.t[0:127, hf * 8:(hf + 1) * 8, :]), reads=kvst.p(hf), writes=Kwb.all())
            for hf in range(2):
                P.add('pool', lambda e, hf=hf: e.tensor_copy(out=Vwb.t[0:127, hf * 8:(hf + 1) * 8, :], in_=h0[0].t[0:127, hf * 8:(hf + 1) * 8, :]), reads=h0[0].all(), writes=Vwb.all())
            for gu, wsrc in enumerate((w_gate, w_up)):
                for k in range(8):
                    P.dma('pool', 'wcast', wgu_bf[:, :, gu, k, :].rearrange("c p j -> p c j"), wsrc[k * 128:(k + 1) * 128, :].rearrange("p (c j) -> p c j", j=128), writes=[('wgu_bf', gu * 8 + k)])
            P.dma('pool', 'wcast', wd_bf[:, :], w_down[:, :], writes=[('wd_bf', 0)])


            Tp = h0[0]
            P.dma('sp', 'expB', Tp.t[:, :, :].rearrange("p a n -> p (a n)"), bass.AP(scrS.tensor, 0, [[8, 128], [1, 2048]]), reads=[('scrS', 0)], writes=Tp.all())
            for q4 in range(4):
                bkx = bank()
                P.add('pe', lambda e, bkx=bkx, q4=q4: e.matmul(bkx.t[:, :], lhsT=antiI.t[:, :], rhs=Tp.t[:, :, :].rearrange("p a n -> p (a n)")[:, q4 * 512:(q4 + 1) * 512], start=True, stop=True), reads=antiI.all() + Tp.all(), writes=bkx.all())
                P.add('act', lambda e, bkx=bkx, q4=q4: e.activation(out=expB.t[:, :, :, :].rearrange("p a q h -> p (a q h)")[:, q4 * 512:(q4 + 1) * 512], in_=bkx.t[:, :], func=AF.Copy), reads=bkx.all(), writes=expB.all())
            bk = bank(); pv = bk.t[:].bitcast(BF16)
            P.add('pe', [(lambda hq, pv=pv: lambda e: e.transpose(out=pv[0:64, hq * SB:(hq + 1) * SB], in_=projb.t[:, hq * 64:(hq + 1) * 64], identity=identb.t[0:SB, 0:SB]))(hq) for hq in range(8)],
                  reads=projb.all() + identb.all(), writes=bk.all())
            P.add('act', lambda e, pv=pv: e.activation(out=qT2.t[:, :, :], in_=pv[0:64, 0:8 * SB].rearrange("p (h b) -> p h b", h=8), func=AF.Copy), reads=bk.all(), writes=qT2.all())
            for g4 in range(4):
                bk = bank(); pv = bk.t[:].bitcast(BF16)
                P.add('pe', [(lambda i, pv=pv, g4=g4: lambda e: e.transpose(out=pv[0:64, i * 128:(i + 1) * 128], in_=Kwb.t[:, (g4 * 8 + i) // 2, ((g4 * 8 + i) % 2) * 64:((g4 * 8 + i) % 2) * 64 + 64], identity=identb.t[:, :]))(i) for i in range(8)],
                      reads=Kwb.all() + identb.all(), writes=bk.all())
                P.add('dve', lambda e, pv=pv, g4=g4: e.tensor_copy(out=KT_s.t[:, g4 * 8:(g4 + 1) * 8, :], in_=pv[0:64, :].rearrange("p (i t) -> p i t", i=8)), reads=bk.all(), writes=KT_s.all())
            bsc = bank()
            P.add('pe', [(lambda b, h: lambda e: e.matmul(bsc.t[:, b * 8 + h * 4:b * 8 + h * 4 + 4], lhsT=KT_s.t[:, b * 2 + h, :], rhs=qT2.t[:, h * 4:(h + 1) * 4, b], start=True, stop=True))(b, h) for b in range(SB) for h in range(2)],
                  reads=KT_s.all() + qT2.all(), writes=bsc.all())
            P.add('act', lambda e: e.activation(out=Es.t[:, :], in_=bsc.t[:, 0:128], func=AF.Exp, scale=SCALE), reads=bsc.all(), writes=Es.all())
            P.add('dve', lambda e: e.tensor_tensor(out=Esb.t[:, :, :], in0=Es.t[:, :].rearrange("p (b h) -> p b h", b=SB), in1=bc(expBs.t[:, :].unsqueeze(1), [128, SB, 8]), op=ALU.mult),
                  reads=Es.all() + expBs.all(), writes=Esb.all())
            bden = bank(); bos = bank()
            P.add('pe', lambda e: e.matmul(bden.t[0:64, 0:128], lhsT=onesb.t[:, :], rhs=Esb.t[:, :, :].rearrange("p b h -> p (b h)"), start=True, stop=True), reads=onesb.all() + Esb.all(), writes=bden.all())
            P.add('pe', [(lambda b, h: lambda e: e.matmul(bos.t[0:64, b * 8 + h * 4:b * 8 + h * 4 + 4], lhsT=Vwb.t[:, b, h * 64:(h + 1) * 64], rhs=Esb.t[:, b, h * 4:(h + 1) * 4], start=True, stop=True))(b, h) for b in range(SB) for h in range(2)],
                  reads=Vwb.all() + Esb.all(), writes=bos.all())
            P.add('dve', lambda e: e.tensor_tensor(out=rec_s.t[:, :].rearrange("p (b h) -> p b h", b=SB), in0=bden.t[0:64, 0:128].rearrange("p (b h) -> p b h", b=SB), in1=bc(esink.t[0:64, :].unsqueeze(1), [64, SB, 8]), op=ALU.add),
                  reads=bden.all() + esink.all(), writes=rec_s.all())
            P.add('dve', lambda e: e.reciprocal(out=rec_s.t[:, :], in_=rec_s.t[:, :]), reads=rec_s.all(), writes=rec_s.all())
            P.add('dve', lambda e: e.tensor_tensor(out=aT_s.t[:, :, :].rearrange("p h b -> p b h"), in0=bos.t[0:64, 0:128].rearrange("p (b h) -> p b h", b=SB), in1=rec_s.t[:, :].rearrange("p (b h) -> p b h", b=SB), op=ALU.mult),
                  reads=bos.all() + rec_s.all(), writes=aT_s.all())
            bq = bank()
            P.add('pe', [(lambda q4: lambda e: e.matmul(bq.t[:, 0:256], lhsT=SelQ.t[:, q4, :], rhs=proj.t[:, 1280 + q4 * 256:1280 + (q4 + 1) * 256], start=(q4 == 0), stop=(q4 == 3)))(q4) for q4 in range(4)],
                  reads=SelQ.all() + proj.all(), writes=bq.all())
            P.add('dve', lambda e: e.tensor_tensor(out=accq.t[:, :], in0=bq.t[:, 0:256], in1=cw_b.t[:, 3, :], op=ALU.mult), reads=bq.all() + cw_b.all(), writes=accq.all())
            P.add('dve', lambda e: e.tensor_tensor(out=accq.t[:, :], in0=accq.t[:, :], in1=cb_b.t[:, :], op=ALU.add), reads=accq.all() + cb_b.all(), writes=accq.all())
            for j in range(3):
                P.add('dve', lambda e, j=j: e.tensor_tensor(out=tmpq.t[:, :], in0=cbuf.t[:, j, :], in1=cw_b.t[:, j, :], op=ALU.mult), reads=cbuf.all() + cw_b.all(), writes=tmpq.all())
                P.add('dve', lambda e: e.tensor_tensor(out=accq.t[:, :], in0=accq.t[:, :], in1=tmpq.t[:, :], op=ALU.add), reads=accq.all() + tmpq.all(), writes=accq.all())
            P.add('act', lambda e: e.activation(out=tmpq.t[:, :], in_=accq.t[:, :], func=AF.Silu), reads=accq.all(), writes=tmpq.all())
            bq2 = [bank(), bank()]
            P.add('pe', [(lambda q4: lambda e: e.matmul(bq2[q4 // 2].t[0:SB, (q4 % 2) * 256:(q4 % 2 + 1) * 256], lhsT=SelQT.t[:, q4, :], rhs=tmpq.t[:, :], start=True, stop=True))(q4) for q4 in range(4)],
                  reads=SelQT.all() + tmpq.all(), writes=bq2[0].all() + bq2[1].all())
            P.add('act', [(lambda i: lambda e: e.activation(out=xc_s.t[:, i * 512:(i + 1) * 512], in_=bq2[i].t[0:SB, :], func=AF.Copy))(i) for i in range(2)], reads=bq2[0].all() + bq2[1].all(), writes=xc_s.all())
            softplus_dt(proj.t[:, 2304:2312], proj.all(), SB, dt_s)
            P.add('dve', lambda e: e.tensor_tensor(out=dec_s.t[:, :], in0=dt_s.t[0:SB, 0:8], in1=A_b.t[0:SB, :], op=ALU.mult), reads=dt_s.all() + A_b.all(), writes=dec_s.all())
            P.add('act', lambda e: e.activation(out=dec_s.t[:, :], in_=dec_s.t[:, :], func=AF.Exp), reads=dec_s.all(), writes=dec_s.all())
            P.add('dve', lambda e: e.tensor_tensor(out=xdt_s.t[:, :].rearrange("p (h d) -> p h d", h=8), in0=xc_s.t[:, 0:512].rearrange("p (h d) -> p h d", h=8), in1=bc(dt_s.t[0:SB, 0:8].unsqueeze(2), [SB, 8, 64]), op=ALU.mult),
                  reads=xc_s.all() + dt_s.all(), writes=xdt_s.all())
            bsel = bank(); bsel2 = bank()
            P.add('pe', [(lambda h: lambda e: e.matmul(bsel.t[:, 0:64], lhsT=SelH.t[0:SB, h, :], rhs=xdt_s.t[:, h * 64:(h + 1) * 64], start=(h == 0), stop=(h == 7)))(h) for h in range(8)]
                  + [(lambda h: lambda e: e.matmul(bsel.t[:, 64:65], lhsT=SelH.t[0:SB, h, :], rhs=dec_s.t[:, h:h + 1], start=(h == 0), stop=(h == 7)))(h) for h in range(8)],
                  reads=SelH.all() + xdt_s.all() + dec_s.all(), writes=bsel.all())
            P.add('pe', [(lambda h: lambda e: e.matmul(bsel2.t[:, 0:128], lhsT=SelH.t[0:SB, h, :], rhs=xc_s.t[:, 512 + (h // 4) * 128:512 + (h // 4 + 1) * 128], start=(h == 0), stop=(h == 7)))(h) for h in range(8)]
                  + [(lambda h: lambda e: e.matmul(bsel2.t[:, 128:256], lhsT=SelH.t[0:SB, h, :], rhs=xc_s.t[:, 768 + (h // 4) * 128:768 + (h // 4 + 1) * 128], start=(h == 0), stop=(h == 7)))(h) for h in range(8)],
                  reads=SelH.all() + xc_s.all(), writes=bsel2.all())
            P.add('dve', [lambda e: e.tensor_copy(out=xdt_bh.t[:, :], in_=bsel.t[:, 0:64]), lambda e: e.tensor_copy(out=dec_bh.t[:, :], in_=bsel.t[:, 64:65])], reads=bsel.all(), writes=xdt_bh.all() + dec_bh.all())
            P.add('dve', [lambda e: e.tensor_copy(out=B_bh.t[:, :], in_=bsel2.t[:, 0:128]), lambda e: e.tensor_copy(out=C_bh.t[:, :], in_=bsel2.t[:, 128:256])], reads=bsel2.all(), writes=B_bh.all() + C_bh.all())
            for hf in range(4):
                hh = h0[hf % 2]; sse = 'dve'
                P.dma('sp', 'sh0_%d' % (hf % 2), hh.t[:, :, :].rearrange("p a n -> p (a n)"), sssm[:, hf * 2048:(hf + 1) * 2048], writes=hh.all())
                P.add(sse, lambda e, hf=hf: e.tensor_tensor(out=otmp.t[:, :, :], in0=bc(B_bh.t[:, :].unsqueeze(1), [128, 16, 128]), in1=bc(xdt_bh.t[:, hf * 16:(hf + 1) * 16].unsqueeze(2), [128, 16, 128]), op=ALU.mult),
                      reads=B_bh.all() + xdt_bh.all(), writes=otmp.all())
                P.add('dve', lambda e, hh=hh: e.scalar_tensor_tensor(out=hh.t[:, :, :], in0=hh.t[:, :, :], scalar=dec_bh.t[:, 0:1], in1=otmp.t[:, :, :], op0=ALU.mult, op1=ALU.add),
                      reads=hh.all() + dec_bh.all() + otmp.all(), writes=hh.all())
                P.dma('sp', 'snss%d' % (hf % 2), nss_s[:, hf * 2048:(hf + 1) * 2048], hh.t[:, :, :].rearrange("p a n -> p (a n)"), reads=hh.all())
                P.add(sse, lambda e, hh=hh: e.tensor_tensor(out=otmp.t[:, :, :], in0=hh.t[:, :, :], in1=bc(C_bh.t[:, :].unsqueeze(1), [128, 16, 128]), op=ALU.mult),
                      reads=hh.all() + C_bh.all(), writes=otmp.all())
                P.add('dve', lambda e, hf=hf: e.tensor_reduce(out=y_bh.t[:, hf * 16:(hf + 1) * 16], in_=otmp.t[:, :, :], axis=AX.X, op=ALU.add), reads=otmp.all(), writes=y_bh.all())
            bsel3 = bank()
            P.add('pe', [(lambda h: lambda e: e.matmul(bsel3.t[0:SB, h * 64:(h + 1) * 64], lhsT=SelHT.t[:, h, :], rhs=y_bh.t[:, :], start=True, stop=True))(h) for h in range(8)],
                  reads=SelHT.all() + y_bh.all(), writes=bsel3.all())
            P.add('dve', lambda e: e.tensor_copy(out=y_tok.t[:, :], in_=bsel3.t[0:SB, :]), reads=bsel3.all(), writes=y_tok.all())
            P.add('dve', lambda e: e.tensor_tensor(out=tmp_s.t[:, :].rearrange("p (h d) -> p h d", h=8), in0=xc_s.t[:, 0:512].rearrange("p (h d) -> p h d", h=8), in1=bc(D_b.t[0:SB, :].unsqueeze(2), [SB, 8, 64]), op=ALU.mult),
                  reads=xc_s.all() + D_b.all(), writes=tmp_s.all())
            P.add('dve', lambda e: e.tensor_tensor(out=y_tok.t[:, :], in0=y_tok.t[:, :], in1=tmp_s.t[:, :], op=ALU.add), reads=y_tok.all() + tmp_s.all(), writes=y_tok.all())
            P.add('act', lambda e: e.activation(out=sz_s.t[:, :], in_=proj.t[:, 768:1280], func=AF.Silu), reads=proj.all(), writes=sz_s.all())
            P.add('dve', lambda e: e.tensor_tensor(out=y_tok.t[:, :], in0=y_tok.t[:, :], in1=sz_s.t[:, :], op=ALU.mult), reads=y_tok.all() + sz_s.all(), writes=y_tok.all())

            def group_norm(y, nrow, yn, ssg):
                P.add('act', [(lambda g: lambda e: e.activation(out=yn.t[0:nrow, g * 256:(g + 1) * 256], in_=y.t[0:nrow, g * 256:(g + 1) * 256], func=AF.Square, accum_out=ssg.t[0:nrow, g:g + 1]))(g) for g in range(2)],
                      reads=y.all(), writes=yn.all() + ssg.all())
                P.add('act', lambda e: e.activation(out=ssg.t[0:nrow, 2:4], in_=ssg.t[0:nrow, 0:2], func=AF.Ln, bias=EPS, scale=1.0 / 256), reads=ssg.all(), writes=ssg.all())
                P.add('act', lambda e: e.activation(out=ssg.t[0:nrow, 4:6], in_=ssg.t[0:nrow, 2:4], func=AF.Exp, scale=-0.5), reads=ssg.all(), writes=ssg.all())
                P.add('dve', [(lambda g: lambda e: e.tensor_scalar(out=yn.t[0:nrow, g * 256:(g + 1) * 256], in0=y.t[0:nrow, g * 256:(g + 1) * 256], scalar1=ssg.t[0:nrow, 4 + g:5 + g], scalar2=None, op0=ALU.mult))(g) for g in range(2)],
                      reads=y.all() + ssg.all(), writes=yn.all())
            group_norm(y_tok, SB, yn_s, sm("ssg_s"))
            bk = bank(); pv = bk.t[:].bitcast(BF16)
            P.add('pe', [(lambda c, pv=pv: lambda e: e.transpose(out=pv[:, c * SB:(c + 1) * SB], in_=yn_s.t[:, c * 128:(c + 1) * 128], identity=identb.t[0:SB, 0:SB]))(c) for c in range(4)],
                  reads=yn_s.all() + identb.all(), writes=bk.all())
            P.add('act', lambda e, pv=pv: e.activation(out=yT_s.t[:, :, :], in_=pv[:, 0:4 * SB].rearrange("p (c b) -> p c b", c=4), func=AF.Copy), reads=bk.all(), writes=yT_s.all())

            def out_proj(aT, yT, nrow, xin_ap, xin_keys, x1_ap, x1_keys):
                for half in range(2):
                    bk = bank()
                    fns = [(lambda hq, bk=bk, half=half: lambda e: e.matmul(bk.t[0:nrow, :], lhsT=aT.t[:, hq, 0:nrow], rhs=w_out_a.t[:, hq, half * 512:(half + 1) * 512], start=(hq == 0), stop=False))(hq) for hq in range(8)]
                    fns += [(lambda c, bk=bk, half=half: lambda e: e.matmul(bk.t[0:nrow, :], lhsT=yT.t[:, c, 0:nrow], rhs=w_out_s.t[:, c, half * 512:(half + 1) * 512], start=False, stop=(c == 3)))(c) for c in range(4)]
                    P.add('pe', fns, reads=aT.all() + yT.all() + w_out_a.all() + w_out_s.all(), writes=bk.all())
                    P.add('dve', lambda e, bk=bk, half=half: e.tensor_tensor(out=x1_ap[:, half * 512:(half + 1) * 512], in0=bk.t[0:nrow, :], in1=xin_ap[:, half * 512:(half + 1) * 512], op=ALU.add),
                          reads=bk.all() + xin_keys, writes=x1_keys)
            out_proj(aT_s, yT_s, SB, xs_in.t[:, :], xs_in.all(), x1_s.t[:, :], x1_s.all())
            rmsnorm_T(x1_s.t[:, :], x1_s.all(), SB, w2b, hn_s, h2T_s.t[:, :, :], h2T_s.all(), "s2")

            P.barrier()
        P.marks['sample'] = len(P.ops)

        xin = [sb("xin%d" % i, [128, D]) for i in range(2)]
        hn = sb("hn", [128, D], BF16); hn_t[0] = hn
        hT = sb("hT", [128, 8, 128], BF16)
        qTs = [sb("qT%d" % i, [128, 4, 128], BF16) for i in range(2)]
        Kh = [sb("Kh%d" % i, [128, 2, 128], BF16) for i in range(3)]
        Vh = [sb("Vh%d" % i, [128, 2, 128], BF16) for i in range(3)]
        xs_tok = sb("xs_tok", [128, 512], BF16)
        xbcT = sb("xbcT", [128, 8, 131], BF16)
        xcT = sb("xcT", [128, 8, 128], BF16, nparts=8)
        Btok = sb("Btok", [128, 256], BF16)
        szts = [sb("szt%d" % i, [128, 512]) for i in range(2)]; dtts = [sb("dtt%d" % i, [128, 16]) for i in range(2)]
        scals = [sm("scal", 64), sb("scal1", [128, 64])]
        Lt = sb("Lt", [128, 8, 128]); MT = sb("MT", [128, 8, 128], BF16)
        xdt = sb("xdt", [128, 512], BF16); xdte = sb("xdte", [128, 512], BF16)
        yt = sb("yt", [128, 512]); ytmp = sb("ytmp", [128, 512]); yn = sb("yn", [128, 512], BF16); yT = sb("yT", [128, 4, 128], BF16)
        Eraw = sb("Eraw", [128, 4, 128]); ET = [sb("ET%d" % i, [128, 8, 128], BF16) for i in range(2)]
        rec = sb("rec", [64, 1, 512]); aT = sb("aT", [64, 8, 128], BF16)
        x1 = sb("x1", [128, GB, D], nparts=GB); h2T = sb("h2T", [128, 8, GB * 128], BF16, nparts=GB)
        actT = sb("actT", [128, NFC // 2, GB * 128], BF16, nparts=NFC // 2)
        kvout = ytmp; ncv = sb("ncv", [128, 8, 3]); ncvT = yt
        nss_sb = Lt
        ffn_sg["m"] = [yt, ytmp]

        P.add('dve', lambda e: e.memset(stateT.t[:, :], 0.0), writes=stateT.all())
        P.add('dve', lambda e: e.memset(stateTb.t[:, :], 0.0), writes=stateTb.all())
        P.add('dve', lambda e: e.memset(xbcT.t[:, :, :], 0.0), writes=xbcT.all())
        for i in range(3):
            P.add('dve', lambda e, i=i: e.memset(Vh[i].t[:, :, 64:128], 1.0), writes=Vh[i].all())

        blk_ctr = [0]
        preloaded = set()

        def preload_x(xsrc, t):
            gi = blk_ctr[0]
            preloaded.add(gi)
            P.dma('sp', 'xin%d' % (gi % 2), xin[gi % 2].t[:, :], xsrc[t * 128:(t + 1) * 128, :], writes=xin[gi % 2].all())

        def block(xsrc, t, mode):
            main = mode == 'main'
            gi = blk_ctr[0]
            blk_ctr[0] += 1
            bank_mode[0] = gi % 2
            bank_sub[0] = None
            xt = xin[gi % 2]
            qT = qTs[gi % 2]; szt = szts[gi % 2]; dtt = dtts[gi % 2]; scal = scals[gi % 2]
            pe2 = 'pool' if main else 'dve'
            if gi not in preloaded:
                P.dma('sp', 'xin%d' % (gi % 2), xt.t[:, :], xsrc[t * 128:(t + 1) * 128, :], writes=xt.all())
            rmsnorm_T(xt.t[:, :], xt.all(), 128, None, hn, hT.t[:, :, :], hT.all(), "m1", cp='act' if main else 'dve')
            chunks = []
            if main:
                chunks += [(c * 128, 'q', c) for c in range(4)]
            if main or mode == 'prelast':
                chunks += [(DIN + h * 128, 'k', h) for h in range(2)]
            nx = 8 if (main or mode == 'prelast') else 6
            chunks += [(1280 + c * 128, 'x', c) for c in range(nx)]
            slot = gi % 3
            i = 0
            while i < len(chunks):
                grp = chunks[i:i + 4]
                i += 4
                bk = bank()
                fns = []
                for gi_, (c0, kind, idx) in enumerate(grp):
                    fns += [(lambda k, gi_=gi_, c0=c0, bk=bk: lambda e: e.matmul(bk.t[:, gi_ * 128:(gi_ + 1) * 128], lhsT=w_in_sb.t[:, k, c0:c0 + 128], rhs=hT.t[:, k, :], start=(k == 0), stop=(k == 7)))(k) for k in range(8)]
                P.add('pe', fns, reads=w_in_sb.all() + hT.all(), writes=bk.all())
                j = 0
                while j < len(grp):
                    kind = grp[j][1]
                    j2 = j
                    while j2 < len(grp) and grp[j2][1] == kind:
                        j2 += 1
                    i0 = grp[j][2]
                    nn = j2 - j
                    src = bk.t[:, j * 128:j2 * 128].rearrange("p (c t) -> p c t", c=nn)
                    if kind == 'q':
                        P.add('act', lambda e, src=src, i0=i0, nn=nn: e.activation(out=qT.t[:, i0:i0 + nn, :], in_=src, func=AF.Copy), reads=bk.all(), writes=qT.all())
                    elif kind == 'k':
                        P.add('dve', lambda e, src=src, i0=i0, nn=nn: e.tensor_copy(out=Kh[slot].t[:, i0:i0 + nn, :], in_=src), reads=bk.all(), writes=Kh[slot].all())
                    elif main:
                        P.add('act', lambda e, src=src, i0=i0, nn=nn: e.activation(out=xbcT.t[:, i0:i0 + nn, 3:131], in_=src, func=AF.Copy), reads=bk.all(), writes=xbcT.all())
                        if t == NBLK - 1:
                            P.add('dve', lambda e, src=src, i0=i0, nn=nn: e.tensor_copy(out=ncv.t[:, i0:i0 + nn, :], in_=src[:, :, 125:128]), reads=bk.all(), writes=ncv.all())
                    else:
                        P.add('dve', lambda e, src=src, i0=i0, nn=nn: e.tensor_copy(out=xbcT.t[:, i0:i0 + nn, 3:131], in_=src), reads=bk.all(), writes=xbcT.all())
                    j = j2
            bdt = None
            if main:
                bA, bB = bank(), bank()
                P.add('pe', [(lambda k: lambda e: e.matmul(bA.t[:, :], lhsT=hT.t[:, k, :], rhs=w_in_sb.t[:, k, 640:1152], start=(k == 0), stop=(k == 7)))(k) for k in range(8)],
                      reads=w_in_sb.all() + hT.all(), writes=bA.all())
                P.add('pe', [(lambda k: lambda e: e.matmul(bB.t[:, 0:128], lhsT=hT.t[:, k, :], rhs=w_in_sb.t[:, k, 1152:1280], start=(k == 0), stop=(k == 7)))(k) for k in range(8)]
                      + [(lambda k: lambda e: e.matmul(bB.t[:, 128:136], lhsT=hT.t[:, k, :], rhs=w_in_sb.t[:, k, 2304:2312], start=(k == 0), stop=(k == 7)))(k) for k in range(8)],
                      reads=w_in_sb.all() + hT.all(), writes=bB.all())
                P.add('dve', lambda e: e.tensor_copy(out=Vh[slot].t[:, :, 0:64], in_=bA.t[:, 0:128].rearrange("p (h d) -> p h d", h=2)), reads=bA.all(), writes=Vh[slot].all())
                P.add('act', lambda e: e.activation(out=szt.t[:, 0:384], in_=bA.t[:, 128:512], func=AF.Silu), reads=bA.all(), writes=szt.all())
                P.add('act', lambda e: e.activation(out=szt.t[:, 384:512], in_=bB.t[:, 0:128], func=AF.Silu), reads=bB.all(), writes=szt.all())
                softplus_dt(bB.t[:, 128:136], bB.all(), 128, dtt)
                if t == NBLK - 1:
                    bKt = bank()
                    P.add('pe', [(lambda k: lambda e: e.matmul(bKt.t[:, 0:128], lhsT=hT.t[:, k, :], rhs=w_in_sb.t[:, k, 512:640], start=(k == 0), stop=(k == 7)))(k) for k in range(8)],
                          reads=w_in_sb.all() + hT.all(), writes=bKt.all())
                    P.add('dve', lambda e: e.tensor_copy(out=kvout.t[:, 0:128], in_=bKt.t[:, 0:128]), reads=bKt.all(), writes=kvout.all())
                    P.add('dve', lambda e: e.tensor_copy(out=kvout.t[:, 128:256], in_=bA.t[:, 0:128]), reads=bA.all(), writes=kvout.all())
                    P.dma('sp', 'pout', nk_p[:, :], kvout.t[:, 0:128], reads=kvout.all())
                    P.dma('sp', 'pout', nv_p[:, :], kvout.t[:, 128:256], reads=kvout.all())
            else:
                bB = bank()
                fns = [(lambda k: lambda e: e.matmul(bB.t[:, 128:136], lhsT=hT.t[:, k, :], rhs=w_in_sb.t[:, k, 2304:2312], start=(k == 0), stop=(k == 7)))(k) for k in range(8)]
                if mode == 'prelast':
                    fns += [(lambda k: lambda e: e.matmul(bB.t[:, 0:128], lhsT=hT.t[:, k, :], rhs=w_in_sb.t[:, k, 640:768], start=(k == 0), stop=(k == 7)))(k) for k in range(8)]
                P.add('pe', fns, reads=w_in_sb.all() + hT.all(), writes=bB.all())
                if mode == 'prelast':
                    P.add('dve', lambda e: e.tensor_copy(out=Vh[slot].t[:, :, 0:64], in_=bB.t[:, 0:128].rearrange("p (h d) -> p h d", h=2)), reads=bB.all(), writes=Vh[slot].all())
                softplus_dt(bB.t[:, 128:136], bB.all(), 128, dtt)
            if main:
                P.capture()
                bank_sub[0] = 0
            for c4 in range(0, nx, 4):
                bk = bank()
                ncc = min(4, nx - c4)
                fns = []
                for ci in range(ncc):
                    c = c4 + ci
                    fns += [(lambda j, c=c, ci=ci, bk=bk: lambda e: e.matmul(bk.t[:, ci * 128:(ci + 1) * 128], lhsT=convdiag.t[:, c, j, :], rhs=xbcT.t[:, c, j:j + 128], start=(j == 0), stop=(j == 3)))(j) for j in range(4)]
                P.add('pe', fns, reads=convdiag.all() + xbcT.all(), writes=bk.all())
                P.add('act', [(lambda ci, bk=bk, c4=c4: lambda e: e.activation(out=xcT.t[:, c4 + ci, :], in_=bk.t[:, ci * 128:(ci + 1) * 128], func=AF.Silu, bias=cbT.t[:, c4 + ci:c4 + ci + 1]))(ci) for ci in range(ncc)],
                      reads=bk.all() + cbT.all(), writes=xcT.p(*range(c4, c4 + ncc)))
            P.add(pe2, lambda e: e.tensor_copy(out=xbcT.t[:, :, 0:3], in_=xbcT.t[:, :, 128:131]), reads=xbcT.all(), writes=xbcT.all())
            bX = bank(); pX = bX.t[:].bitcast(BF16)
            P.add('pe', [(lambda c: lambda e: e.transpose(out=pX[:, c * 128:(c + 1) * 128], in_=xcT.t[:, c, :], identity=identb.t[:, :]))(c) for c in range(6)],
                  reads=xcT.p(0, 1, 2, 3, 4, 5) + identb.all(), writes=bX.all())
            if main:
                P.add('act', lambda e: e.activation(out=Btok.t[:, :], in_=pX[:, 512:768], func=AF.Copy), reads=bX.all(), writes=Btok.all())
                P.add('act', lambda e: e.activation(out=xs_tok.t[:, :], in_=pX[:, 0:512], func=AF.Copy), reads=bX.all(), writes=xs_tok.all())
            else:
                P.add('dve', lambda e: e.tensor_copy(out=Btok.t[:, :], in_=pX[:, 512:768]), reads=bX.all(), writes=Btok.all())
                P.add('dve', lambda e: e.tensor_copy(out=xs_tok.t[:, :], in_=pX[:, 0:512]), reads=bX.all(), writes=xs_tok.all())
            a_ = scal.t[:, 0:8]
            P.add('dve', lambda e: e.tensor_tensor(out=a_, in0=dtt.t[:, 0:8], in1=A_b.t[:, :], op=ALU.mult), reads=dtt.all() + A_b.all(), writes=scal.all())
            if main:
                P.add('pool', lambda e: e.tensor_tensor(out=Lt.t[:, :, :], in0=bc(tri.t[:, :].unsqueeze(1), [128, 8, 128]), in1=bc(a_.unsqueeze(2), [128, 8, 128]), op=ALU.mult),
                      reads=tri.all() + scal.all(), writes=Lt.all())
                bcs = bank()
                P.add('pe', lambda e: e.matmul(bcs.t[:, 0:8], lhsT=tri.t[:, :], rhs=a_, start=True, stop=True), reads=tri.all() + scal.all(), writes=bcs.all())
                P.add('dve', lambda e: e.tensor_scalar(out=scal.t[:, 8:16], in0=bcs.t[:, 0:8], scalar1=-1.0, scalar2=None, op0=ALU.mult), reads=bcs.all(), writes=scal.all())
                P.add('act', lambda e: e.activation(out=scal.t[:, 16:24], in_=bcs.t[:, 0:8], func=AF.Exp), reads=bcs.all(), writes=scal.all())
                bC0, bC1 = bank(), bank()
                for hf, bk in enumerate((bC0, bC1)):
                    P.add('pe', [lambda e, bk=bk, hf=hf: e.matmul(bk.t[:, :], lhsT=onesf.t[:, :], rhs=Lt.t[:, hf * 4:(hf + 1) * 4, :].rearrange("p h i -> p (h i)"), start=True, stop=False),
                                 lambda e, bk=bk: e.matmul(bk.t[:, :], lhsT=identb.t[:, :], rhs=negm.t[:, :, :].rearrange("p h i -> p (h i)"), start=False, stop=True)],
                          reads=onesf.all() + Lt.all() + identb.all() + negm.all(), writes=bk.all())
                for hf, bk in enumerate((bC0, bC1)):
                    P.add('act', lambda e, bk=bk, hf=hf: e.activation(out=scal.t[:, 24 + hf * 4:28 + hf * 4], in_=bk.t[:, :].rearrange("p (h i) -> p h i", h=4)[:, :, 127], func=AF.Exp), reads=bk.all(), writes=scal.all())
                    P.add('act', [(lambda hh, bk=bk, hf=hf: lambda e: e.activation(out=Lt.t[:, hf * 4 + hh, :], in_=bk.t[:, hh * 128:(hh + 1) * 128], func=AF.Exp, bias=scal.t[:, 8 + hf * 4 + hh:9 + hf * 4 + hh]))(hh) for hh in range(4)],
                          reads=bk.all() + scal.all(), writes=Lt.all())
                bCB = bank()
                P.add('pe', [(lambda g: lambda e: e.matmul(bCB.t[:, g * 128:(g + 1) * 128], lhsT=xcT.t[:, 4 + g, :], rhs=xcT.t[:, 6 + g, :], start=True, stop=True))(g) for g in range(2)],
                      reads=xcT.p(4, 5, 6, 7), writes=bCB.all())
                P.add('dve', lambda e: e.tensor_tensor(out=MT.t[:, :, :].rearrange("p (g r) i -> p g r i", g=2), in0=Lt.t[:, :, :].rearrange("p (g r) i -> p g r i", g=2),
                                                       in1=bc(bCB.t[:, 0:256].rearrange("p (g i) -> p g i", g=2).unsqueeze(2), [128, 2, 4, 128]), op=ALU.mult),
                      reads=Lt.all() + bCB.all(), writes=MT.all())
                dte_ap = Lt.t[:, :, 127]
                dte_keys = Lt.all()
                cd_ap = scal.t[:, 24:32]
            else:
                bsf = bank()
                P.add('pe', [lambda e: e.matmul(bsf.t[:, 0:8], lhsT=striu.t[:, :], rhs=a_, start=True, stop=True),
                             lambda e: e.matmul(bsf.t[:, 8:16], lhsT=onesf.t[:, :], rhs=a_, start=True, stop=True)], reads=striu.all() + onesf.all() + scal.all(), writes=bsf.all())
                P.add('act', lambda e: e.activation(out=scal.t[:, 16:32], in_=bsf.t[:, 0:16], func=AF.Exp), reads=bsf.all(), writes=scal.all())
                dte_ap = scal.t[:, 16:24]
                dte_keys = scal.all()
                cd_ap = scal.t[:, 24:32]
            P.add('dve', lambda e: e.tensor_tensor(out=scal.t[:, 32:40], in0=dtt.t[:, 0:8], in1=dte_ap, op=ALU.mult), reads=dtt.all() + dte_keys, writes=scal.all())
            pXs = xs_tok.t[:, :].rearrange("p (h d) -> p h d", h=8)
            P.add(pe2, lambda e: e.tensor_tensor(out=xdte.t[:, :].rearrange("p (h d) -> p h d", h=8), in0=pXs, in1=bc(scal.t[:, 32:40].unsqueeze(2), [128, 8, 64]), op=ALU.mult),
                  reads=xs_tok.all() + scal.all(), writes=xdte.all())
            if main:
                P.add('pool', lambda e: e.tensor_tensor(out=xdt.t[:, :].rearrange("p (h d) -> p h d", h=8), in0=pXs, in1=bc(dtt.t[:, 0:8].unsqueeze(2), [128, 8, 64]), op=ALU.mult),
                      reads=xs_tok.all() + dtt.all(), writes=xdt.all())
                bY, bYo = bank(), bank()
                P.add('pe', [(lambda h: lambda e: e.matmul(bY.t[:, h * 64:(h + 1) * 64], lhsT=MT.t[:, h, :], rhs=xdt.t[:, h * 64:(h + 1) * 64], start=True, stop=True))(h) for h in range(8)],
                      reads=MT.all() + xdt.all(), writes=bY.all())
                P.add('pe', [(lambda g: lambda e: e.matmul(bYo.t[:, g * 256:(g + 1) * 256], lhsT=xcT.t[:, 6 + g, :], rhs=stateTb.t[:, g * 256:(g + 1) * 256], start=True, stop=True))(g) for g in range(2)],
                      reads=xcT.p(6, 7) + stateTb.all(), writes=bYo.all())
                P.add('dve', lambda e: e.tensor_tensor(out=yt.t[:, :].rearrange("p (h d) -> p h d", h=8), in0=bYo.t[:, :].rearrange("p (h d) -> p h d", h=8), in1=bc(scal.t[:, 16:24].unsqueeze(2), [128, 8, 64]), op=ALU.mult),
                      reads=bYo.all() + scal.all(), writes=yt.all())
                P.add('dve', lambda e: e.tensor_tensor(out=yt.t[:, :], in0=yt.t[:, :], in1=bY.t[:, :], op=ALU.add), reads=yt.all() + bY.all(), writes=yt.all())
                P.add('pool', lambda e: e.tensor_tensor(out=ytmp.t[:, :].rearrange("p (h d) -> p h d", h=8), in0=pXs, in1=bc(D_b.t[:, :].unsqueeze(2), [128, 8, 64]), op=ALU.mult),
                      reads=xs_tok.all() + D_b.all(), writes=ytmp.all())
                P.add('dve', lambda e: e.tensor_tensor(out=yt.t[:, :], in0=yt.t[:, :], in1=ytmp.t[:, :], op=ALU.add), reads=yt.all() + ytmp.all(), writes=yt.all())
                P.add('dve', lambda e: e.tensor_tensor(out=yt.t[:, :], in0=yt.t[:, :], in1=szt.t[:, :], op=ALU.mult), reads=yt.all() + szt.all(), writes=yt.all())
                group_norm(yt, 128, yn, sm("ssg_m"))
                bk = bank(); pv = bk.t[:].bitcast(BF16)
                P.add('pe', [(lambda c, pv=pv: lambda e: e.transpose(out=pv[:, c * 128:(c + 1) * 128], in_=yn.t[:, c * 128:(c + 1) * 128], identity=identb.t[:, :]))(c) for c in range(4)],
                      reads=yn.all() + identb.all(), writes=bk.all())
                P.add('act', lambda e, pv=pv: e.activation(out=yT.t[:, :, :], in_=pv[:, 0:512].rearrange("p (c t) -> p c t", c=4), func=AF.Copy), reads=bk.all(), writes=yT.all())
            bS = bank()
            P.add('pe', [(lambda g: lambda e: e.matmul(bS.t[:, g * 256:(g + 1) * 256], lhsT=Btok.t[:, g * 128:(g + 1) * 128], rhs=xdte.t[:, g * 256:(g + 1) * 256], start=True, stop=True))(g) for g in range(2)],
                  reads=Btok.all() + xdte.all(), writes=bS.all())
            P.add(pe2, lambda e: e.tensor_tensor(out=stateT.t[:, :].rearrange("p (h d) -> p h d", h=8), in0=stateT.t[:, :].rearrange("p (h d) -> p h d", h=8), in1=bc(cd_ap.unsqueeze(2), [128, 8, 64]), op=ALU.mult),
                  reads=stateT.all() + scal.all(), writes=stateT.all())
            P.add('dve', lambda e: e.tensor_tensor(out=stateT.t[:, :], in0=stateT.t[:, :], in1=bS.t[:, :], op=ALU.add), reads=stateT.all() + bS.all(), writes=stateT.all())
            if mode == 'prelast':
                P.add('dve', lambda e: e.tensor_scalar(out=stateT.t[:, :], in0=stateT.t[:, :], scalar1=flag.t[:, 0:1], scalar2=None, op0=ALU.mult), reads=stateT.all() + flag.all(), writes=stateT.all())
                P.add('dve', lambda e: e.tensor_scalar(out=xbcT.t[:, :, 0:3], in0=xbcT.t[:, :, 0:3], scalar1=flag.t[:, 0:1], scalar2=None, op0=ALU.mult), reads=xbcT.all() + flag.all(), writes=xbcT.all())
            if main or mode == 'prelast':
                P.add('act', lambda e: e.activation(out=stateTb.t[:, :], in_=stateT.t[:, :], func=AF.Copy), reads=stateT.all(), writes=stateTb.all())
            if not main:
                return
            ssd_ops = P.end_capture()
            P.capture()
            bank_sub[0] = 1
            pslot = (gi - 1) % 3
            for kb, sl in enumerate((pslot, slot)):
                bS0, bS1 = bank(), bank()
                for par, bk in enumerate((bS0, bS1)):
                    P.add('pe', [(lambda hh, j, bk=bk, par=par, sl=sl: lambda e: e.matmul(bk.t[:, (hh * 2 + j) * 128:(hh * 2 + j + 1) * 128], lhsT=Kh[sl].t[par * 64:par * 64 + 64, hh, :], rhs=qT.t[par * 64:par * 64 + 64, hh * 2 + j, :], start=True, stop=True))(hh, j) for hh in range(2) for j in range(2)],
                          reads=Kh[sl].all() + qT.all(), writes=bk.all())
                ebi = 1 - kb
                for par, bk in enumerate((bS0, bS1)):
                    P.add('act', lambda e, bk=bk: e.activation(out=Eraw.t[:, :, :], in_=bk.t[:, :].rearrange("p (c q) -> p c q", c=4), func=AF.Exp, scale=SCALE), reads=bk.all(), writes=Eraw.all())
                    P.add('dve', lambda e, kb=kb, ebi=ebi, par=par: e.tensor_tensor(out=ET[kb].t[:, :, :].rearrange("p (c par) q -> p par c q", par=2)[:, par], in0=Eraw.t[:, :, :],
                                                                            in1=expB.t[:, ebi, :, :].rearrange("p q (c par) -> p par c q", par=2)[:, par], op=ALU.mult),
                          reads=Eraw.all() + expB.all(), writes=ET[kb].all())
                if kb == 0 and t == 0:
                    P.add('dve', lambda e: e.tensor_scalar(out=ET[0].t[:, :, :], in0=ET[0].t[:, :, :], scalar1=flag.t[:, 0:1], scalar2=None, op0=ALU.mult), reads=ET[0].all() + flag.all(), writes=ET[0].all())
            bO = [bank(), bank()]
            for hh in range(2):
                P.add('pe', [(lambda kb, hh=hh: lambda e: e.matmul(bO[hh].t[:, :], lhsT=Vh[(pslot, slot)[kb]].t[:, hh, :], rhs=ET[kb].t[:, hh * 4:(hh + 1) * 4, :].rearrange("p r q -> p (r q)"), start=(kb == 0), stop=(kb == 1)))(kb) for kb in range(2)],
                      reads=Vh[pslot].all() + Vh[slot].all() + ET[0].all() + ET[1].all(), writes=bO[hh].all())
                P.add('dve', lambda e, hh=hh: e.tensor_tensor(out=rec.t[:, 0, :].rearrange("p (r q) -> p r q", r=4), in0=bO[hh].t[64:128, :].rearrange("p (r q) -> p r q", r=4), in1=bc(esink.t[64:128, hh * 4:(hh + 1) * 4].unsqueeze(2), [64, 4, 128]), op=ALU.add),
                      reads=bO[hh].all() + esink.all(), writes=rec.all())
                P.add('dve', lambda e, hh=hh: e.reciprocal(out=rec.t[:, 0, :], in_=rec.t[:, 0, :]), reads=rec.all(), writes=rec.all())
                P.add('dve', lambda e, hh=hh: e.tensor_tensor(out=aT.t[:, hh * 4:(hh + 1) * 4, :].rearrange("p r q -> p (r q)"), in0=bO[hh].t[0:64, :], in1=rec.t[:, 0, :], op=ALU.mult),
                      reads=bO[hh].all() + rec.all(), writes=aT.all())
            att_ops = P.end_capture()
            bank_sub[0] = None
            P.ops.extend(Prog.merge(ssd_ops, att_ops))
            b_in_g = t % GB
            out_proj(aT, yT, 128, xt.t[:, :], xt.all(), x1.t[:, b_in_g, :], x1.p(b_in_g))
            rmsnorm_T(x1.t[:, b_in_g, :], x1.p(b_in_g), 128, w2b, hn, h2T.t[:, :, b_in_g * 128:(b_in_g + 1) * 128], h2T.p(b_in_g), "m2")

        for t in range(NBLK):
            P.capture()
            block(xp, t, 'prelast' if t == NBLK - 1 else 'pre')
            bank_mode[0] = None
            P.pipe_push(P.end_capture())
        P.pipe_drain()
        hn_t[0] = hn_sf
        def s_out(bi, full, keys):
            P.dma('sp', 'sout', y_s[:, :], full, reads=keys)
        P.capture()
        bank_mode[0] = 1
        ffn(h2T_s, SB, [(0, SB)], lambda bi, half: x1_s.t[:, :] if half is None else x1_s.t[:, half * 512:(half + 1) * 512], lambda bi: x1_s.all(), s_out, actT_s, "s")
        bank_mode[0] = None
        P.pipe_push(P.end_capture(), split=False)
        hn_t[0] = hn
        P.marks['prefix'] = len(P.ops)
        for t in range(NBLK):
            P.marks['main%d' % t] = len(P.ops)
            P.capture()
            block(xm, t, 'main')
            bank_mode[0] = None
            P.pipe_push(P.end_capture())
            if t % GB == GB - 1:
                g0 = t - (GB - 1)
                P.pipe_drain()
                if t + 1 < NBLK:
                    preload_x(xm, t + 1)

                def m_out(bi, full, keys, g0=g0):
                    P.dma('pool', 'yout%d' % bi, y_m[(g0 + bi) * 128:(g0 + bi + 1) * 128, :], full, reads=keys)
                tail = ffn(h2T, GB * 128, [(bi * 128, 128) for bi in range(GB)],
                           lambda bi, half: x1.t[:, bi, :] if half is None else x1.t[:, bi, half * 512:(half + 1) * 512],
                           lambda bi: x1.p(bi), m_out, actT, "m", defer_tail=True)
                P.pipe_push(tail, split=False)
        P.pipe_drain()
        P.marks['mainend'] = len(P.ops)
        bk = bank()
        P.add('pe', lambda e: e.matmul(bk.t[0:24, 0:128], lhsT=ncv.t[:, :, :].rearrange("p c j -> p (c j)"), rhs=identf.t[:, :], start=True, stop=True), reads=ncv.all() + identf.all(), writes=bk.all())
        P.add('dve', lambda e: e.tensor_copy(out=ncvT.t[0:24, 0:128], in_=bk.t[0:24, 0:128]), reads=bk.all(), writes=ncvT.all())
        for c in range(8):
            P.dma('sp', 'pout', ncv_p[:, c * 128:(c + 1) * 128], ncvT.t[c * 3:(c + 1) * 3, 0:128], reads=ncvT.all())
        bk2 = bank()
        P.add('pe', [(lambda c: lambda e: e.matmul(bk2.t[:, c * 128:(c + 1) * 128], lhsT=stateT.t[:, c * 128:(c + 1) * 128], rhs=identf.t[:, :], start=True, stop=True))(c) for c in range(4)],
              reads=stateT.all() + identf.all(), writes=bk2.all())
        P.add('dve', lambda e: e.tensor_copy(out=nss_sb.t[:, 0:4, :], in_=bk2.t[:, :].rearrange("p (c n) -> p c n", c=4)), reads=bk2.all(), writes=nss_sb.all())
        P.dma('sp', 'pout', nss_p.rearrange("(c p) n -> p c n", p=128), nss_sb.t[:, 0:4, :], reads=nss_sb.all())

        _P[0] = P
        with nc.Block() as blockctx:
            P.finish(st)
    return nc


def _bucket_onehot():
    n = np.arange(128)
    exact = 16
    nf = np.maximum(n, 1).astype(np.float32)
    large = exact + (np.log(nf / exact) / math.log(128 / exact) * (32 - exact)).astype(np.int32)
    bucket = np.where(n < exact, n, np.minimum(large, 31))
    oh = np.zeros((32, 128), np.float32)
    oh[bucket, n] = 1.0
    return oh


_NC = [None]
_P = [None]


def kernel(**inp):
    f = lambda a: np.ascontiguousarray(np.asarray(a, dtype=np.float32))
    x_prompt = f(inp["x_prompt"]); x_sample = f(inp["x_sample"])
    if _NC[0] is None:
        _NC[0] = build()
    nc = _NC[0]
    shared = {
        "rel_bias": f(inp["rel_bias"]), "onehot": _bucket_onehot(),
        "norm1_w": f(inp["norm1_w"]).reshape(1, D), "w_in": f(inp["w_in"])[0], "attn_sinks": f(inp["attn_sinks"]).reshape(1, 8),
        "conv_w": f(inp["conv_w"])[0], "conv_b": f(inp["conv_b"]).reshape(1, 1024), "dt_bias": f(inp["dt_bias"]).reshape(1, 8),
        "A_log": f(inp["A_log"]).reshape(1, 8), "D_skip": f(inp["D_skip"]).reshape(1, 8), "ssm_norm_w": f(inp["ssm_norm_w"]).reshape(1, 512),
        "w_out": f(inp["w_out"])[0], "norm2_w": f(inp["norm2_w"]).reshape(1, D), "w_gate": f(inp["w_gate"])[0], "w_up": f(inp["w_up"])[0],
        "w_down": f(inp["w_down"])[0], "final_norm_w": f(inp["final_norm_w"]).reshape(1, D),
    }
    ck = f(inp["cache_k"])[0].reshape(128, 128, 128); cv = f(inp["cache_v"])[0].reshape(128, 128, 128)
    sconv = f(inp["state_conv"])[0]; sssm = f(inp["state_ssm"])[0].reshape(128 * 8, 64 * 128)
    in_maps = []
    for c in range(NCORES):
        b, half = c // 2, c % 2
        m = dict(shared)
        m["xm"] = x_prompt[b, half * 2048:(half + 1) * 2048]
        m["xp"] = x_prompt[b, 0:2048] if half == 1 else np.zeros((2048, D), np.float32)
        m["flag"] = np.full((128, 1), float(half), np.float32)
        m["xsm"] = x_sample[c * SB:(c + 1) * SB, 0]
        m["ck"] = ck[c * SB:(c + 1) * SB]; m["cv"] = cv[c * SB:(c + 1) * SB]
        m["sconv"] = sconv[c * SB:(c + 1) * SB]; m["sssm"] = sssm[c * SB * 8:(c + 1) * SB * 8]
        in_maps.append({k: np.ascontiguousarray(v) for k, v in m.items()})
    res = run_bass_kernel_spmd(nc, in_maps, core_ids=list(range(NCORES))).results
    yp = np.zeros((4, 4096, D), np.float32)
    for c in range(NCORES):
        yp[c // 2, (c % 2) * 2048:(c % 2 + 1) * 2048] = res[c]["y_m"]
    ys = np.concatenate([res[c]["y_s"] for c in range(NCORES)], 0).reshape(128, 1, D)
    odd = [1, 3, 5, 7]
    nkp = np.stack([res[c]["nk_p"].reshape(128, 2, 64) for c in odd])[None]
    nvp = np.stack([res[c]["nv_p"].reshape(128, 2, 64) for c in odd])[None]
    ncp = np.stack([res[c]["ncv_p"] for c in odd])[None]
    nsp = np.stack([res[c]["nss_p"].reshape(8, 64, 128) for c in odd])[None]
    nks = np.concatenate([res[c]["nk_s"] for c in range(NCORES)], 0).reshape(1, 128, 128, 2, 64)
    nvs = np.concatenate([res[c]["nv_s"] for c in range(NCORES)], 0).reshape(1, 128, 128, 2, 64)
    ncs = np.concatenate([res[c]["ncv_s"] for c in range(NCORES)], 0)[None]
    nsss = np.concatenate([res[c]["nss_s"] for c in range(NCORES)], 0).reshape(1, 128, 8, 64, 128)
    return (yp, ys, nkp.astype(np.float32), nvp.astype(np.float32), ncp.astype(np.float32), nsp.astype(np.float32),
            nks, nvs, ncs, nsss)
```

```python
import math
from contextlib import ExitStack
import numpy as np
import concourse.bass as bass
import concourse.mybir as mybir
from concourse.bass_utils import run_bass_kernel_spmd

F32 = mybir.dt.float32
BF16 = mybir.dt.bfloat16
AF = mybir.ActivationFunctionType
ALU = mybir.AluOpType
AX = mybir.AxisListType

NCORES = 8
D = 1024
NBLK = 16
GB = 4
DFF = 2816
NFC = DFF // 128
DIN = 2312
WIN_W = DIN + 256
SCALE = 0.125
EPS = 1e-6
NEG = -30000.0
SB = 16


class Tl:
    def __init__(self, name, t, nparts=1):
        self.name, self.t, self.nparts = name, t, nparts

    def all(self):
        return [(self.name, i) for i in range(self.nparts)]

    def p(self, *idx):
        return [(self.name, i) for i in idx]


class Prog:
    def __init__(self, nc):
        self.nc = nc
        self.ops = []
        self.marks = {}
        self.eng = {'pe': nc.tensor, 'act': nc.scalar, 'dve': nc.vector, 'pool': nc.gpsimd, 'sp': nc.sync}

    def add(self, eng, fns, reads=(), writes=(), stream=None):
        if callable(fns):
            fns = [fns]
        import sys as _s
        fr = _s._getframe(1)
        if fr.f_code.co_name == 'dma':
            fr = fr.f_back
        self.ops.append(dict(eng=eng, fns=list(fns), reads=list(reads), writes=list(writes), stream=stream, barrier=False, line=fr.f_lineno))

    def dma(self, eng, stream, out, in_, reads=(), writes=(), **kw):
        self.add(eng, [lambda e: e.dma_start(out=out, in_=in_, **kw)], reads, writes, stream=stream)

    def capture(self):
        if not hasattr(self, '_cstack'):
            self._cstack = []
        self._cstack.append(self.ops)
        self.ops = []

    def end_capture(self):
        got = self.ops
        self.ops = self._cstack.pop()
        return got

    @staticmethod
    def merge(a, b):
        out = []
        i = j = 0
        while i < len(a) or j < len(b):
            if j >= len(b) or (i < len(a) and i * len(b) <= j * len(a)):
                out.append(a[i]); i += 1
            else:
                out.append(b[j]); j += 1
        return out

    def pipe_push(self, blk_ops, split=True, frac=0.44):
        cur = getattr(self, '_cur', [])
        if split:
            h = int(len(blk_ops) * frac)
            head, tail = blk_ops[:h], blk_ops[h:]
        else:
            head, tail = [], blk_ops
        i = j = 0
        a, b = len(cur), len(head)
        while i < a or j < b:
            if j >= b or (i < a and i * b <= j * a):
                self.ops.append(cur[i]); i += 1
            else:
                self.ops.append(head[j]); j += 1
        self._cur = tail

    def pipe_drain(self):
        self.ops.extend(getattr(self, '_cur', []))
        self._cur = []

    def barrier(self):
        for e in ('pe', 'act', 'dve', 'pool', 'sp'):
            self.ops.append(dict(eng=e, fns=[], reads=[], writes=[], stream=None, barrier=True))

    def finish(self, stack):
        nc = self.nc
        import os
        mx = os.environ.get("K_MAXOPS")
        if mx:
            mx = self.marks.get(mx, None) if not mx.isdigit() else int(mx)
            self.ops = self.ops[:mx]
        ops = self.ops
        n = len(ops)
        last_w, readers = {}, {}
        deps = [set() for _ in range(n)]
        for i, op in enumerate(ops):
            if op['barrier']:
                seen_e = {}
                for j in range(i - 1, -1, -1):
                    o = ops[j]
                    if o['barrier']:
                        continue
                    key = o['stream'] if o['stream'] else o['eng']
                    if key == 'wcast':
                        continue
                    if key not in seen_e:
                        seen_e[key] = j
                deps[i] = set(seen_e.values())
                continue
            dset = deps[i]
            for k in op['reads']:
                if k in last_w:
                    dset.add(last_w[k])
                if k[0].startswith('ps'):
                    for r in readers.get(k, ()):
                        if ops[r]['eng'] != op['eng']:
                            dset.add(r)
            for k in op['writes']:
                if k in last_w:
                    dset.add(last_w[k])
                for r in readers.get(k, ()):
                    dset.add(r)
            dset.discard(i)
            for k in op['reads']:
                readers.setdefault(k, []).append(i)
            for k in op['writes']:
                last_w[k] = i
                readers[k] = []
            if op['eng'] == 'pe' and op['stream'] is None:
                deps[i] = {j for j in dset if not (ops[j]['eng'] == 'pe' and ops[j]['stream'] is None)}
        needed = [False] * n
        for i in range(n):
            for j in deps[i]:
                needed[j] = True
        sems = {}

        def getsem(name):
            if name not in sems:
                sems[name] = stack.enter_context(nc.semaphore("s_" + name))
            return sems[name]
        counts = {}
        sig = [None] * n
        for i, op in enumerate(ops):
            if op['barrier'] or not op['fns']:
                continue
            if op['stream']:
                key = 'd_' + op['stream']
                counts[key] = counts.get(key, 0) + 16 * len(op['fns'])
                sig[i] = (key, counts[key])
            elif needed[i]:
                key = 'e_' + op['eng']
                counts[key] = counts.get(key, 0) + 1
                sig[i] = (key, counts[key])
        seen = {e: {} for e in self.eng}
        issued = {}
        for i, op in enumerate(ops):
            e = self.eng[op['eng']]
            waits = {}
            for j in deps[i]:
                if sig[j] is None:
                    continue
                k, v = sig[j]
                if k.startswith('d_'):
                    v = issued[k]
                if v > waits.get(k, 0):
                    waits[k] = v
            wl = [(k, v) for k, v in waits.items() if seen[op['eng']].get(k, 0) < v]
            for k, v in wl:
                seen[op['eng']][k] = v
            if op['barrier'] or not op['fns']:
                for k, v in wl:
                    e.wait_ge(getsem(k), v)
                continue
            attach = None
            if wl and op['stream'] is None:
                attach = wl[0]
                wl = wl[1:]
            for k, v in wl:
                e.wait_ge(getsem(k), v)
            ins = None
            for fi, fn in enumerate(op['fns']):
                ins = fn(e)
                if fi == 0 and attach is not None:
                    ins._wait_ge(getsem(attach[0]), attach[1])
                if op['stream']:
                    ins.then_inc(getsem(sig[i][0]), 16)
            if sig[i] is not None and not op['stream']:
                ins.then_inc(getsem(sig[i][0]), 1)
            if op['stream']:
                issued[sig[i][0]] = sig[i][1]
        sp = nc.sync
        for k, v in counts.items():
            if k.startswith('d_'):
                sp.wait_ge(getsem(k), v)


def bc(ap, shape):
    return ap.broadcast_to(shape)


def build():
    nc = bass.Bass("TRN2", target_bir_lowering=False)
    P = Prog(nc)

    def din(name, shape):
        return nc.dram_tensor(name, shape, F32, kind="ExternalInput").ap()

    def dout(name, shape):
        return nc.dram_tensor(name, shape, F32, kind="ExternalOutput").ap()

    xm = din("xm", [NBLK * 128, D]); xp = din("xp", [NBLK * 128, D]); flag_d = din("flag", [128, 1])
    xsm = din("xsm", [SB, D]); ck = din("ck", [SB, 128, 128]); cv = din("cv", [SB, 128, 128])
    sconv = din("sconv", [SB, 3, 1024]); sssm = din("sssm", [SB * 8, 64 * 128])
    rel_bias = din("rel_bias", [32, 8]); onehot_d = din("onehot", [32, 128])
    norm1_w = din("norm1_w", [1, D]); w_in = din("w_in", [D, DIN]); sinks = din("attn_sinks", [1, 8])
    conv_w = din("conv_w", [4, 1024]); conv_b = din("conv_b", [1, 1024]); dt_bias = din("dt_bias", [1, 8])
    A_log = din("A_log", [1, 8]); D_skip = din("D_skip", [1, 8]); ssm_norm_w = din("ssm_norm_w", [1, 512])
    w_out = din("w_out", [D, D]); norm2_w = din("norm2_w", [1, D]); w_gate = din("w_gate", [D, DFF])
    w_up = din("w_up", [D, DFF]); w_down = din("w_down", [DFF, D]); fnorm_w = din("final_norm_w", [1, D])

    y_m = dout("y_m", [NBLK * 128, D]); y_s = dout("y_s", [SB, D])
    nk_p = dout("nk_p", [128, 128]); nv_p = dout("nv_p", [128, 128]); ncv_p = dout("ncv_p", [3, 1024])
    nss_p = dout("nss_p", [512, 128])
    nk_s = dout("nk_s", [SB, 128, 128]); nv_s = dout("nv_s", [SB, 128, 128])
    ncv_s = dout("ncv_s", [SB, 3, 1024]); nss_s = dout("nss_s", [SB * 8, 64 * 128])
    scrS = nc.dram_tensor("scrS", [383, 8], F32).ap()
    scr1 = nc.dram_tensor("scr1", [SB, 512], F32).ap()
    scr2 = nc.dram_tensor("scr2", [SB, 8], F32).ap()
    scr3 = nc.dram_tensor("scr3", [SB, 512], F32).ap()
    scr4 = nc.dram_tensor("scr4", [SB * 8, 64], F32).ap()
    scrKV = nc.dram_tensor("scrKV", [SB, 256], BF16).ap()
    wgu_bf = nc.dram_tensor("wgu_bf", [NFC, 128, 2, 8, 128], BF16).ap()
    wd_bf = nc.dram_tensor("wd_bf", [DFF, D], BF16).ap()

    st = ExitStack()
    with st:
        def sb(name, shape, dt=F32, nparts=1, stack=st):
            return Tl(name, stack.enter_context(nc.sbuf_tensor(name, shape, dt)), nparts)

        ps = [Tl("ps%d" % i, st.enter_context(nc.psum_tensor("ps%d" % i, [128, 512], F32))) for i in range(8)]
        psrr = [0, 0, 0]
        psub = {}
        bank_mode = [None]
        bank_sub = [None]

        def bank():
            m = bank_mode[0]
            if m is None:
                b = ps[psrr[2] % 8]
                psrr[2] += 1
            elif bank_sub[0] is None:
                b = ps[m * 4 + psrr[m] % 4]
                psrr[m] += 1
            else:
                k = (m, bank_sub[0])
                psub[k] = psub.get(k, 0) + 1
                b = ps[m * 4 + bank_sub[0] * 2 + psub[k] % 2]
            return b

        identf = sb("identf", [128, 128]); identb = sb("identb", [128, 128], BF16)
        tri = sb("tri", [128, 128])
        striu = sb("striu", [128, 128])
        onesf = sb("onesf", [128, 128]); onesb = sb("onesb", [128, 64], BF16)
        negm = sb("negm", [128, 4, 128], BF16)
        expB = sb("expB", [128, 2, 128, 8])
        expBs = sb("expBs", [128, 8])
        w1T = sb("w1T", [128, 8]); w2b = sb("w2b", [128, D]); wfb = sb("wfb", [128, D]); wsT = sb("wsT", [128, 4])
        convdiag = sb("convdiag", [128, 8, 4, 128], BF16)
        cwT = sb("cwT", [128, 4, 8]); cbT = sb("cbT", [128, 8])
        A_b = sb("A_b", [128, 8]); dtb_b = sb("dtb_b", [128, 8]); D_b = sb("D_b", [128, 8]); esink = sb("esink", [128, 8])
        flag = sb("flag_sb", [128, 1])
        zero = sb("zero_sb", [128, 8])
        w_in_sb = sb("w_in_sb", [128, 8, WIN_W], BF16, nparts=8)
        w_out_a = sb("w_out_a", [64, 8, D], BF16); w_out_s = sb("w_out_s", [128, 4, D], BF16)
        NSL = 3
        wgu_sl = [sb("wgu%d" % i, [128, 2, 8, 128], BF16) for i in range(NSL)]
        wd_sl = [sb("wd%d" % i, [128, D], BF16) for i in range(NSL)]
        stateT = sb("stateT", [128, 512]); stateTb = sb("stateTb", [128, 512], BF16)
        small = {}

        def sm(name, w=8):
            if name not in small:
                small[name] = sb("sm_" + name, [128, w])
            return small[name]

        def cst(tl, fn):
            for f_ in (fn if isinstance(fn, (list, tuple)) else [fn]):
                P.add('pool', f_, reads=tl.all(), writes=tl.all())
        cst(identf, [lambda e: e.memset(identf.t[:], 1.0),
                     lambda e: e.affine_select(out=identf.t[:], in_=identf.t[:], pattern=[[-1, 128]], compare_op=ALU.is_equal, fill=0.0, base=0, channel_multiplier=1)])
        cst(tri, [lambda e: e.memset(tri.t[:], 1.0),
                  lambda e: e.affine_select(out=tri.t[:], in_=tri.t[:], pattern=[[1, 128]], compare_op=ALU.is_ge, fill=0.0, base=0, channel_multiplier=-1)])
        cst(striu, [lambda e: e.memset(striu.t[:], 1.0),
                    lambda e: e.affine_select(out=striu.t[:], in_=striu.t[:], pattern=[[-1, 128]], compare_op=ALU.is_gt, fill=0.0, base=0, channel_multiplier=1)])
        cst(onesf, lambda e: e.memset(onesf.t[:], 1.0))
        cst(onesb, lambda e: e.memset(onesb.t[:], 1.0))
        cst(zero, lambda e: e.memset(zero.t[:], 0.0))
        P.add('dve', lambda e: e.tensor_copy(out=identb.t[:], in_=identf.t[:]), reads=identf.all(), writes=identb.all())

        def bload(tl, src, width):
            P.dma('sp', 'cst', tl.t[:, 0:width], src[0:1, :].partition_broadcast(128), writes=tl.all())
        bload(w2b, norm2_w, D); bload(wfb, fnorm_w, D)
        P.dma('sp', 'cst', wsT.t[:], ssm_norm_w.rearrange("o (c p) -> p (o c)", p=128), writes=wsT.all(), allow_slow_non_contiguous=True)
        bload(A_b, A_log, 8); bload(dtb_b, dt_bias, 8); bload(D_b, D_skip, 8); bload(esink, sinks, 8)
        P.dma('sp', 'cst', flag.t[:], flag_d[:, :], writes=flag.all())
        P.dma('sp', 'cst', w1T.t[:], norm1_w.rearrange("o (k p) -> p (o k)", p=128), writes=w1T.all(), allow_slow_non_contiguous=True)
        for j in range(4):
            P.dma('sp', 'cst', cwT.t[:, j, :], conv_w[j:j + 1, :].rearrange("o (c p) -> p (o c)", p=128), writes=cwT.all(), allow_slow_non_contiguous=True)
        P.dma('sp', 'cst', cbT.t[:], conv_b.rearrange("o (c p) -> p (o c)", p=128), writes=cbT.all(), allow_slow_non_contiguous=True)
        P.add('act', lambda e: e.activation(out=A_b.t[:], in_=A_b.t[:], func=AF.Exp), reads=A_b.all(), writes=A_b.all())
        P.add('dve', lambda e: e.tensor_scalar(out=A_b.t[:], in0=A_b.t[:], scalar1=-1.0, scalar2=None, op0=ALU.mult), reads=A_b.all(), writes=A_b.all())
        P.add('act', lambda e: e.activation(out=esink.t[:], in_=esink.t[:], func=AF.Exp), reads=esink.all(), writes=esink.all())
        for c in range(8):
            for j in range(4):
                P.add('dve', (lambda c, j: lambda e: e.tensor_scalar(out=convdiag.t[:, c, j, :], in0=identf.t[:], scalar1=cwT.t[:, j, c:c + 1], scalar2=None, op0=ALU.mult))(c, j),
                      reads=identf.all() + cwT.all(), writes=convdiag.all())
        P.dma('pool', 'win', w_in_sb.t[:, :, 0:DIN], w_in.rearrange("(k p) n -> p k n", p=128), writes=w_in_sb.all())
        for k in range(8):
            P.add('dve', lambda e, k=k: e.tensor_scalar(out=w_in_sb.t[:, k, 0:DIN], in0=w_in_sb.t[:, k, 0:DIN], scalar1=w1T.t[:, k:k + 1], scalar2=None, op0=ALU.mult), reads=w_in_sb.all() + w1T.all(), writes=w_in_sb.all())
        for h in range(2):
            for dup in range(2):
                o0 = DIN + h * 128 + dup * 64
                P.add('dve', lambda e, o0=o0, h=h: e.tensor_copy(out=w_in_sb.t[:, :, o0:o0 + 64], in_=w_in_sb.t[:, :, 512 + h * 64:512 + (h + 1) * 64]), reads=w_in_sb.all(), writes=w_in_sb.all())
        P.dma('pool', 'wout', w_out_a.t[:, :, :], w_out[0:512, :].rearrange("(h d) n -> d h n", d=64), writes=w_out_a.all())
        P.dma('pool', 'wout', w_out_s.t[:, :, :], w_out[512:1024, :].rearrange("(c p) n -> p c n", p=128), writes=w_out_s.all())
        for c in range(4):
            P.add('dve', lambda e, c=c: e.tensor_scalar(out=w_out_s.t[:, c, :], in0=w_out_s.t[:, c, :], scalar1=wsT.t[:, c:c + 1], scalar2=None, op0=ALU.mult), reads=w_out_s.all() + wsT.all(), writes=w_out_s.all())
        def rmsnorm_T(xin_ap, xin_keys, nrow, wb, hn, hTdst, hT_keys, tag, cp='act'):
            ss = sm("ss_" + tag, 4)
            junk = hn
            P.add('act', lambda e: e.activation(out=junk.t[0:nrow, :], in_=xin_ap, func=AF.Square, accum_out=ss.t[0:nrow, 0:1]),
                  reads=xin_keys, writes=junk.all() + ss.all())
            P.add('act', lambda e: e.activation(out=ss.t[0:nrow, 1:2], in_=ss.t[0:nrow, 0:1], func=AF.Ln, bias=EPS, scale=1.0 / D), reads=ss.all(), writes=ss.all())
            P.add('act', lambda e: e.activation(out=ss.t[0:nrow, 2:3], in_=ss.t[0:nrow, 1:2], func=AF.Exp, scale=-0.5), reads=ss.all(), writes=ss.all())
            if wb is None:
                P.add('dve', lambda e: e.tensor_scalar(out=hn.t[0:nrow, :], in0=xin_ap, scalar1=ss.t[0:nrow, 2:3], scalar2=None, op0=ALU.mult),
                      reads=xin_keys + ss.all(), writes=hn.all())
            else:
                P.add('dve', lambda e: e.scalar_tensor_tensor(out=hn.t[0:nrow, :], in0=xin_ap, scalar=ss.t[0:nrow, 2:3], in1=wb.t[0:nrow, :], op0=ALU.mult, op1=ALU.mult),
                      reads=xin_keys + ss.all() + wb.all(), writes=hn.all())
            bk = bank()
            pv = bk.t[:].bitcast(BF16)
            P.add('pe', [(lambda k: lambda e: e.transpose(out=pv[:, k * 128:k * 128 + nrow], in_=hn.t[0:nrow, k * 128:(k + 1) * 128], identity=identb.t[0:nrow, 0:nrow]))(k) for k in range(8)],
                  reads=hn.all() + identb.all(), writes=bk.all())
            if cp == 'act':
                P.add('act', lambda e: e.activation(out=hTdst, in_=pv.rearrange("p (k t) -> p k t", k=8)[:, :, 0:nrow], func=AF.Copy), reads=bk.all(), writes=hT_keys)
            else:
                P.add('dve', lambda e: e.tensor_copy(out=hTdst, in_=pv.rearrange("p (k t) -> p k t", k=8)[:, :, 0:nrow]), reads=bk.all(), writes=hT_keys)
            return ss

        def softplus_dt(ps_ap, ps_keys, nrow, dt):
            P.add('dve', lambda e: e.tensor_tensor(out=dt.t[0:nrow, 8:16], in0=ps_ap, in1=dtb_b.t[0:nrow, :], op=ALU.add), reads=ps_keys + dtb_b.all(), writes=dt.all())
            P.add('act', lambda e: e.activation(out=dt.t[0:nrow, 8:16], in_=dt.t[0:nrow, 8:16], func=AF.Exp), reads=dt.all(), writes=dt.all())
            P.add('act', lambda e: e.activation(out=dt.t[0:nrow, 0:8], in_=dt.t[0:nrow, 8:16], func=AF.Ln, bias=1.0, scale=1.0), reads=dt.all(), writes=dt.all())

        def ffn(h2T, ntok, blocks, x1_ap_fn, x1_keys_fn, out_dma_fn, actT, tag, defer_tail=False):
            sgs = ffn_sg[tag]
            HC = NFC // 2
            nb = len(blocks)
            for fh in range(2):
                for ci in range(HC):
                    c = fh * HC + ci
                    wgu = wgu_sl[c % NSL]
                    P.dma('sp', 'wgu%d' % (c % NSL), wgu.t[:, :, :, :], wgu_bf[c], reads=WGU_KEYS, writes=wgu.all())
                    bg, bu = bank(), bank()
                    P.add('pe', [(lambda k, bg=bg, wgu=wgu: lambda e: e.matmul(bg.t[:, 0:ntok], lhsT=wgu.t[:, 0, k, :], rhs=h2T.t[:, k, 0:ntok], start=(k == 0), stop=(k == 7)))(k) for k in range(8)],
                          reads=wgu.all() + h2T.all(), writes=bg.all())
                    P.add('pe', [(lambda k, bu=bu, wgu=wgu: lambda e: e.matmul(bu.t[:, 0:ntok], lhsT=wgu.t[:, 1, k, :], rhs=h2T.t[:, k, 0:ntok], start=(k == 0), stop=(k == 7)))(k) for k in range(8)],
                          reads=wgu.all() + h2T.all(), writes=bu.all())
                    sg = sgs[c % 2]
                    P.add('act', lambda e, bg=bg, sg=sg: e.activation(out=sg.t[:, 0:ntok], in_=bg.t[:, 0:ntok], func=AF.Silu), reads=bg.all(), writes=sg.all())
                    P.add('dve', lambda e, bu=bu, sg=sg, ci=ci: e.tensor_tensor(out=actT.t[:, ci, 0:ntok], in0=sg.t[:, 0:ntok], in1=bu.t[:, 0:ntok], op=ALU.mult),
                          reads=bu.all() + sg.all(), writes=actT.p(ci))
                accs = [[bank(), bank()] for _ in range(nb)]
                for ci in range(HC):
                    c = fh * HC + ci
                    wd = wd_sl[c % NSL]
                    P.dma('sp', 'wd%d' % (c % NSL), wd.t[:, :], wd_bf[c * 128:(c + 1) * 128, :], reads=[('wd_bf', 0)], writes=wd.all())
                    fns = []
                    wr = []
                    for bi, (c0, nrow) in enumerate(blocks):
                        for half in range(2):
                            fns.append((lambda bi, half, c0, nrow, ci=ci, wd=wd, accs=accs: lambda e: e.matmul(accs[bi][half].t[0:nrow, :], lhsT=actT.t[:, ci, c0:c0 + nrow], rhs=wd.t[:, half * 512:(half + 1) * 512], start=(ci == 0), stop=(ci == HC - 1)))(bi, half, c0, nrow))
                            wr += accs[bi][half].all()
                    P.add('pe', fns, reads=wd.all() + actT.p(ci), writes=wr)
                if fh == 0:
                    for bi, (c0, nrow) in enumerate(blocks):
                        x1keys = x1_keys_fn(bi)
                        for half in range(2):
                            xa = x1_ap_fn(bi, half)
                            P.add('dve', lambda e, xa=xa, a=accs[bi][half], nrow=nrow: e.tensor_tensor(out=xa, in0=a.t[0:nrow, :], in1=xa, op=ALU.add),
                                  reads=accs[bi][half].all() + x1keys, writes=x1keys)
            for bi, (c0, nrow) in enumerate(blocks):
                x1keys = x1_keys_fn(bi)
                for half in range(2):
                    xa = x1_ap_fn(bi, half)
                    P.add('dve', lambda e, xa=xa, a=accs[bi][half], nrow=nrow: e.tensor_tensor(out=xa, in0=a.t[0:nrow, :], in1=xa, op=ALU.add),
                          reads=accs[bi][half].all() + x1keys, writes=x1keys)
            if defer_tail:
                P.capture()
            for bi, (c0, nrow) in enumerate(blocks):
                x1keys = x1_keys_fn(bi)
                ss = sm("ssf_" + tag, 4)
                full = x1_ap_fn(bi, None)
                if tag == "m":
                    jk = sgs[0]
                    jk_ap = jk.t[0:nrow, :].bitcast(BF16)
                else:
                    jk = hn_t[0]
                    jk_ap = jk.t[0:nrow, :]
                P.add('act', lambda e, full=full, nrow=nrow, jk_ap=jk_ap: e.activation(out=jk_ap, in_=full, func=AF.Square, accum_out=ss.t[0:nrow, 0:1]),
                      reads=x1keys, writes=jk.all() + ss.all())
                P.add('act', lambda e, nrow=nrow: e.activation(out=ss.t[0:nrow, 1:2], in_=ss.t[0:nrow, 0:1], func=AF.Ln, bias=EPS, scale=1.0 / D), reads=ss.all(), writes=ss.all())
                P.add('act', lambda e, nrow=nrow: e.activation(out=ss.t[0:nrow, 2:3], in_=ss.t[0:nrow, 1:2], func=AF.Exp, scale=-0.5), reads=ss.all(), writes=ss.all())
                P.add('dve', lambda e, full=full, nrow=nrow: e.scalar_tensor_tensor(out=full, in0=full, scalar=ss.t[0:nrow, 2:3], in1=wfb.t[0:nrow, :], op0=ALU.mult, op1=ALU.mult),
                      reads=x1keys + ss.all() + wfb.all(), writes=x1keys)
                out_dma_fn(bi, full, x1keys)
            if defer_tail:
                return P.end_capture()
            return None

        ffn_sg = {}
        hn_t = [None]
        WGU_KEYS = [('wgu_bf', i) for i in range(16)]
        xs_in = sb("xs_in", [SB, D]); h2T_s = sb("h2T_s", [128, 8, SB], BF16); actT_s = sb("actT_s", [128, NFC // 2, SB], BF16, nparts=NFC // 2)
        ffn_sg["s"] = [sb("sg_s%d" % i, [128, SB]) for i in range(2)]
        hn_sf = sb("hn_sf", [SB, D], BF16)
        for nm_, w_ in [("ss_s1", 4), ("ss_s2", 4), ("ssf_s", 4), ("ssg_s", 8), ("ss_m1", 4), ("ss_m2", 4), ("ssf_m", 4), ("ssg_m", 8), ("scal", 64)]:
            sm(nm_, w_)

        import os as _os
        for _i in range(int(_os.environ.get('K_DUMMY', '0'))):
            P.add('dve', lambda e: e.memset(zero.t[:], 0.0), writes=zero.all())
        P.marks['setup'] = len(P.ops)
        sst = ExitStack()
        with sst:
            def ssb(name, shape, dt=F32, nparts=1):
                return sb(name, shape, dt, nparts, stack=sst)
            h0 = [ssb("h0_%d" % i, [128, 16, 128]) for i in range(2)]; otmp = ssb("otmp", [128, 16, 128], nparts=2); kvst = otmp
            Kwb = ssb("Kwb", [128, SB, 128], BF16); Vwb = ssb("Vwb", [128, SB, 128], BF16)
            SelH = Tl("h0_1", h0[1].t, 1); SelHT = ssb("SelHT", [128, 8, SB])
            P.dma('sp', 'xs', xs_in.t[:, :], xsm[:, :], writes=xs_in.all())
            for hf, q in ((0, 'sp'), (1, 'act')):
                P.dma(q, 'kvst%d' % hf, kvst.t[0:127, hf * 8:(hf + 1) * 8, :], ck[hf * 8:(hf + 1) * 8, 1:128, :].rearrange("b p f -> p b f"), writes=kvst.p(hf))
            for hf, q in ((0, 'sp'), (1, 'act')):
                P.dma(q, 'kvsv%d' % hf, h0[0].t[0:127, hf * 8:(hf + 1) * 8, :], cv[hf * 8:(hf + 1) * 8, 1:128, :].rearrange("b p f -> p b f"), writes=h0[0].all())
            SelQ = ssb("SelQ", [SB, 4, 128]); SelQT = ssb("SelQT", [128, 4, SB])
            cbuf = ssb("cbuf", [128, 3, 256]); cw_b = ssb("cw_b", [128, 4, 256]); cb_b = ssb("cb_b", [128, 256])
            accq = ssb("accq", [128, 256]); tmpq = ssb("tmpq", [128, 256])
            P.add('dve', [lambda e: e.memset(cbuf.t[:, :, :], 0.0), lambda e: e.memset(cw_b.t[:, :, :], 0.0), lambda e: e.memset(cb_b.t[:, :], 0.0)],
                  writes=cbuf.all() + cw_b.all() + cb_b.all())
            for q4 in range(4):
                p0, cs0 = q4 * 32, q4 * 256
                P.dma('sp', 'sconv', cbuf.t[p0:p0 + SB, :, :], sconv[:, :, cs0:cs0 + 256], writes=cbuf.all())
                P.dma('sp', 'sconv', cw_b.t[p0:p0 + SB, :, :], bass.AP(conv_w.tensor, cs0, [[0, SB], [1024, 4], [1, 256]]), writes=cw_b.all())
                P.dma('sp', 'sconv', cb_b.t[p0:p0 + SB, :], conv_b[0:1, cs0:cs0 + 256].partition_broadcast(SB), writes=cb_b.all())
            antiI = ssb("antiI", [128, 128])
            cst(antiI, [lambda e: e.memset(antiI.t[:], 1.0),
                        lambda e: e.affine_select(out=antiI.t[:], in_=antiI.t[:], pattern=[[1, 128]], compare_op=ALU.is_equal, fill=0.0, base=-127, channel_multiplier=1)])
            cst(SelH, [lambda e: e.memset(SelH.t[0:SB, 0:8, :], 1.0),
                       lambda e: e.affine_select(out=SelH.t[0:SB, 0:8, :], in_=SelH.t[0:SB, 0:8, :], pattern=[[-1, 8], [1, 128]], compare_op=ALU.is_equal, fill=0.0, base=0, channel_multiplier=-8)])
            cst(SelHT, [lambda e: e.memset(SelHT.t[:, :, :], 1.0),
                       lambda e: e.affine_select(out=SelHT.t[:, :, :], in_=SelHT.t[:, :, :], pattern=[[-1, 8], [-8, SB]], compare_op=ALU.is_equal, fill=0.0, base=0, channel_multiplier=1)])
            cst(SelQ, [lambda e: e.memset(SelQ.t[:, :, :], 1.0),
                       lambda e: e.affine_select(out=SelQ.t[:, :, :], in_=SelQ.t[:, :, :], pattern=[[-32, 4], [1, 128]], compare_op=ALU.is_equal, fill=0.0, base=0, channel_multiplier=-1)])
            cst(SelQT, [lambda e: e.memset(SelQT.t[:, :, :], 1.0),
                       lambda e: e.affine_select(out=SelQT.t[:, :, :], in_=SelQT.t[:, :, :], pattern=[[-32, 4], [-1, SB]], compare_op=ALU.is_equal, fill=0.0, base=0, channel_multiplier=1)])
            sb_saved = sb
            sb = lambda name, shape, dt=F32, nparts=1, stack=sst: sb_saved(name, shape, dt, nparts, stack=sst)
            negf = sb("negf", [128, 128])
            cst(negf, [lambda e: e.memset(negf.t[:], 0.0),
                       lambda e: e.affine_select(out=negf.t[:], in_=negf.t[:], pattern=[[1, 128]], compare_op=ALU.is_ge, fill=NEG, base=0, channel_multiplier=-1)])
            P.add('dve', lambda e: e.tensor_copy(out=negm.t[:], in_=bc(negf.t[:, :].unsqueeze(1), [128, 4, 128])), reads=negf.all(), writes=negm.all())
            oh_sb = sb("oh_sb", [32, 128]); rb_sb = sb("rb_sb", [32, 8]); eb = sb("eb", [128, 8])
            P.dma('sp', 'cst', oh_sb.t[:], onehot_d[:, :], writes=oh_sb.all())
            P.dma('sp', 'cst', rb_sb.t[:], rel_bias[:, :], writes=rb_sb.all())
            b0 = bank()
            P.add('pe', lambda e: e.matmul(b0.t[:, 0:8], lhsT=oh_sb.t[:, :], rhs=rb_sb.t[:, :], start=True, stop=True), reads=oh_sb.all() + rb_sb.all(), writes=b0.all())
            P.add('act', lambda e: e.activation(out=eb.t[:], in_=b0.t[:, 0:8], func=AF.Exp), reads=b0.all(), writes=eb.all())
            b1 = bank()
            P.add('pe', lambda e: e.matmul(b1.t[:, 0:8], lhsT=antiI.t[:, :], rhs=eb.t[:, :], start=True, stop=True), reads=antiI.all() + eb.all(), writes=b1.all())
            P.add('dve', lambda e: e.tensor_copy(out=expBs.t[:], in_=b1.t[:, 0:8]), reads=b1.all(), writes=expBs.all())
            P.dma('sp', 'scrS', scrS[0:127, :], zero.t[0:127, :], reads=zero.all(), writes=[('scrS', 0)])
            P.dma('sp', 'scrS', scrS[255:383, :], zero.t[:, :], reads=zero.all(), writes=[('scrS', 0)])
            P.dma('sp', 'scrS', scrS[127:255, :], eb.t[:, :], reads=eb.all(), writes=[('scrS', 0)])
            sb = sb_saved
            hn_s = ssb("hn_s", [SB, D], BF16)
            hn_t[0] = hn_s
            hT_s = ssb("hT_s", [128, 8, SB], BF16)
            proj = ssb("proj_s", [SB, DIN]); projb = ssb("projb_s", [SB, 768], BF16)
            KT_s = ssb("KT_s", [64, SB * 2, 128], BF16); qT2 = ssb("qT2", [64, 8, SB], BF16)
            Es = ssb("Es", [128, 128]); Esb = ssb("Esb", [128, SB, 8], BF16)
            rec_s = ssb("rec_s", [64, 128]); aT_s = ssb("aT_s", [64, 8, SB], BF16)
            xc_s = ssb("xc_s", [SB, 1024]); tmp_s = ssb("tmp_s", [SB, 512])
            dt_s = ssb("dt_s", [128, 16]); dec_s = ssb("dec_s", [SB, 8]); xdt_s = ssb("xdt_s", [SB, 512])
            xdt_bh = ssb("xdt_bh", [128, 64]); dec_bh = ssb("dec_bh", [128, 1]); B_bh = ssb("B_bh", [128, 128]); C_bh = ssb("C_bh", [128, 128])
            y_bh = ssb("y_bh", [128, 64]); y_tok = ssb("y_tok", [SB, 512]); sz_s = ssb("sz_s", [SB, 512])
            yn_s = ssb("yn_s", [SB, 512], BF16); yT_s = ssb("yT_s", [128, 4, SB], BF16)
            x1_s = xs_in

            rmsnorm_T(xs_in.t[:, :], xs_in.all(), SB, None, hn_s, hT_s.t[:, :, :], hT_s.all(), "s1")
            colr = [(0, 512), (512, 1024), (1024, 1536), (1536, 2048), (2048, DIN)]
            for (c0, c1) in colr:
                bk = bank()
                P.add('pe', [(lambda k, bk=bk, c0=c0, c1=c1: lambda e: e.matmul(bk.t[0:SB, 0:c1 - c0], lhsT=hT_s.t[:, k, :], rhs=w_in_sb.t[:, k, c0:c1], start=(k == 0), stop=(k == 7)))(k) for k in range(8)],
                      reads=hT_s.all() + w_in_sb.all(), writes=bk.all())
                P.add('act', lambda e, bk=bk, c0=c0, c1=c1: e.activation(out=proj.t[:, c0:c1], in_=bk.t[0:SB, 0:c1 - c0], func=AF.Copy), reads=bk.all(), writes=proj.all())
            P.add('dve', lambda e: e.tensor_copy(out=projb.t[:, :], in_=proj.t[:, 0:768]), reads=proj.all(), writes=projb.all())
            P.dma('sp', 'skv0', scrKV[:, :], projb.t[:, 512:768], reads=projb.all(), writes=[('scrKV', 0)])
            P.dma('sp', 'skv', Kwb.t[127:128, :, :], scrKV[:, 0:128].rearrange("(o b) f -> o b f", o=1), reads=[('scrKV', 0)], writes=Kwb.all())
            P.dma('sp', 'skv', Vwb.t[127:128, :, :], scrKV[:, 128:256].rearrange("(o b) f -> o b f", o=1), reads=[('scrKV', 0)], writes=Vwb.all())
            P.dma('sp', 'sout', nk_s[:, 0:127, :], ck[:, 1:128, :])
            P.dma('sp', 'sout', nv_s[:, 0:127, :], cv[:, 1:128, :])
            P.dma('sp', 'sout', nk_s[:, 127, :], proj.t[:, 512:640], reads=proj.all())
            P.dma('sp', 'sout', nv_s[:, 127, :], proj.t[:, 640:768], reads=proj.all())
            P.dma('sp', 'sout', ncv_s[:, 0:2, :], sconv[:, 1:3, :])
            P.dma('sp', 'sout', ncv_s[:, 2, :], proj.t[:, 1280:2304], reads=proj.all())
            for hf in range(2):
                P.add('pool', lambda e, hf=hf: e.tensor_copy(out=Kwb.t[0:127, hf * 8:(hf + 1) * 8, :], in_=kvst.t[0:127, hf * 8:(hf + 1) * 8, :]), reads=kvst.p(hf), writes=Kwb.all())
            for hf in range(2):
                P.add('pool', lambda e, hf=hf: e.tensor_copy(out=Vwb.t[0:127, hf * 8:(hf + 1) * 8, :], in_=h0[0].t[0:127, hf * 8:(hf + 1) * 8, :]), reads=h0[0].all(), writes=Vwb.all())
            for gu, wsrc in enumerate((w_gate, w_up)):
                for k in range(8):
                    P.dma('pool', 'wcast', wgu_bf[:, :, gu, k, :].rearrange("c p j -> p c j"), wsrc[k * 128:(k + 1) * 128, :].rearrange("p (c j) -> p c j", j=128), writes=[('wgu_bf', gu * 8 + k)])
            P.dma('pool', 'wcast', wd_bf[:, :], w_down[:, :], writes=[('wd_bf', 0)])


            Tp = h0[0]
            P.dma('sp', 'expB', Tp.t[:, :, :].rearrange("p a n -> p (a n)"), bass.AP(scrS.tensor, 0, [[8, 128], [1, 2048]]), reads=[('scrS', 0)], writes=Tp.all())
            for q4 in range(4):
                bkx = bank()
                P.add('pe', lambda e, bkx=bkx, q4=q4: e.matmul(bkx.t[:, :], lhsT=antiI.t[:, :], rhs=Tp.t[:, :, :].rearrange("p a n -> p (a n)")[:, q4 * 512:(q4 + 1) * 512], start=True, stop=True), reads=antiI.all() + Tp.all(), writes=bkx.all())
                P.add('act', lambda e, bkx=bkx, q4=q4: e.activation(out=expB.t[:, :, :, :].rearrange("p a q h -> p (a q h)")[:, q4 * 512:(q4 + 1) * 512], in_=bkx.t[:, :], func=AF.Copy), reads=bkx.all(), writes=expB.all())
            bk = bank(); pv = bk.t[:].bitcast(BF16)
            P.add('pe', [(lambda hq, pv=pv: lambda e: e.transpose(out=pv[0:64, hq * SB:(hq + 1) * SB], in_=projb.t[:, hq * 64:(hq + 1) * 64], identity=identb.t[0:SB, 0:SB]))(hq) for hq in range(8)],
                  reads=projb.all() + identb.all(), writes=bk.all())
            P.add('act', lambda e, pv=pv: e.activation(out=qT2.t[:, :, :], in_=pv[0:64, 0:8 * SB].rearrange("p (h b) -> p h b", h=8), func=AF.Copy), reads=bk.all(), writes=qT2.all())
            for g4 in range(4):
                bk = bank(); pv = bk.t[:].bitcast(BF16)
                P.add('pe', [(lambda i, pv=pv, g4=g4: lambda e: e.transpose(out=pv[0:64, i * 128:(i + 1) * 128], in_=Kwb.t[:, (g4 * 8 + i) // 2, ((g4 * 8 + i) % 2) * 64:((g4 * 8 + i) % 2) * 64 + 64], identity=identb.t[:, :]))(i) for i in range(8)],
                      reads=Kwb.all() + identb.all(), writes=bk.all())
                P.add('dve', lambda e, pv=pv, g4=g4: e.tensor_copy(out=KT_s.t[:, g4 * 8:(g4 + 1) * 8, :], in_=pv[0:64, :].rearrange("p (i t) -> p i t", i=8)), reads=bk.all(), writes=KT_s.all())
            bsc = bank()
            P.add('pe', [(lambda b, h: lambda e: e.matmul(bsc.t[:, b * 8 + h * 4:b * 8 + h * 4 + 4], lhsT=KT_s.t[:, b * 2 + h, :], rhs=qT2.t[:, h * 4:(h + 1) * 4, b], start=True, stop=True))(b, h) for b in range(SB) for h in range(2)],
                  reads=KT_s.all() + qT2.all(), writes=bsc.all())
            P.add('act', lambda e: e.activation(out=Es.t[:, :], in_=bsc.t[:, 0:128], func=AF.Exp, scale=SCALE), reads=bsc.all(), writes=Es.all())
            P.add('dve', lambda e: e.tensor_tensor(out=Esb.t[:, :, :], in0=Es.t[:, :].rearrange("p (b h) -> p b h", b=SB), in1=bc(expBs.t[:, :].unsqueeze(1), [128, SB, 8]), op=ALU.mult),
                  reads=Es.all() + expBs.all(), writes=Esb.all())
            bden = bank(); bos = bank()
            P.add('pe', lambda e: e.matmul(bden.t[0:64, 0:128], lhsT=onesb.t[:, :], rhs=Esb.t[:, :, :].rearrange("p b h -> p (b h)"), start=True, stop=True), reads=onesb.all() + Esb.all(), writes=bden.all())
            P.add('pe', [(lambda b, h: lambda e: e.matmul(bos.t[0:64, b * 8 + h * 4:b * 8 + h * 4 + 4], lhsT=Vwb.t[:, b, h * 64:(h + 1) * 64], rhs=Esb.t[:, b, h * 4:(h + 1) * 4], start=True, stop=True))(b, h) for b in range(SB) for h in range(2)],
                  reads=Vwb.all() + Esb.all(), writes=bos.all())
            P.add('dve', lambda e: e.tensor_tensor(out=rec_s.t[:, :].rearrange("p (b h) -> p b h", b=SB), in0=bden.t[0:64, 0:128].rearrange("p (b h) -> p b h", b=SB), in1=bc(esink.t[0:64, :].unsqueeze(1), [64, SB, 8]), op=ALU.add),
                  reads=bden.all() + esink.all(), writes=rec_s.all())
            P.add('dve', lambda e: e.reciprocal(out=rec_s.t[:, :], in_=rec_s.t[:, :]), reads=rec_s.all(), writes=rec_s.all())
            P.add('dve', lambda e: e.tensor_tensor(out=aT_s.t[:, :, :].rearrange("p h b -> p b h"), in0=bos.t[0:64, 0:128].rearrange("p (b h) -> p b h", b=SB), in1=rec_s.t[:, :].rearrange("p (b h) -> p b h", b=SB), op=ALU.mult),
                  reads=bos.all() + rec_s.all(), writes=aT_s.all())
            bq = bank()
            P.add('pe', [(lambda q4: lambda e: e.matmul(bq.t[:, 0:256], lhsT=SelQ.t[:, q4, :], rhs=proj.t[:, 1280 + q4 * 256:1280 + (q4 + 1) * 256], start=(q4 == 0), stop=(q4 == 3)))(q4) for q4 in range(4)],
                  reads=SelQ.all() + proj.all(), writes=bq.all())
            P.add('dve', lambda e: e.tensor_tensor(out=accq.t[:, :], in0=bq.t[:, 0:256], in1=cw_b.t[:, 3, :], op=ALU.mult), reads=bq.all() + cw_b.all(), writes=accq.all())
            P.add('dve', lambda e: e.tensor_tensor(out=accq.t[:, :], in0=accq.t[:, :], in1=cb_b.t[:, :], op=ALU.add), reads=accq.all() + cb_b.all(), writes=accq.all())
            for j in range(3):
                P.add('dve', lambda e, j=j: e.tensor_tensor(out=tmpq.t[:, :], in0=cbuf.t[:, j, :], in1=cw_b.t[:, j, :], op=ALU.mult), reads=cbuf.all() + cw_b.all(), writes=tmpq.all())
                P.add('dve', lambda e: e.tensor_tensor(out=accq.t[:, :], in0=accq.t[:, :], in1=tmpq.t[:, :], op=ALU.add), reads=accq.all() + tmpq.all(), writes=accq.all())
            P.add('act', lambda e: e.activation(out=tmpq.t[:, :], in_=accq.t[:, :], func=AF.Silu), reads=accq.all(), writes=tmpq.all())
            bq2 = [bank(), bank()]
            P.add('pe', [(lambda q4: lambda e: e.matmul(bq2[q4 // 2].t[0:SB, (q4 % 2) * 256:(q4 % 2 + 1) * 256], lhsT=SelQT.t[:, q4, :], rhs=tmpq.t[:, :], start=True, stop=True))(q4) for q4 in range(4)],
                  reads=SelQT.all() + tmpq.all(), writes=bq2[0].all() + bq2[1].all())
            P.add('act', [(lambda i: lambda e: e.activation(out=xc_s.t[:, i * 512:(i + 1) * 512], in_=bq2[i].t[0:SB, :], func=AF.Copy))(i) for i in range(2)], reads=bq2[0].all() + bq2[1].all(), writes=xc_s.all())
            softplus_dt(proj.t[:, 2304:2312], proj.all(), SB, dt_s)
            P.add('dve', lambda e: e.tensor_tensor(out=dec_s.t[:, :], in0=dt_s.t[0:SB, 0:8], in1=A_b.t[0:SB, :], op=ALU.mult), reads=dt_s.all() + A_b.all(), writes=dec_s.all())
            P.add('act', lambda e: e.activation(out=dec_s.t[:, :], in_=dec_s.t[:, :], func=AF.Exp), reads=dec_s.all(), writes=dec_s.all())
            P.add('dve', lambda e: e.tensor_tensor(out=xdt_s.t[:, :].rearrange("p (h d) -> p h d", h=8), in0=xc_s.t[:, 0:512].rearrange("p (h d) -> p h d", h=8), in1=bc(dt_s.t[0:SB, 0:8].unsqueeze(2), [SB, 8, 64]), op=ALU.mult),
                  reads=xc_s.all() + dt_s.all(), writes=xdt_s.all())
            bsel = bank(); bsel2 = bank()
            P.add('pe', [(lambda h: lambda e: e.matmul(bsel.t[:, 0:64], lhsT=SelH.t[0:SB, h, :], rhs=xdt_s.t[:, h * 64:(h + 1) * 64], start=(h == 0), stop=(h == 7)))(h) for h in range(8)]
                  + [(lambda h: lambda e: e.matmul(bsel.t[:, 64:65], lhsT=SelH.t[0:SB, h, :], rhs=dec_s.t[:, h:h + 1], start=(h == 0), stop=(h == 7)))(h) for h in range(8)],
                  reads=SelH.all() + xdt_s.all() + dec_s.all(), writes=bsel.all())
            P.add('pe', [(lambda h: lambda e: e.matmul(bsel2.t[:, 0:128], lhsT=SelH.t[0:SB, h, :], rhs=xc_s.t[:, 512 + (h // 4) * 128:512 + (h // 4 + 1) * 128], start=(h == 0), stop=(h == 7)))(h) for h in range(8)]
                  + [(lambda h: lambda e: e.matmul(bsel2.t[:, 128:256], lhsT=SelH.t[0:SB, h, :], rhs=xc_s.t[:, 768 + (h // 4) * 128:768 + (h // 4 + 1) * 128], start=(h == 0), stop=(h == 7)))(h) for h in range(8)],
                  reads=SelH.all() + xc_s.all(), writes=bsel2.all())
            P.add('dve', [lambda e: e.tensor_copy(out=xdt_bh.t[:, :], in_=bsel.t[:, 0:64]), lambda e: e.tensor_copy(out=dec_bh.t[:, :], in_=bsel.t[:, 64:65])], reads=bsel.all(), writes=xdt_bh.all() + dec_bh.all())
            P.add('dve', [lambda e: e.tensor_copy(out=B_bh.t[:, :], in_=bsel2.t[:, 0:128]), lambda e: e.tensor_copy(out=C_bh.t[:, :], in_=bsel2.t[:, 128:256])], reads=bsel2.all(), writes=B_bh.all() + C_bh.all())
            for hf in range(4):
                hh = h0[hf % 2]; sse = 'dve'
                P.dma('sp', 'sh0_%d' % (hf % 2), hh.t[:, :, :].rearrange("p a n -> p (a n)"), sssm[:, hf * 2048:(hf + 1) * 2048], writes=hh.all())
                P.add(sse, lambda e, hf=hf: e.tensor_tensor(out=otmp.t[:, :, :], in0=bc(B_bh.t[:, :].unsqueeze(1), [128, 16, 128]), in1=bc(xdt_bh.t[:, hf * 16:(hf + 1) * 16].unsqueeze(2), [128, 16, 128]), op=ALU.mult),
                      reads=B_bh.all() + xdt_bh.all(), writes=otmp.all())
                P.add('dve', lambda e, hh=hh: e.scalar_tensor_tensor(out=hh.t[:, :, :], in0=hh.t[:, :, :], scalar=dec_bh.t[:, 0:1], in1=otmp.t[:, :, :], op0=ALU.mult, op1=ALU.add),
                      reads=hh.all() + dec_bh.all() + otmp.all(), writes=hh.all())
                P.dma('sp', 'snss%d' % (hf % 2), nss_s[:, hf * 2048:(hf + 1) * 2048], hh.t[:, :, :].rearrange("p a n -> p (a n)"), reads=hh.all())
                P.add(sse, lambda e, hh=hh: e.tensor_tensor(out=otmp.t[:, :, :], in0=hh.t[:, :, :], in1=bc(C_bh.t[:, :].unsqueeze(1), [128, 16, 128]), op=ALU.mult),
                      reads=hh.all() + C_bh.all(), writes=otmp.all())
                P.add('dve', lambda e, hf=hf: e.tensor_reduce(out=y_bh.t[:, hf * 16:(hf + 1) * 16], in_=otmp.t[:, :, :], axis=AX.X, op=ALU.add), reads=otmp.all(), writes=y_bh.all())
            bsel3 = bank()
            P.add('pe', [(lambda h: lambda e: e.matmul(bsel3.t[0:SB, h * 64:(h + 1) * 64], lhsT=SelHT.t[:, h, :], rhs=y_bh.t[:, :], start=True, stop=True))(h) for h in range(8)],
                  reads=SelHT.all() + y_bh.all(), writes=bsel3.all())
            P.add('dve', lambda e: e.tensor_copy(out=y_tok.t[:, :], in_=bsel3.t[0:SB, :]), reads=bsel3.all(), writes=y_tok.all())
            P.add('dve', lambda e: e.tensor_tensor(out=tmp_s.t[:, :].rearrange("p (h d) -> p h d", h=8), in0=xc_s.t[:, 0:512].rearrange("p (h d) -> p h d", h=8), in1=bc(D_b.t[0:SB, :].unsqueeze(2), [SB, 8, 64]), op=ALU.mult),
                  reads=xc_s.all() + D_b.all(), writes=tmp_s.all())
            P.add('dve', lambda e: e.tensor_tensor(out=y_tok.t[:, :], in0=y_tok.t[:, :], in1=tmp_s.t[:, :], op=ALU.add), reads=y_tok.all() + tmp_s.all(), writes=y_tok.all())
            P.add('act', lambda e: e.activation(out=sz_s.t[:, :], in_=proj.t[:, 768:1280], func=AF.Silu), reads=proj.all(), writes=sz_s.all())
            P.add('dve', lambda e: e.tensor_tensor(out=y_tok.t[:, :], in0=y_tok.t[:, :], in1=sz_s.t[:, :], op=ALU.mult), reads=y_tok.all() + sz_s.all(), writes=y_tok.all())

            def group_norm(y, nrow, yn, ssg):
                P.add('act', [(lambda g: lambda e: e.activation(out=yn.t[0:nrow, g * 256:(g + 1) * 256], in_=y.t[0:nrow, g * 256:(g + 1) * 256], func=AF.Square, accum_out=ssg.t[0:nrow, g:g + 1]))(g) for g in range(2)],
                      reads=y.all(), writes=yn.all() + ssg.all())
                P.add('act', lambda e: e.activation(out=ssg.t[0:nrow, 2:4], in_=ssg.t[0:nrow, 0:2], func=AF.Ln, bias=EPS, scale=1.0 / 256), reads=ssg.all(), writes=ssg.all())
                P.add('act', lambda e: e.activation(out=ssg.t[0:nrow, 4:6], in_=ssg.t[0:nrow, 2:4], func=AF.Exp, scale=-0.5), reads=ssg.all(), writes=ssg.all())
                P.add('dve', [(lambda g: lambda e: e.tensor_scalar(out=yn.t[0:nrow, g * 256:(g + 1) * 256], in0=y.t[0:nrow, g * 256:(g + 1) * 256], scalar1=ssg.t[0:nrow, 4 + g:5 + g], scalar2=None, op0=ALU.mult))(g) for g in range(2)],
                      reads=y.all() + ssg.all(), writes=yn.all())
            group_norm(y_tok, SB, yn_s, sm("ssg_s"))
            bk = bank(); pv = bk.t[:].bitcast(BF16)
            P.add('pe', [(lambda c, pv=pv: lambda e: e.transpose(out=pv[:, c * SB:(c + 1) * SB], in_=yn_s.t[:, c * 128:(c + 1) * 128], identity=identb.t[0:SB, 0:SB]))(c) for c in range(4)],
                  reads=yn_s.all() + identb.all(), writes=bk.all())
            P.add('act', lambda e, pv=pv: e.activation(out=yT_s.t[:, :, :], in_=pv[:, 0:4 * SB].rearrange("p (c b) -> p c b", c=4), func=AF.Copy), reads=bk.all(), writes=yT_s.all())

            def out_proj(aT, yT, nrow, xin_ap, xin_keys, x1_ap, x1_keys):
                for half in range(2):
                    bk = bank()
                    fns = [(lambda hq, bk=bk, half=half: lambda e: e.matmul(bk.t[0:nrow, :], lhsT=aT.t[:, hq, 0:nrow], rhs=w_out_a.t[:, hq, half * 512:(half + 1) * 512], start=(hq == 0), stop=False))(hq) for hq in range(8)]
                    fns += [(lambda c, bk=bk, half=half: lambda e: e.matmul(bk.t[0:nrow, :], lhsT=yT.t[:, c, 0:nrow], rhs=w_out_s.t[:, c, half * 512:(half + 1) * 512], start=False, stop=(c == 3)))(c) for c in range(4)]
                    P.add('pe', fns, reads=aT.all() + yT.all() + w_out_a.all() + w_out_s.all(), writes=bk.all())
                    P.add('dve', lambda e, bk=bk, half=half: e.tensor_tensor(out=x1_ap[:, half * 512:(half + 1) * 512], in0=bk.t[0:nrow, :], in1=xin_ap[:, half * 512:(half + 1) * 512], op=ALU.add),
                          reads=bk.all() + xin_keys, writes=x1_keys)
            out_proj(aT_s, yT_s, SB, xs_in.t[:, :], xs_in.all(), x1_s.t[:, :], x1_s.all())
            rmsnorm_T(x1_s.t[:, :], x1_s.all(), SB, w2b, hn_s, h2T_s.t[:, :, :], h2T_s.all(), "s2")

            P.barrier()
        P.marks['sample'] = len(P.ops)

        xin = [sb("xin%d" % i, [128, D]) for i in range(2)]
        hn = sb("hn", [128, D], BF16); hn_t[0] = hn
        hT = sb("hT", [128, 8, 128], BF16)
        qTs = [sb("qT%d" % i, [128, 4, 128], BF16) for i in range(2)]
        Kh = [sb("Kh%d" % i, [128, 2, 128], BF16) for i in range(3)]
        Vh = [sb("Vh%d" % i, [128, 2, 128], BF16) for i in range(3)]
        xs_tok = sb("xs_tok", [128, 512], BF16)
        xbcT = sb("xbcT", [128, 8, 131], BF16)
        xcT = sb("xcT", [128, 8, 128], BF16, nparts=8)
        Btok = sb("Btok", [128, 256], BF16)
        szts = [sb("szt%d" % i, [128, 512]) for i in range(2)]; dtts = [sb("dtt%d" % i, [128, 16]) for i in range(2)]
        scals = [sm("scal", 64), sb("scal1", [128, 64])]
        Lt = sb("Lt", [128, 8, 128]); MT = sb("MT", [128, 8, 128], BF16)
        xdt = sb("xdt", [128, 512], BF16); xdte = sb("xdte", [128, 512], BF16)
        yt = sb("yt", [128, 512]); ytmp = sb("ytmp", [128, 512]); yn = sb("yn", [128, 512], BF16); yT = sb("yT", [128, 4, 128], BF16)
        Eraw = sb("Eraw", [128, 4, 128]); ET = [sb("ET%d" % i, [128, 8, 128], BF16) for i in range(2)]
        rec = sb("rec", [64, 1, 512]); aT = sb("aT", [64, 8, 128], BF16)
        x1 = sb("x1", [128, GB, D], nparts=GB); h2T = sb("h2T", [128, 8, GB * 128], BF16, nparts=GB)
        actT = sb("actT", [128, NFC // 2, GB * 128], BF16, nparts=NFC // 2)
        kvout = ytmp; ncv = sb("ncv", [128, 8, 3]); ncvT = yt
        nss_sb = Lt
        ffn_sg["m"] = [yt] * 2

        P.add('dve', lambda e: e.memset(stateT.t[:, :], 0.0), writes=stateT.all())
        P.add('dve', lambda e: e.memset(stateTb.t[:, :], 0.0), writes=stateTb.all())
        P.add('dve', lambda e: e.memset(xbcT.t[:, :, :], 0.0), writes=xbcT.all())
        for i in range(3):
            P.add('dve', lambda e, i=i: e.memset(Vh[i].t[:, :, 64:128], 1.0), writes=Vh[i].all())

        blk_ctr = [0]

        def block(xsrc, t, mode):
            main = mode == 'main'
            gi = blk_ctr[0]
            blk_ctr[0] += 1
            bank_mode[0] = gi % 2
            bank_sub[0] = None
            xt = xin[gi % 2]
            qT = qTs[gi % 2]; szt = szts[gi % 2]; dtt = dtts[gi % 2]; scal = scals[gi % 2]
            pe2 = 'pool' if main else 'dve'
            P.dma('sp', 'xin%d' % (gi % 2), xt.t[:, :], xsrc[t * 128:(t + 1) * 128, :], writes=xt.all())
            rmsnorm_T(xt.t[:, :], xt.all(), 128, None, hn, hT.t[:, :, :], hT.all(), "m1", cp='act' if main else 'dve')
            chunks = []
            if main:
                chunks += [(c * 128, 'q', c) for c in range(4)]
            if main or mode == 'prelast':
                chunks += [(DIN + h * 128, 'k', h) for h in range(2)]
            nx = 8 if (main or mode == 'prelast') else 6
            chunks += [(1280 + c * 128, 'x', c) for c in range(nx)]
            slot = gi % 3
            i = 0
            while i < len(chunks):
                grp = chunks[i:i + 4]
                i += 4
                bk = bank()
                fns = []
                for gi_, (c0, kind, idx) in enumerate(grp):
                    fns += [(lambda k, gi_=gi_, c0=c0, bk=bk: lambda e: e.matmul(bk.t[:, gi_ * 128:(gi_ + 1) * 128], lhsT=w_in_sb.t[:, k, c0:c0 + 128], rhs=hT.t[:, k, :], start=(k == 0), stop=(k == 7)))(k) for k in range(8)]
                P.add('pe', fns, reads=w_in_sb.all() + hT.all(), writes=bk.all())
                j = 0
                while j < len(grp):
                    kind = grp[j][1]
                    j2 = j
                    while j2 < len(grp) and grp[j2][1] == kind:
                        j2 += 1
                    i0 = grp[j][2]
                    nn = j2 - j
                    src = bk.t[:, j * 128:j2 * 128].rearrange("p (c t) -> p c t", c=nn)
                    if kind == 'q':
                        P.add('act', lambda e, src=src, i0=i0, nn=nn: e.activation(out=qT.t[:, i0:i0 + nn, :], in_=src, func=AF.Copy), reads=bk.all(), writes=qT.all())
                    elif kind == 'k':
                        P.add('dve', lambda e, src=src, i0=i0, nn=nn: e.tensor_copy(out=Kh[slot].t[:, i0:i0 + nn, :], in_=src), reads=bk.all(), writes=Kh[slot].all())
                    elif main:
                        P.add('act', lambda e, src=src, i0=i0, nn=nn: e.activation(out=xbcT.t[:, i0:i0 + nn, 3:131], in_=src, func=AF.Copy), reads=bk.all(), writes=xbcT.all())
                        if t == NBLK - 1:
                            P.add('dve', lambda e, src=src, i0=i0, nn=nn: e.tensor_copy(out=ncv.t[:, i0:i0 + nn, :], in_=src[:, :, 125:128]), reads=bk.all(), writes=ncv.all())
                    else:
                        P.add('dve', lambda e, src=src, i0=i0, nn=nn: e.tensor_copy(out=xbcT.t[:, i0:i0 + nn, 3:131], in_=src), reads=bk.all(), writes=xbcT.all())
                    j = j2
            bdt = None
            if main:
                bA, bB = bank(), bank()
                P.add('pe', [(lambda k: lambda e: e.matmul(bA.t[:, :], lhsT=hT.t[:, k, :], rhs=w_in_sb.t[:, k, 640:1152], start=(k == 0), stop=(k == 7)))(k) for k in range(8)],
                      reads=w_in_sb.all() + hT.all(), writes=bA.all())
                P.add('pe', [(lambda k: lambda e: e.matmul(bB.t[:, 0:128], lhsT=hT.t[:, k, :], rhs=w_in_sb.t[:, k, 1152:1280], start=(k == 0), stop=(k == 7)))(k) for k in range(8)]
                      + [(lambda k: lambda e: e.matmul(bB.t[:, 128:136], lhsT=hT.t[:, k, :], rhs=w_in_sb.t[:, k, 2304:2312], start=(k == 0), stop=(k == 7)))(k) for k in range(8)],
                      reads=w_in_sb.all() + hT.all(), writes=bB.all())
                P.add('dve', lambda e: e.tensor_copy(out=Vh[slot].t[:, :, 0:64], in_=bA.t[:, 0:128].rearrange("p (h d) -> p h d", h=2)), reads=bA.all(), writes=Vh[slot].all())
                P.add('act', lambda e: e.activation(out=szt.t[:, 0:384], in_=bA.t[:, 128:512], func=AF.Silu), reads=bA.all(), writes=szt.all())
                P.add('act', lambda e: e.activation(out=szt.t[:, 384:512], in_=bB.t[:, 0:128], func=AF.Silu), reads=bB.all(), writes=szt.all())
                softplus_dt(bB.t[:, 128:136], bB.all(), 128, dtt)
                if t == NBLK - 1:
                    bKt = bank()
                    P.add('pe', [(lambda k: lambda e: e.matmul(bKt.t[:, 0:128], lhsT=hT.t[:, k, :], rhs=w_in_sb.t[:, k, 512:640], start=(k == 0), stop=(k == 7)))(k) for k in range(8)],
                          reads=w_in_sb.all() + hT.all(), writes=bKt.all())
                    P.add('dve', lambda e: e.tensor_copy(out=kvout.t[:, 0:128], in_=bKt.t[:, 0:128]), reads=bKt.all(), writes=kvout.all())
                    P.add('dve', lambda e: e.tensor_copy(out=kvout.t[:, 128:256], in_=bA.t[:, 0:128]), reads=bA.all(), writes=kvout.all())
                    P.dma('sp', 'pout', nk_p[:, :], kvout.t[:, 0:128], reads=kvout.all())
                    P.dma('sp', 'pout', nv_p[:, :], kvout.t[:, 128:256], reads=kvout.all())
            else:
                bB = bank()
                fns = [(lambda k: lambda e: e.matmul(bB.t[:, 128:136], lhsT=hT.t[:, k, :], rhs=w_in_sb.t[:, k, 2304:2312], start=(k == 0), stop=(k == 7)))(k) for k in range(8)]
                if mode == 'prelast':
                    fns += [(lambda k: lambda e: e.matmul(bB.t[:, 0:128], lhsT=hT.t[:, k, :], rhs=w_in_sb.t[:, k, 640:768], start=(k == 0), stop=(k == 7)))(k) for k in range(8)]
                P.add('pe', fns, reads=w_in_sb.all() + hT.all(), writes=bB.all())
                if mode == 'prelast':
                    P.add('dve', lambda e: e.tensor_copy(out=Vh[slot].t[:, :, 0:64], in_=bB.t[:, 0:128].rearrange("p (h d) -> p h d", h=2)), reads=bB.all(), writes=Vh[slot].all())
                softplus_dt(bB.t[:, 128:136], bB.all(), 128, dtt)
            if main:
                P.capture()
                bank_sub[0] = 0
            for c4 in range(0, nx, 4):
                bk = bank()
                ncc = min(4, nx - c4)
                fns = []
                for ci in range(ncc):
                    c = c4 + ci
                    fns += [(lambda j, c=c, ci=ci, bk=bk: lambda e: e.matmul(bk.t[:, ci * 128:(ci + 1) * 128], lhsT=convdiag.t[:, c, j, :], rhs=xbcT.t[:, c, j:j + 128], start=(j == 0), stop=(j == 3)))(j) for j in range(4)]
                P.add('pe', fns, reads=convdiag.all() + xbcT.all(), writes=bk.all())
                P.add('act', [(lambda ci, bk=bk, c4=c4: lambda e: e.activation(out=xcT.t[:, c4 + ci, :], in_=bk.t[:, ci * 128:(ci + 1) * 128], func=AF.Silu, bias=cbT.t[:, c4 + ci:c4 + ci + 1]))(ci) for ci in range(ncc)],
                      reads=bk.all() + cbT.all(), writes=xcT.p(*range(c4, c4 + ncc)))
            P.add(pe2, lambda e: e.tensor_copy(out=xbcT.t[:, :, 0:3], in_=xbcT.t[:, :, 128:131]), reads=xbcT.all(), writes=xbcT.all())
            bX = bank(); pX = bX.t[:].bitcast(BF16)
            P.add('pe', [(lambda c: lambda e: e.transpose(out=pX[:, c * 128:(c + 1) * 128], in_=xcT.t[:, c, :], identity=identb.t[:, :]))(c) for c in range(6)],
                  reads=xcT.p(0, 1, 2, 3, 4, 5) + identb.all(), writes=bX.all())
            if main:
                P.add('act', lambda e: e.activation(out=Btok.t[:, :], in_=pX[:, 512:768], func=AF.Copy), reads=bX.all(), writes=Btok.all())
                P.add('act', lambda e: e.activation(out=xs_tok.t[:, :], in_=pX[:, 0:512], func=AF.Copy), reads=bX.all(), writes=xs_tok.all())
            else:
                P.add('dve', lambda e: e.tensor_copy(out=Btok.t[:, :], in_=pX[:, 512:768]), reads=bX.all(), writes=Btok.all())
                P.add('dve', lambda e: e.tensor_copy(out=xs_tok.t[:, :], in_=pX[:, 0:512]), reads=bX.all(), writes=xs_tok.all())
            a_ = scal.t[:, 0:8]
            P.add('dve', lambda e: e.tensor_tensor(out=a_, in0=dtt.t[:, 0:8], in1=A_b.t[:, :], op=ALU.mult), reads=dtt.all() + A_b.all(), writes=scal.all())
            if main:
                P.add('pool', lambda e: e.tensor_tensor(out=Lt.t[:, :, :], in0=bc(tri.t[:, :].unsqueeze(1), [128, 8, 128]), in1=bc(a_.unsqueeze(2), [128, 8, 128]), op=ALU.mult),
                      reads=tri.all() + scal.all(), writes=Lt.all())
                bcs = bank()
                P.add('pe', lambda e: e.matmul(bcs.t[:, 0:8], lhsT=tri.t[:, :], rhs=a_, start=True, stop=True), reads=tri.all() + scal.all(), writes=bcs.all())
                P.add('dve', lambda e: e.tensor_scalar(out=scal.t[:, 8:16], in0=bcs.t[:, 0:8], scalar1=-1.0, scalar2=None, op0=ALU.mult), reads=bcs.all(), writes=scal.all())
                P.add('act', lambda e: e.activation(out=scal.t[:, 16:24], in_=bcs.t[:, 0:8], func=AF.Exp), reads=bcs.all(), writes=scal.all())
                bC0, bC1 = bank(), bank()
                for hf, bk in enumerate((bC0, bC1)):
                    P.add('pe', [lambda e, bk=bk, hf=hf: e.matmul(bk.t[:, :], lhsT=onesf.t[:, :], rhs=Lt.t[:, hf * 4:(hf + 1) * 4, :].rearrange("p h i -> p (h i)"), start=True, stop=False),
                                 lambda e, bk=bk: e.matmul(bk.t[:, :], lhsT=identb.t[:, :], rhs=negm.t[:, :, :].rearrange("p h i -> p (h i)"), start=False, stop=True)],
                          reads=onesf.all() + Lt.all() + identb.all() + negm.all(), writes=bk.all())
                for hf, bk in enumerate((bC0, bC1)):
                    P.add('act', lambda e, bk=bk, hf=hf: e.activation(out=scal.t[:, 24 + hf * 4:28 + hf * 4], in_=bk.t[:, :].rearrange("p (h i) -> p h i", h=4)[:, :, 127], func=AF.Exp), reads=bk.all(), writes=scal.all())
                    P.add('act', [(lambda hh, bk=bk, hf=hf: lambda e: e.activation(out=Lt.t[:, hf * 4 + hh, :], in_=bk.t[:, hh * 128:(hh + 1) * 128], func=AF.Exp, bias=scal.t[:, 8 + hf * 4 + hh:9 + hf * 4 + hh]))(hh) for hh in range(4)],
                          reads=bk.all() + scal.all(), writes=Lt.all())
                bCB = bank()
                P.add('pe', [(lambda g: lambda e: e.matmul(bCB.t[:, g * 128:(g + 1) * 128], lhsT=xcT.t[:, 4 + g, :], rhs=xcT.t[:, 6 + g, :], start=True, stop=True))(g) for g in range(2)],
                      reads=xcT.p(4, 5, 6, 7), writes=bCB.all())
                P.add('dve', lambda e: e.tensor_tensor(out=MT.t[:, :, :].rearrange("p (g r) i -> p g r i", g=2), in0=Lt.t[:, :, :].rearrange("p (g r) i -> p g r i", g=2),
                                                       in1=bc(bCB.t[:, 0:256].rearrange("p (g i) -> p g i", g=2).unsqueeze(2), [128, 2, 4, 128]), op=ALU.mult),
                      reads=Lt.all() + bCB.all(), writes=MT.all())
                dte_ap = Lt.t[:, :, 127]
                dte_keys = Lt.all()
                cd_ap = scal.t[:, 24:32]
            else:
                bsf = bank()
                P.add('pe', [lambda e: e.matmul(bsf.t[:, 0:8], lhsT=striu.t[:, :], rhs=a_, start=True, stop=True),
                             lambda e: e.matmul(bsf.t[:, 8:16], lhsT=onesf.t[:, :], rhs=a_, start=True, stop=True)], reads=striu.all() + onesf.all() + scal.all(), writes=bsf.all())
                P.add('act', lambda e: e.activation(out=scal.t[:, 16:32], in_=bsf.t[:, 0:16], func=AF.Exp), reads=bsf.all(), writes=scal.all())
                dte_ap = scal.t[:, 16:24]
                dte_keys = scal.all()
                cd_ap = scal.t[:, 24:32]
            P.add('dve', lambda e: e.tensor_tensor(out=scal.t[:, 32:40], in0=dtt.t[:, 0:8], in1=dte_ap, op=ALU.mult), reads=dtt.all() + dte_keys, writes=scal.all())
            pXs = xs_tok.t[:, :].rearrange("p (h d) -> p h d", h=8)
            P.add(pe2, lambda e: e.tensor_tensor(out=xdte.t[:, :].rearrange("p (h d) -> p h d", h=8), in0=pXs, in1=bc(scal.t[:, 32:40].unsqueeze(2), [128, 8, 64]), op=ALU.mult),
                  reads=xs_tok.all() + scal.all(), writes=xdte.all())
            if main:
                P.add('pool', lambda e: e.tensor_tensor(out=xdt.t[:, :].rearrange("p (h d) -> p h d", h=8), in0=pXs, in1=bc(dtt.t[:, 0:8].unsqueeze(2), [128, 8, 64]), op=ALU.mult),
                      reads=xs_tok.all() + dtt.all(), writes=xdt.all())
                bY, bYo = bank(), bank()
                P.add('pe', [(lambda h: lambda e: e.matmul(bY.t[:, h * 64:(h + 1) * 64], lhsT=MT.t[:, h, :], rhs=xdt.t[:, h * 64:(h + 1) * 64], start=True, stop=True))(h) for h in range(8)],
                      reads=MT.all() + xdt.all(), writes=bY.all())
                P.add('pe', [(lambda g: lambda e: e.matmul(bYo.t[:, g * 256:(g + 1) * 256], lhsT=xcT.t[:, 6 + g, :], rhs=stateTb.t[:, g * 256:(g + 1) * 256], start=True, stop=True))(g) for g in range(2)],
                      reads=xcT.p(6, 7) + stateTb.all(), writes=bYo.all())
                P.add('dve', lambda e: e.tensor_tensor(out=yt.t[:, :].rearrange("p (h d) -> p h d", h=8), in0=bYo.t[:, :].rearrange("p (h d) -> p h d", h=8), in1=bc(scal.t[:, 16:24].unsqueeze(2), [128, 8, 64]), op=ALU.mult),
                      reads=bYo.all() + scal.all(), writes=yt.all())
                P.add('dve', lambda e: e.tensor_tensor(out=yt.t[:, :], in0=yt.t[:, :], in1=bY.t[:, :], op=ALU.add), reads=yt.all() + bY.all(), writes=yt.all())
                P.add('pool', lambda e: e.tensor_tensor(out=ytmp.t[:, :].rearrange("p (h d) -> p h d", h=8), in0=pXs, in1=bc(D_b.t[:, :].unsqueeze(2), [128, 8, 64]), op=ALU.mult),
                      reads=xs_tok.all() + D_b.all(), writes=ytmp.all())
                P.add('dve', lambda e: e.tensor_tensor(out=yt.t[:, :], in0=yt.t[:, :], in1=ytmp.t[:, :], op=ALU.add), reads=yt.all() + ytmp.all(), writes=yt.all())
                P.add('dve', lambda e: e.tensor_tensor(out=yt.t[:, :], in0=yt.t[:, :], in1=szt.t[:, :], op=ALU.mult), reads=yt.all() + szt.all(), writes=yt.all())
                group_norm(yt, 128, yn, sm("ssg_m"))
                bk = bank(); pv = bk.t[:].bitcast(BF16)
                P.add('pe', [(lambda c, pv=pv: lambda e: e.transpose(out=pv[:, c * 128:(c + 1) * 128], in_=yn.t[:, c * 128:(c + 1) * 128], identity=identb.t[:, :]))(c) for c in range(4)],
                      reads=yn.all() + identb.all(), writes=bk.all())
                P.add('act', lambda e, pv=pv: e.activation(out=yT.t[:, :, :], in_=pv[:, 0:512].rearrange("p (c t) -> p c t", c=4), func=AF.Copy), reads=bk.all(), writes=yT.all())
            bS = bank()
            P.add('pe', [(lambda g: lambda e: e.matmul(bS.t[:, g * 256:(g + 1) * 256], lhsT=Btok.t[:, g * 128:(g + 1) * 128], rhs=xdte.t[:, g * 256:(g + 1) * 256], start=True, stop=True))(g) for g in range(2)],
                  reads=Btok.all() + xdte.all(), writes=bS.all())
            P.add(pe2, lambda e: e.tensor_tensor(out=stateT.t[:, :].rearrange("p (h d) -> p h d", h=8), in0=stateT.t[:, :].rearrange("p (h d) -> p h d", h=8), in1=bc(cd_ap.unsqueeze(2), [128, 8, 64]), op=ALU.mult),
                  reads=stateT.all() + scal.all(), writes=stateT.all())
            P.add('dve', lambda e: e.tensor_tensor(out=stateT.t[:, :], in0=stateT.t[:, :], in1=bS.t[:, :], op=ALU.add), reads=stateT.all() + bS.all(), writes=stateT.all())
            if mode == 'prelast':
                P.add('dve', lambda e: e.tensor_scalar(out=stateT.t[:, :], in0=stateT.t[:, :], scalar1=flag.t[:, 0:1], scalar2=None, op0=ALU.mult), reads=stateT.all() + flag.all(), writes=stateT.all())
                P.add('dve', lambda e: e.tensor_scalar(out=xbcT.t[:, :, 0:3], in0=xbcT.t[:, :, 0:3], scalar1=flag.t[:, 0:1], scalar2=None, op0=ALU.mult), reads=xbcT.all() + flag.all(), writes=xbcT.all())
            if main or mode == 'prelast':
                P.add('act', lambda e: e.activation(out=stateTb.t[:, :], in_=stateT.t[:, :], func=AF.Copy), reads=stateT.all(), writes=stateTb.all())
            if not main:
                return
            ssd_ops = P.end_capture()
            P.capture()
            bank_sub[0] = 1
            pslot = (gi - 1) % 3
            for kb, sl in enumerate((pslot, slot)):
                bS0, bS1 = bank(), bank()
                for par, bk in enumerate((bS0, bS1)):
                    P.add('pe', [(lambda hh, j, bk=bk, par=par, sl=sl: lambda e: e.matmul(bk.t[:, (hh * 2 + j) * 128:(hh * 2 + j + 1) * 128], lhsT=Kh[sl].t[par * 64:par * 64 + 64, hh, :], rhs=qT.t[par * 64:par * 64 + 64, hh * 2 + j, :], start=True, stop=True))(hh, j) for hh in range(2) for j in range(2)],
                          reads=Kh[sl].all() + qT.all(), writes=bk.all())
                ebi = 1 - kb
                for par, bk in enumerate((bS0, bS1)):
                    P.add('act', lambda e, bk=bk: e.activation(out=Eraw.t[:, :, :], in_=bk.t[:, :].rearrange("p (c q) -> p c q", c=4), func=AF.Exp, scale=SCALE), reads=bk.all(), writes=Eraw.all())
                    P.add('dve', lambda e, kb=kb, ebi=ebi, par=par: e.tensor_tensor(out=ET[kb].t[:, :, :].rearrange("p (c par) q -> p par c q", par=2)[:, par], in0=Eraw.t[:, :, :],
                                                                            in1=expB.t[:, ebi, :, :].rearrange("p q (c par) -> p par c q", par=2)[:, par], op=ALU.mult),
                          reads=Eraw.all() + expB.all(), writes=ET[kb].all())
                if kb == 0 and t == 0:
                    P.add('dve', lambda e: e.tensor_scalar(out=ET[0].t[:, :, :], in0=ET[0].t[:, :, :], scalar1=flag.t[:, 0:1], scalar2=None, op0=ALU.mult), reads=ET[0].all() + flag.all(), writes=ET[0].all())
            bO = [bank(), bank()]
            for hh in range(2):
                P.add('pe', [(lambda kb, hh=hh: lambda e: e.matmul(bO[hh].t[:, :], lhsT=Vh[(pslot, slot)[kb]].t[:, hh, :], rhs=ET[kb].t[:, hh * 4:(hh + 1) * 4, :].rearrange("p r q -> p (r q)"), start=(kb == 0), stop=(kb == 1)))(kb) for kb in range(2)],
                      reads=Vh[pslot].all() + Vh[slot].all() + ET[0].all() + ET[1].all(), writes=bO[hh].all())
                P.add('dve', lambda e, hh=hh: e.tensor_tensor(out=rec.t[:, 0, :].rearrange("p (r q) -> p r q", r=4), in0=bO[hh].t[64:128, :].rearrange("p (r q) -> p r q", r=4), in1=bc(esink.t[64:128, hh * 4:(hh + 1) * 4].unsqueeze(2), [64, 4, 128]), op=ALU.add),
                      reads=bO[hh].all() + esink.all(), writes=rec.all())
                P.add('dve', lambda e, hh=hh: e.reciprocal(out=rec.t[:, 0, :], in_=rec.t[:, 0, :]), reads=rec.all(), writes=rec.all())
                P.add('dve', lambda e, hh=hh: e.tensor_tensor(out=aT.t[:, hh * 4:(hh + 1) * 4, :].rearrange("p r q -> p (r q)"), in0=bO[hh].t[0:64, :], in1=rec.t[:, 0, :], op=ALU.mult),
                      reads=bO[hh].all() + rec.all(), writes=aT.all())
            att_ops = P.end_capture()
            bank_sub[0] = None
            P.ops.extend(Prog.merge(ssd_ops, att_ops))
            b_in_g = t % GB
            out_proj(aT, yT, 128, xt.t[:, :], xt.all(), x1.t[:, b_in_g, :], x1.p(b_in_g))
            rmsnorm_T(x1.t[:, b_in_g, :], x1.p(b_in_g), 128, w2b, hn, h2T.t[:, :, b_in_g * 128:(b_in_g + 1) * 128], h2T.p(b_in_g), "m2")

        for t in range(NBLK):
            P.capture()
            block(xp, t, 'prelast' if t == NBLK - 1 else 'pre')
            bank_mode[0] = None
            P.pipe_push(P.end_capture(), frac=0.65)
        P.pipe_drain()
        hn_t[0] = hn_sf
        def s_out(bi, full, keys):
            P.dma('sp', 'sout', y_s[:, :], full, reads=keys)
        P.capture()
        bank_mode[0] = 1
        ffn(h2T_s, SB, [(0, SB)], lambda bi, half: x1_s.t[:, :] if half is None else x1_s.t[:, half * 512:(half + 1) * 512], lambda bi: x1_s.all(), s_out, actT_s, "s")
        bank_mode[0] = None
        P.pipe_push(P.end_capture(), split=False)
        hn_t[0] = hn
        P.marks['prefix'] = len(P.ops)
        for t in range(NBLK):
            P.marks['main%d' % t] = len(P.ops)
            P.capture()
            block(xm, t, 'main')
            bank_mode[0] = None
            P.pipe_push(P.end_capture())
            if t % GB == GB - 1:
                g0 = t - (GB - 1)
                P.pipe_drain()

                def m_out(bi, full, keys, g0=g0):
                    P.dma('pool', 'yout%d' % bi, y_m[(g0 + bi) * 128:(g0 + bi + 1) * 128, :], full, reads=keys)
                tail = ffn(h2T, GB * 128, [(bi * 128, 128) for bi in range(GB)],
                           lambda bi, half: x1.t[:, bi, :] if half is None else x1.t[:, bi, half * 512:(half + 1) * 512],
                           lambda bi: x1.p(bi), m_out, actT, "m", defer_tail=True)
                P.pipe_push(tail, split=False)
        P.pipe_drain()
        P.marks['mainend'] = len(P.ops)
        bk = bank()
        P.add('pe', lambda e: e.matmul(bk.t[0:24, 0:128], lhsT=ncv.t[:, :, :].rearrange("p c j -> p (c j)"), rhs=identf.t[:, :], start=True, stop=True), reads=ncv.all() + identf.all(), writes=bk.all())
        P.add('dve', lambda e: e.tensor_copy(out=ncvT.t[0:24, 0:128], in_=bk.t[0:24, 0:128]), reads=bk.all(), writes=ncvT.all())
        for c in range(8):
            P.dma('sp', 'pout', ncv_p[:, c * 128:(c + 1) * 128], ncvT.t[c * 3:(c + 1) * 3, 0:128], reads=ncvT.all())
        bk2 = bank()
        P.add('pe', [(lambda c: lambda e: e.matmul(bk2.t[:, c * 128:(c + 1) * 128], lhsT=stateT.t[:, c * 128:(c + 1) * 128], rhs=identf.t[:, :], start=True, stop=True))(c) for c in range(4)],
              reads=stateT.all() + identf.all(), writes=bk2.all())
        P.add('dve', lambda e: e.tensor_copy(out=nss_sb.t[:, 0:4, :], in_=bk2.t[:, :].rearrange("p (c n) -> p c n", c=4)), reads=bk2.all(), writes=nss_sb.all())
        P.dma('sp', 'pout', nss_p.rearrange("(c p) n -> p c n", p=128), nss_sb.t[:, 0:4, :], reads=nss_sb.all())

        _P[0] = P
        with nc.Block() as blockctx:
            P.finish(st)
    return nc


def _bucket_onehot():
    n = np.arange(128)
    exact = 16
    nf = np.maximum(n, 1).astype(np.float32)
    large = exact + (np.log(nf / exact) / math.log(128 / exact) * (32 - exact)).astype(np.int32)
    bucket = np.where(n < exact, n, np.minimum(large, 31))
    oh = np.zeros((32, 128), np.float32)
    oh[bucket, n] = 1.0
    return oh


_NC = [None]
_P = [None]


def kernel(**inp):
    f = lambda a: np.ascontiguousarray(np.asarray(a, dtype=np.float32))
    x_prompt = f(inp["x_prompt"]); x_sample = f(inp["x_sample"])
    if _NC[0] is None:
        _NC[0] = build()
    nc = _NC[0]
    shared = {
        "rel_bias": f(inp["rel_bias"]), "onehot": _bucket_onehot(),
        "norm1_w": f(inp["norm1_w"]).reshape(1, D), "w_in": f(inp["w_in"])[0], "attn_sinks": f(inp["attn_sinks"]).reshape(1, 8),
        "conv_w": f(inp["conv_w"])[0], "conv_b": f(inp["conv_b"]).reshape(1, 1024), "dt_bias": f(inp["dt_bias"]).reshape(1, 8),
        "A_log": f(inp["A_log"]).reshape(1, 8), "D_skip": f(inp["D_skip"]).reshape(1, 8), "ssm_norm_w": f(inp["ssm_norm_w"]).reshape(1, 512),
        "w_out": f(inp["w_out"])[0], "norm2_w": f(inp["norm2_w"]).reshape(1, D), "w_gate": f(inp["w_gate"])[0], "w_up": f(inp["w_up"])[0],
        "w_down": f(inp["w_down"])[0], "final_norm_w": f(inp["final_norm_w"]).reshape(1, D),
    }
    ck = f(inp["cache_k"])[0].reshape(128, 128, 128); cv = f(inp["cache_v"])[0].reshape(128, 128, 128)
    sconv = f(inp["state_conv"])[0]; sssm = f(inp["state_ssm"])[0].reshape(128 * 8, 64 * 128)
    in_maps = []
    for c in range(NCORES):
        b, half = c // 2, c % 2
        m = dict(shared)
        m["xm"] = x_prompt[b, half * 2048:(half + 1) * 2048]
        m["xp"] = x_prompt[b, 0:2048] if half == 1 else np.zeros((2048, D), np.float32)
        m["flag"] = np.full((128, 1), float(half), np.float32)
        m["xsm"] = x_sample[c * SB:(c + 1) * SB, 0]
        m["ck"] = ck[c * SB:(c + 1) * SB]; m["cv"] = cv[c * SB:(c + 1) * SB]
        m["sconv"] = sconv[c * SB:(c + 1) * SB]; m["sssm"] = sssm[c * SB * 8:(c + 1) * SB * 8]
        in_maps.append({k: np.ascontiguousarray(v) for k, v in m.items()})
    res = run_bass_kernel_spmd(nc, in_maps, core_ids=list(range(NCORES))).results
    yp = np.zeros((4, 4096, D), np.float32)
    for c in range(NCORES):
        yp[c // 2, (c % 2) * 2048:(c % 2 + 1) * 2048] = res[c]["y_m"]
    ys = np.concatenate([res[c]["y_s"] for c in range(NCORES)], 0).reshape(128, 1, D)
    odd = [1, 3, 5, 7]
    nkp = np.stack([res[c]["nk_p"].reshape(128, 2, 64) for c in odd])[None]
    nvp = np.stack([res[c]["nv_p"].reshape(128, 2, 64) for c in odd])[None]
    ncp = np.stack([res[c]["ncv_p"] for c in odd])[None]
    nsp = np.stack([res[c]["nss_p"].reshape(8, 64, 128) for c in odd])[None]
    nks = np.concatenate([res[c]["nk_s"] for c in range(NCORES)], 0).reshape(1, 128, 128, 2, 64)
    nvs = np.concatenate([res[c]["nv_s"] for c in range(NCORES)], 0).reshape(1, 128, 128, 2, 64)
    ncs = np.concatenate([res[c]["ncv_s"] for c in range(NCORES)], 0)[None]
    nsss = np.concatenate([res[c]["nss_s"] for c in range(NCORES)], 0).reshape(1, 128, 8, 64, 128)
    return (yp, ys, nkp.astype(np.float32), nvp.astype(np.float32), ncp.astype(np.float32), nsp.astype(np.float32),
            nks, nvs, ncs, nsss)
```
